# Optimizing a Trainium2 kernel written in Bass

```python
import math
import jax, jax.numpy as jnp
from jax import lax
import numpy as np

D_MODEL = 1024
BATCH = 8
SEQ = 2048
DEPTH = 4

GDN_HEADS = 4
GDN_HEAD_DIM = 128
GDN_CONV = 4
GDN_CHUNK = 64
DSW_HEADS = 4
DSW_HEAD_DIM = 64
DSW_PATTERNS = ((128, 1), (512, 4), (2048, 16))
WIN_BLOCK = 128
DIFF_HEADS = 4
DIFF_QK_DIM = 32
DIFF_V_DIM = 64
ATTN_QBLOCK = 128
D_FF = 2752
FFN_CONV = 3
DEEPNORM_ALPHA = (2 * DEPTH) ** 0.25
DEEPNORM_BETA = (8 * DEPTH) ** -0.25
EPS = 1e-5

GDN_W = GDN_HEADS * GDN_HEAD_DIM
DSW_W = DSW_HEADS * DSW_HEAD_DIM
DIFF_W = DIFF_HEADS * DIFF_V_DIM
MIX_WIDTH = GDN_W + DSW_W + DIFF_W
DIFF_QK_W = DIFF_HEADS * 2 * DIFF_QK_DIM
IN_SPLITS = (3 * GDN_W, GDN_W, GDN_HEADS, GDN_HEADS, 3 * DSW_W, 2 * DIFF_QK_W + DIFF_W)
IN_WIDTH = sum(IN_SPLITS)

kernel_name = "hybrid_gdn_dilated_diff_deepnorm"


def _layer_norm(x, g, b):
    xf = x.astype(jnp.float32)
    mu = jnp.mean(xf, axis=-1, keepdims=True)
    var = jnp.mean(jnp.square(xf - mu), axis=-1, keepdims=True)
    return ((xf - mu) * lax.rsqrt(var + EPS) * g + b).astype(x.dtype)


def _rms_norm(x, w):
    xf = x.astype(jnp.float32)
    return xf * lax.rsqrt(jnp.mean(jnp.square(xf), axis=-1, keepdims=True) + EPS) * w


def _l2norm(x):
    return x * lax.rsqrt(jnp.sum(jnp.square(x), axis=-1, keepdims=True) + 1e-6)


def _heads(a, n_heads):
    B, T, _ = a.shape
    return a.reshape(B, T, n_heads, -1).transpose(0, 2, 1, 3)


def _merge_heads(a):
    B, H, T, D = a.shape
    return a.transpose(0, 2, 1, 3).reshape(B, T, H * D)


def _causal_depthwise_conv(x, w):
    K = w.shape[0]
    T = x.shape[1]
    xp = jnp.pad(x, ((0, 0), (K - 1, 0), (0, 0)))
    y = xp[:, 0:T] * w[0]
    for j in range(1, K):
        y = y + xp[:, j:j + T] * w[j]
    return y


def _chunked_gated_delta_rule(q, k, v, g, beta):
    B, H, T, Dk = q.shape
    Dv = v.shape[-1]
    n = T // GDN_CHUNK
    C = GDN_CHUNK
    rs = lambda a: a.reshape(B, H, n, C, *a.shape[3:])
    q, k, v, g, beta = rs(q), rs(k), rs(v), rs(g), rs(beta)
    g = jnp.cumsum(g, axis=-1)
    idx = jnp.arange(C)
    lower_incl = idx[:, None] >= idx[None, :]
    strict = idx[:, None] > idx[None, :]
    decay = jnp.exp(jnp.where(lower_incl, g[..., :, None] - g[..., None, :], -jnp.inf))
    k_beta = k * beta[..., None]
    m = jnp.where(strict, jnp.einsum('bhncd,bhnsd->bhncs', k_beta, k) * decay, 0.0)
    a_mat = jnp.eye(C, dtype=jnp.float32) + m
    u = lax.linalg.triangular_solve(a_mat, v * beta[..., None], left_side=True, lower=True, unit_diagonal=True)
    w = lax.linalg.triangular_solve(a_mat, k_beta * jnp.exp(g)[..., None], left_side=True, lower=True, unit_diagonal=True)
    qk = jnp.where(lower_incl, jnp.einsum('bhncd,bhnsd->bhncs', q, k) * decay, 0.0)
    g_last = g[..., -1]
    k_tail = k * jnp.exp(g_last[..., None] - g)[..., None]
    q_dec = q * jnp.exp(g)[..., None]

    def step(S, inp):
        q_i, w_i, u_i, qk_i, kt_i, gl_i = inp
        v_new = u_i - jnp.einsum('bhcd,bhde->bhce', w_i, S)
        o = jnp.einsum('bhcd,bhde->bhce', q_i, S) + jnp.einsum('bhcs,bhse->bhce', qk_i, v_new)
        S = S * jnp.exp(gl_i)[..., None, None] + jnp.einsum('bhcd,bhce->bhde', kt_i, v_new)
        return S, o

    xs = tuple(jnp.moveaxis(t, 2, 0) for t in (q_dec, w, u, qk, k_tail, g_last))
    S0 = jnp.zeros((B, H, Dk, Dv), jnp.float32)
    _, o = lax.scan(step, S0, xs)
    return jnp.moveaxis(o, 0, 2).reshape(B, H, T, Dv)


def _gated_deltanet(qkv, z, b, a, conv_w, a_log, dt_bias, norm_w):
    f32 = jnp.float32
    B, T, _ = z.shape
    qkv = jax.nn.silu(_causal_depthwise_conv(qkv, conv_w)).astype(f32)
    q, k, v = (_heads(t, GDN_HEADS) for t in jnp.split(qkv, 3, axis=-1))
    q = _l2norm(q) * GDN_HEAD_DIM ** -0.5
    k = _l2norm(k)
    beta = jax.nn.sigmoid(b.astype(f32)).transpose(0, 2, 1)
    g = (-jnp.exp(a_log.astype(f32)) * jax.nn.softplus(a.astype(f32) + dt_bias.astype(f32))).transpose(0, 2, 1)
    o = _chunked_gated_delta_rule(q, k, v, g, beta).transpose(0, 2, 1, 3)
    zz = z.reshape(B, T, GDN_HEADS, GDN_HEAD_DIM).astype(f32)
    y = _rms_norm(o, norm_w) * jax.nn.silu(zz)
    return y.reshape(B, T, GDN_W).astype(z.dtype)


def _banded_window_attn(q, k, v, window):
    *lead, L, D = q.shape
    nb = -(-L // WIN_BLOCK)
    pad = nb * WIN_BLOCK - L
    lp = [(0, 0)] * len(lead)
    qb = jnp.pad(q, lp + [(0, pad), (0, 0)]).reshape(*lead, nb, WIN_BLOCK, D)

    def kv_blocks(t):
        t = jnp.pad(t, lp + [(WIN_BLOCK, pad), (0, 0)]).reshape(*lead, nb + 1, WIN_BLOCK, D)
        return jnp.concatenate([t[..., :-1, :, :], t[..., 1:, :, :]], axis=-2)

    kb, vb = kv_blocks(k), kv_blocks(v)
    i = jnp.arange(WIN_BLOCK)[:, None]
    j = jnp.arange(2 * WIN_BLOCK)[None, :]
    dist = WIN_BLOCK + i - j
    blk = jnp.arange(nb)[:, None, None]
    valid = (dist >= 0) & (dist <= window) & ((blk > 0) | (j >= WIN_BLOCK))
    s = jnp.einsum('...nqd,...nkd->...nqk', qb, kb).astype(jnp.float32) * D ** -0.5
    s = jnp.where(valid, s, -jnp.inf)
    mx = jnp.max(s, axis=-1, keepdims=True)
    p = jnp.exp(s - mx)
    den = jnp.sum(p, axis=-1, keepdims=True)
    o = jnp.einsum('...nqk,...nkd->...nqd', p, vb.astype(jnp.float32)) / den
    lse = (mx + jnp.log(den))[..., 0]
    o = o.reshape(*lead, nb * WIN_BLOCK, D)[..., :L, :]
    lse = lse.reshape(*lead, nb * WIN_BLOCK)[..., :L]
    return o, lse


def _dilated_window_group(qkv):
    q, k, v = (_heads(t, DSW_HEADS) for t in jnp.split(qkv, 3, axis=-1))
    B, H, T, D = q.shape
    outs, lses = [], []
    for window, dilation in DSW_PATTERNS:
        L = T // dilation
        regroup = lambda t: jnp.swapaxes(t.reshape(B, H, L, dilation, D), 2, 3)
        o, lse = _banded_window_attn(regroup(q), regroup(k), regroup(v), window // dilation)
        outs.append(jnp.swapaxes(o, 2, 3).reshape(B, H, T, D))
        lses.append(jnp.swapaxes(lse, 2, 3).reshape(B, H, T))
    wts = jax.nn.softmax(jnp.stack(lses, axis=0), axis=0)
    o = jnp.sum(wts[..., None] * jnp.stack(outs, axis=0), axis=0)
    return _merge_heads(o).astype(qkv.dtype)


def _diff_attention_group(qkv, lam_vecs, norm_w, lam_init):
    f32 = jnp.float32
    B, T, _ = qkv.shape
    q, k, v = jnp.split(qkv, [DIFF_QK_W, 2 * DIFF_QK_W], axis=-1)
    two_maps = lambda t: t.reshape(B, T, DIFF_HEADS, 2, DIFF_QK_DIM).transpose(0, 2, 3, 1, 4)
    q, k = two_maps(q), two_maps(k)
    v = _heads(v, DIFF_HEADS).astype(f32)
    lv = lam_vecs.astype(f32)
    lam = jnp.exp(jnp.sum(lv[0] * lv[1])) - jnp.exp(jnp.sum(lv[2] * lv[3])) + lam_init
    nq = T // ATTN_QBLOCK
    q_blocks = jnp.moveaxis(q.reshape(B, DIFF_HEADS, 2, nq, ATTN_QBLOCK, DIFF_QK_DIM), 3, 0)
    kpos = jnp.arange(T)
    scale = DIFF_QK_DIM ** -0.5

    def one_block(args):
        q_blk, bi = args
        s = jnp.einsum('bhmqd,bhmkd->bhmqk', q_blk, k).astype(f32) * scale
        qpos = bi * ATTN_QBLOCK + jnp.arange(ATTN_QBLOCK)
        s = jnp.where(kpos[None, :] <= qpos[:, None], s, -jnp.inf)
        p = jax.nn.softmax(s, axis=-1)
        return jnp.einsum('bhqk,bhkd->bhqd', p[:, :, 0] - lam * p[:, :, 1], v)

    o = lax.map(one_block, (q_blocks, jnp.arange(nq)))
    o = jnp.moveaxis(o, 0, 2).reshape(B, DIFF_HEADS, T, DIFF_V_DIM)
    o = _rms_norm(o, norm_w) * (1.0 - lam_init)
    return _merge_heads(o).astype(qkv.dtype)


def _token_mixer(h, w_in, conv_w, a_log, dt_bias, gdn_norm_w, lam_vecs, diff_norm_w, w_out, lam_init):
    proj = h @ w_in
    a_qkv, a_z, a_b, a_dec, b_qkv, c_qkv = jnp.split(proj, np.cumsum(IN_SPLITS)[:-1], axis=-1)
    y_a = _gated_deltanet(a_qkv, a_z, a_b, a_dec, conv_w, a_log, dt_bias, gdn_norm_w)
    y_b = _dilated_window_group(b_qkv)
    y_c = _diff_attention_group(c_qkv, lam_vecs, diff_norm_w, lam_init)
    return jnp.concatenate([y_a, y_b, y_c], axis=-1) @ w_out


def _conv_glu_ffn(h, w_up, conv_w, w_down):
    u = _causal_depthwise_conv(h @ w_up, conv_w)
    gate, val = jnp.split(u, 2, axis=-1)
    return (jax.nn.silu(gate) * val) @ w_down


def setup_inputs(seed: int = 0) -> dict:
    key = jax.random.key(seed)
    ks = jax.random.split(key, 16)
    f32 = jnp.float32
    nrm = lambda k, s: jax.random.normal(k, s, f32)
    x = nrm(ks[0], (BATCH, SEQ, D_MODEL))
    w_in = nrm(ks[1], (DEPTH, D_MODEL, IN_WIDTH)) * D_MODEL ** -0.5
    gdn_conv = nrm(ks[2], (DEPTH, GDN_CONV, 3 * GDN_W)) * GDN_CONV ** -0.5
    gdn_a_log = jnp.log(jax.random.uniform(ks[3], (DEPTH, GDN_HEADS), f32, 1.0, 16.0))
    dt = jnp.exp(jax.random.uniform(ks[4], (DEPTH, GDN_HEADS), f32, math.log(1e-3), math.log(1e-1)))
    gdn_dt_bias = dt + jnp.log(-jnp.expm1(-dt))
    gdn_norm = 1.0 + 0.02 * nrm(ks[5], (DEPTH, GDN_HEAD_DIM))
    diff_lambda = 0.1 * nrm(ks[6], (DEPTH, 4, DIFF_QK_DIM))
    diff_norm = 1.0 + 0.02 * nrm(ks[7], (DEPTH, DIFF_V_DIM))
    w_out = nrm(ks[8], (DEPTH, MIX_WIDTH, D_MODEL)) * MIX_WIDTH ** -0.5 * DEEPNORM_BETA
    ln1_g = 1.0 + 0.02 * nrm(ks[9], (DEPTH, D_MODEL))
    ln1_b = 0.02 * nrm(ks[10], (DEPTH, D_MODEL))
    w_up = nrm(ks[11], (DEPTH, D_MODEL, 2 * D_FF)) * D_MODEL ** -0.5
    ffn_conv = nrm(ks[12], (DEPTH, FFN_CONV, 2 * D_FF)) * FFN_CONV ** -0.5
    w_down = nrm(ks[13], (DEPTH, D_FF, D_MODEL)) * D_FF ** -0.5 * DEEPNORM_BETA
    ln2_g = 1.0 + 0.02 * nrm(ks[14], (DEPTH, D_MODEL))
    ln2_b = 0.02 * nrm(ks[15], (DEPTH, D_MODEL))
    return {"x": x, "w_in": w_in, "gdn_conv": gdn_conv, "gdn_a_log": gdn_a_log,
            "gdn_dt_bias": gdn_dt_bias, "gdn_norm": gdn_norm, "diff_lambda": diff_lambda,
            "diff_norm": diff_norm, "w_out": w_out, "ln1_g": ln1_g, "ln1_b": ln1_b,
            "w_up": w_up, "ffn_conv": ffn_conv, "w_down": w_down, "ln2_g": ln2_g, "ln2_b": ln2_b}


def reference(x, w_in, gdn_conv, gdn_a_log, gdn_dt_bias, gdn_norm, diff_lambda, diff_norm,
              w_out, ln1_g, ln1_b, w_up, ffn_conv, w_down, ln2_g, ln2_b):
    for l in range(DEPTH):
        lam_init = 0.8 - 0.6 * math.exp(-0.3 * l)
        y = _token_mixer(x, w_in[l], gdn_conv[l], gdn_a_log[l], gdn_dt_bias[l], gdn_norm[l],
                         diff_lambda[l], diff_norm[l], w_out[l], lam_init)
        x = _layer_norm(DEEPNORM_ALPHA * x + y, ln1_g[l], ln1_b[l])
        f = _conv_glu_ffn(x, w_up[l], ffn_conv[l], w_down[l])
        x = _layer_norm(DEEPNORM_ALPHA * x + f, ln2_g[l], ln2_b[l])
    return x
```

```python
import math
import numpy as np
from contextlib import ExitStack
import concourse.bass as bass
import concourse.mybir as mybir
from concourse.bass_utils import run_bass_kernel_spmd

F32 = mybir.dt.float32
BF16 = mybir.dt.bfloat16
U16 = mybir.dt.uint16
AF = mybir.ActivationFunctionType
ALU = mybir.AluOpType
AX = mybir.AxisListType


class Region:
    __slots__ = ("name", "w", "rs", "dsem", "dcnt")

    def __init__(self, name, dsem=None):
        self.name = name
        self.w = None
        self.rs = {}
        self.dsem = dsem
        self.dcnt = 0


class _Rec:
    def __init__(self):
        self.calls = []

    def __getattr__(self, name):
        def f(*a, **k):
            self.calls.append((name, a, k))
            return self
        return f


class _Eng:
    def __init__(self, name, eng, sem):
        self.name = name
        self.eng = eng
        self.sem = sem
        self.count = 0
        self.seen = {}
        self.ops = []


class Sched:
    def __init__(self, nc, st):
        self.nc = nc
        self.st = st
        self.E = {}
        for name, eng in (("pe", nc.tensor), ("act", nc.scalar), ("dve", nc.vector),
                          ("pool", nc.gpsimd), ("sp", nc.sync)):
            sem = st.enter_context(nc.semaphore("s_" + name))
            self.E[name] = _Eng(name, eng, sem)
        self.n_ops = 0
        self._uid = 0
        self.out_region = self.region("out", dma=True)

    def sb(self, name, shape, dtype):
        return self.st.enter_context(self.nc.sbuf_tensor("sb_" + name, shape, dtype))

    def ps(self, name, shape, dtype):
        return self.st.enter_context(self.nc.psum_tensor("ps_" + name, shape, dtype))

    def region(self, name, dma=False):
        dsem = None
        if dma:
            self._uid += 1
            dsem = self.st.enter_context(self.nc.semaphore("d%d_%s" % (self._uid, name)))
        return Region(name, dsem)

    def _need(self, me, tok, waits, raw):
        if tok is None:
            return
        kind, key, sem, cnt = tok
        if kind == "e" and key == me.name:
            if not raw or me.name in ("pe", "sp"):
                return
        if me.seen.get(key, 0) >= cnt:
            return
        me.seen[key] = cnt
        waits[key] = (sem, cnt)

    def _collect(self, me, reads, writes):
        waits = {}
        for r in reads:
            self._need(me, r.w, waits, True)
        for r in writes:
            self._need(me, r.w, waits, False)
            for t in r.rs.values():
                self._need(me, t, waits, False)
        return list(waits.values())

    def op(self, engname, fn, reads=(), writes=()):
        me = self.E[engname]
        waits = self._collect(me, reads, writes)
        me.count += 1
        tok = ("e", me.name, me.sem, me.count)
        for r in reads:
            r.rs[me.name] = tok
        for r in writes:
            r.w = tok
            r.rs = {}
        sem = me.sem
        rec = _Rec()
        fn(rec)
        assert len(rec.calls) == 1
        name, a, k = rec.calls[0]

        def emit(e, waits=waits, name=name, a=a, k=k, sem=sem):
            for s, v in waits:
                e.wait_ge(s, v)
            getattr(e, name)(*a, **k).then_inc(sem, 1)
        me.ops.append(emit)
        self.n_ops += 1

    def dma(self, queue, out_ap, in_ap, reads=(), writes=()):
        me = self.E[queue]
        waits = self._collect(me, reads, writes)
        tgt = None
        for r in writes:
            if r.dsem is not None:
                tgt = r
        assert tgt is not None, "dma needs a dma-region target"
        tgt.dcnt += 16
        tok = ("d", "d:" + tgt.name + str(id(tgt)), tgt.dsem, tgt.dcnt)
        for r in reads:
            r.rs[tok[1]] = tok
        for r in writes:
            r.w = tok
            r.rs = {}
        sem = tgt.dsem

        def emit(e, waits=waits, sem=sem, out_ap=out_ap, in_ap=in_ap):
            for s, v in waits:
                e.wait_ge(s, v)
            e.dma_start(out=out_ap, in_=in_ap).then_inc(sem, 16)
        me.ops.append(emit)
        self.n_ops += 1

    def finish(self):
        outr = self.out_region
        sp = self.E["sp"]
        total = outr.dcnt

        def fin(e, s=outr.dsem, v=total):
            e.wait_ge(s, v)
        sp.ops.append(fin)
        with self.nc.Block() as block:
            @block.tensor
            def _(e):
                for f in self.E["pe"].ops:
                    f(e)

            @block.scalar
            def _(e):
                for f in self.E["act"].ops:
                    f(e)

            @block.vector
            def _(e):
                for f in self.E["dve"].ops:
                    f(e)

            @block.gpsimd
            def _(e):
                for f in self.E["pool"].ops:
                    f(e)

            @block.sync
            def _(e):
                for f in self.E["sp"].ops:
                    f(e)


T = 2048
DM = 1024
NT = 16
PADX = 8
DEPTH = 4
DFF = 2752
NFT = 22
ALPHA = (2 * DEPTH) ** 0.25
EPS = 1e-5
NLEV = 7


class Flat3:
    def __init__(self, t, n, w):
        self.t, self.n, self.w = t, n, w

    def __getitem__(self, key):
        p, c, sl = key
        a = 0 if sl.start is None else sl.start
        b = self.w if sl.stop is None else sl.stop
        return self.t[p, c * self.w + a:c * self.w + b]


class Ring:
    def __init__(self, S, name, n, shape, dtype, psum=False, dma=False):
        self.b = []
        for i in range(n):
            nm = "%s%d" % (name, i)
            t = (S.ps if psum else S.sb)(nm, shape, dtype)
            self.b.append((t, S.region(nm, dma=dma)))
        self.i = 0

    def get(self):
        x = self.b[self.i % len(self.b)]
        self.i += 1
        return x


def q4(ap):
    return ap.rearrange("p (j f) -> p j f", j=4)


def host_consts():
    f = {}
    I = np.eye(128, dtype=np.float32)
    j = np.arange(128)[:, None]
    i = np.arange(128)[None, :]
    sel = np.zeros((128, 128), np.float32)
    sel[0, :] = 1.0
    sel[64, :] = 1.0
    cf = np.concatenate([I, -I, (j <= i).astype(np.float32), sel, np.ones((128, 640), np.float32)], axis=1)
    negl = np.where(i < j, -30000.0, 0.0).astype(np.float32)
    om, omt = [], []
    for L in range(NLEV):
        b = 1 << L
        m = ((j // (2 * b) == i // (2 * b)) & ((j // b) % 2 == 0) & ((i // b) % 2 == 1)).astype(np.float32)
        om.append(m)
        omt.append(m.T.copy())
    ik = np.arange(128)[:, None]
    wb = []
    for dl in range(-3, 9):
        d = 128 * dl + np.arange(128)[None, :] - ik
        w = ((d >= 0) & (d <= 128)).astype(np.float32) + ((d >= 0) & (d <= 512) & (d % 4 == 0)) + ((d >= 0) & (d % 16 == 0))
        wb.append(w.astype(np.float32))
    cb = []
    for dl in range(-3, 4):
        d = 128 * dl + np.arange(128)[None, :] - ik
        cb.append((d >= 0).astype(np.float32))
    cbf = np.concatenate([I, negl] + om + omt + wb + cb + [np.ones((128, 128), np.float32)], axis=1)
    return cf.astype(np.float32), cbf.astype(np.float32)


CF_ID, CF_NID, CF_UTRI, CF_SEL, CF_ONES = 0, 128, 256, 384, 512
CB_ID, CB_NEGL, CB_OM, CB_OMT, CB_WB, CB_CB, CB_ONES = 0, 128, 256, 256 + 896, 256 + 1792, 256 + 1792 + 1536, 256 + 1792 + 1536 + 896
NCF = 512 + 640
NCB = CB_ONES + 128


class Prog:
    def __init__(self, nlayers=DEPTH, dbg=None):
        self.nlayers = nlayers
        self.dbg = dbg
        nc = self.nc = bass.Bass("TRN2", target_bir_lowering=False)
        L = DEPTH
        D = {}

        def inp(name, shape, dt=F32):
            D[name] = nc.dram_tensor(name, shape, dt, kind="ExternalInput").ap()
        inp("x", [T, DM])
        inp("w_in_t", [L, 28, 128, 8, 128])
        inp("w_bd", [L, 128, 8, 8])
        inp("w_out", [L, DM, DM])
        inp("w_up_t", [L, NFT, 128, 8, 256])
        inp("w_down_t", [L, NFT, 128, DM])
        inp("gconv", [L, 128, 12, 4])
        inp("fconv", [L, 128, NFT, 2, 3])
        inp("ln1_g", [L, DM]); inp("ln1_b", [L, DM]); inp("ln2_g", [L, DM]); inp("ln2_b", [L, DM])
        inp("a_log", [L, 4]); inp("dt_bias", [L, 4]); inp("gnorm", [L, 128])
        inp("dlam", [L, 128]); inp("dnorm", [L, 128, 1])
        inp("cf", [128, NCF]); inp("cbf", [128, NCB])
        D["y"] = nc.dram_tensor("y", [T, DM], F32, kind="ExternalOutput").ap()
        self.D = D
        with ExitStack() as st:
            self.S = S = Sched(nc, st)
            self.alloc()
            self.prologue()
            if dbg == "pro":
                w, rw = self.W2.get()
                S.op("dve", lambda e: e.tensor_copy(w[:, 0:512], self.XT[:, 0, PADX:PADX + 512]), reads=self.rXT[0:4], writes=[rw])
                self.dump(0, w[:, 0:512], rw)
                self.dump(1, self.XT[:, 0, PADX:PADX + 512], self.rXT[0], bf=True)
                w2, rw2 = self.W2.get()
                S.op("dve", lambda e: e.tensor_copy(w2[:, 0:512], self.cb[:, 0:512]), reads=[self.rCb], writes=[rw2])
                self.dump(2, w2[:, 0:512], rw2)
                self.dump(3, self.X[:, 0, 0:512], self.rX[0])
                xb, rxb = self.XB.b[0]
                w3, rw3 = self.W2.get()
                S.op("dve", lambda e: e.tensor_copy(w3[:, 0:512], xb[:, 0:512]), reads=[rxb], writes=[rw3])
                self.dump(4, w3[:, 0:512], rw3)
                ps, rps = self.PA.get()
                psb = ps[:].bitcast(BF16)
                S.op("pe", lambda e: e.transpose(psb[:, 0:128], xb[:, 0:128], self.cB(CB_ID)), reads=[rxb, self.rCb], writes=[rps])
                w4, rw4 = self.W2.get()
                S.op("dve", lambda e: e.tensor_copy(w4[:, 0:128], psb[:, 0:128]), reads=[rps], writes=[rw4])
                S.op("act", lambda e: e.activation(w4[:, 128:256], psb[:, 0:128], AF.Copy), reads=[rps, rw4], writes=[rw4])
                S.op("dve", lambda e: e.tensor_copy(w4[:, 256:384], self.XT[:, 0, PADX + 1920:PADX + 2048]), reads=[self.rXT[15], rw4], writes=[rw4])
                self.dump(5, w4[:, 0:512], rw4)
                ps2, rps2 = self.PA.get()
                psb2 = ps2[:].bitcast(BF16)
                for c4 in range(4):
                    S.op("pe", lambda e, c4=c4: e.transpose(psb2[:, c4 * 128:(c4 + 1) * 128], xb[:, c4 * 128:(c4 + 1) * 128], self.cB(CB_ID)), reads=[rxb, self.rCb], writes=[rps2])
                b1, rb1 = self.B1.get()
                b2, rb2 = self.B1.get()
                b3, rb3 = self.B1.get()
                S.op("dve", lambda e: e.tensor_copy(b1[:], psb2[:, 0:512]), reads=[rps2], writes=[rb1])
                S.op("dve", lambda e: e.tensor_scalar(b2[:], psb2[:, 0:512], 1.0, None, op0=ALU.mult), reads=[rps2], writes=[rb2])
                S.op("act", lambda e: e.activation(b3[:], psb2[:, 0:512], AF.Identity), reads=[rps2], writes=[rb3])
                for k, (bb, rbb) in enumerate(((b1, rb1), (b2, rb2), (b3, rb3))):
                    wk, rwk = self.W2.get()
                    S.op("dve", lambda e, wk=wk, bb=bb: e.tensor_copy(wk[:, 0:512], bb[:]), reads=[rbb], writes=[rwk])
                    self.dump(6 + k, wk[:, 0:512], rwk)
            else:
                for l in range(nlayers):
                    self.layer(l)
            S.finish()

    def alloc(self):
        S = self.S
        self.X = S.sb("X", [128, NT, DM], F32)
        self.rX = [S.region("X%d" % t) for t in range(NT)]
        self.XT = Flat3(S.sb("XT", [128, 8 * (PADX + T)], BF16), 8, PADX + T)
        self.rXT = [S.region("XT%d" % t) for t in range(NT)]
        self.rXTpad = S.region("XTpad")
        self.PA = Ring(S, "pa", 6, [128, 512], F32, psum=True)
        self.PB = Ring(S, "pb", 2, [128, 512], F32, psum=True)
        self.W2 = Ring(S, "w2", 4, [128, 516], F32)
        self.B1 = Ring(S, "b1", 5, [128, 512], BF16)
        self.A4 = Ring(S, "a4", 5, [128, T], BF16, dma=True)
        self.VT = Ring(S, "vt", 1, [128, NT, 192], BF16)
        self.Wt = Ring(S, "wt", 4, [128, 8, 128], BF16, dma=True)
        self.Wup = Ring(S, "wup", 2, [128, 8, 256], BF16, dma=True)
        self.Wrow = Ring(S, "wrow", 4, [128, DM], BF16, dma=True)
        self.TN = Ring(S, "tn", 2, [128, 512], BF16)
        self.XB = Ring(S, "xb", 1, [128, DM], BF16)
        self.LNS = Ring(S, "lns", 3, [128, 16], F32)
        self.gq = {}
        for n in "qT kT vT vtok DU EGr Np Mp X4 Y4 kg kt wTn qkT qdec vn".split():
            self.gq[n] = (S.sb("gq_" + n, [128, 512], BF16), S.region("gq_" + n))
        self.gacc = {n: (S.sb("gacc_" + n, [128, 512], F32), S.region("gacc_" + n)) for n in "qkv"}
        self.HALO = S.sb("halo", [128, 3, 4], F32)
        self.rHALO = [S.region("halo%d" % i) for i in range(3)]
        self.SM = S.sb("SM", [128, 8, NT, 4], F32)
        self.rSM = S.region("SM")
        self.aexp = S.sb("aexp", [128, 4], F32)
        self.raexp = S.region("aexp")
        self.Sf = S.sb("Sf", [128, 128], F32)
        self.rSf = S.region("Sf")
        self.Sb = Ring(S, "Sb", 2, [128, 128], BF16)
        self.ssq = S.sb("ssq", [128, 8], F32)
        self.rssq = S.region("ssq")
        self.cwg = S.sb("cwg", [128, 12, 4], F32)
        self.cwf = S.sb("cwf", [128, NFT, 2, 3], F32)
        self.alog = S.sb("alog", [128, 4], F32)
        self.dtb = S.sb("dtb", [128, 4], F32)
        self.gnw = S.sb("gnw", [128, 128], F32)
        self.lv = S.sb("lv", [128, 128], F32)
        self.dnw = S.sb("dnw", [128, 1], F32)
        self.rLP = S.region("LP", dma=True)
        self.wbd = S.sb("wbd", [128, 8, 8], BF16)
        self.rwbd = S.region("wbd", dma=True)
        self.lam = S.sb("lam", [128, 8], F32)
        self.rlam = S.region("lam")
        self.rrow = S.sb("rrow", [128, 512], F32)
        self.rrrow = S.region("rrow")
        self.cf = S.sb("cf", [128, NCF], F32)
        self.cb = S.sb("cb", [128, NCB], BF16)
        self.rC = S.region("C", dma=True)
        self.rCb = S.region("Cb", dma=True)
        self.eps5 = S.sb("eps5", [128, 2], F32)
        self.reps = S.region("eps")
        self.rXload = [S.region("xl%d" % i, dma=True) for i in range(4)]

    def cF(self, off, n=128):
        return self.cf[:, off:off + n]

    def cB(self, off, n=128):
        return self.cb[:, off:off + n]

    def prologue(self):
        S, D = self.S, self.D
        S.dma("sp", self.cf[:], D["cf"], writes=[self.rC])
        S.dma("pool", self.cb[:], D["cbf"], writes=[self.rCb])
        for i in range(4):
            for t in range(4 * i, 4 * i + 4):
                S.dma("sp", self.X[:, t, :], D["x"][t * 128:(t + 1) * 128, :], writes=[self.rXload[i]])
        for c in range(8):
            S.op("pool", lambda e, c=c: e.memset(self.XT[:, c, 0:PADX], 0.0), writes=[self.rXTpad])
        S.op("pool", lambda e: e.memset(self.eps5[:, 0:1], EPS), writes=[self.reps])
        S.op("pool", lambda e: e.memset(self.eps5[:, 1:2], 1e-6), reads=[self.reps], writes=[self.reps])
        S.op("pool", lambda e: e.memset(self.rrow[:], 0.0), writes=[self.rrrow])
        for t in range(NT):
            self.x_tile_out(t, extra_reads=[self.rXload[t // 4]], scale=True)

    def x_tile_out(self, t, extra_reads=(), scale=True):
        S = self.S
        xb, rxb = self.XB.get()
        xbv = xb[:, 0:DM]
        S.op("act", lambda e: e.activation(xbv, self.X[:, t, :], AF.Copy), reads=[self.rX[t]] + list(extra_reads), writes=[rxb])
        for half in range(2):
            ps, rps = self.PA.get()
            psb = ps[:].bitcast(BF16)
            for c4 in range(4):
                c = half * 4 + c4
                S.op("pe", lambda e, c=c, c4=c4: e.transpose(psb[:, c4 * 128:(c4 + 1) * 128], xb[:, c * 128:(c + 1) * 128], self.cB(CB_ID)),
                     reads=[rxb, self.rCb], writes=[rps])
            for c4 in range(4):
                c = half * 4 + c4
                dst = self.XT[:, c, PADX + t * 128:PADX + (t + 1) * 128]
                src = psb[:, c4 * 128:(c4 + 1) * 128]
                if c4 % 2 == 0:
                    S.op("dve", lambda e, dst=dst, src=src: e.tensor_copy(dst, src), reads=[rps, self.rXT[t]], writes=[self.rXT[t]])
                else:
                    S.op("act", lambda e, dst=dst, src=src: e.activation(dst, src, AF.Copy), reads=[rps, self.rXT[t]], writes=[self.rXT[t]])
        if scale:
            S.op("pool", lambda e: e.tensor_scalar(self.X[:, t, :], self.X[:, t, :], ALPHA, None, op0=ALU.mult),
                 reads=[self.rX[t]] + list(extra_reads), writes=[self.rX[t]])

    def load_wt(self, l, ti):
        w, rw = self.Wt.get()
        self.S.dma("pool", w[:], self.D["w_in_t"][l, ti], writes=[rw])
        return w, rw

    def layer(self, l):
        S, D = self.S, self.D
        lam_init = 0.8 - 0.6 * math.exp(-0.3 * l)
        rLP = self.rLP
        S.dma("sp", self.cwg[:], D["gconv"][l], writes=[rLP])
        S.dma("sp", self.cwf[:], D["fconv"][l], writes=[rLP])
        S.dma("sp", self.alog[:], D["a_log"][l:l + 1, :].partition_broadcast(128), writes=[rLP])
        S.dma("sp", self.dtb[:], D["dt_bias"][l:l + 1, :].partition_broadcast(128), writes=[rLP])
        S.dma("sp", self.gnw[:], D["gnorm"][l:l + 1, :].partition_broadcast(128), writes=[rLP])
        S.dma("sp", self.lv[:], D["dlam"][l:l + 1, :].partition_broadcast(128), writes=[rLP])
        S.dma("sp", self.dnw[:], D["dnorm"][l], writes=[rLP])
        S.dma("pool", self.wbd[:], D["w_bd"][l], writes=[self.rwbd])
        self.lam_calc(lam_init)
        self.gdn_prep(l)
        outs = []
        for h in range(4):
            outs.append(self.gdn_head(l, h))
            if self.dbg == "g0":
                return
            if h % 2 == 1:
                self.wout_partial(l, [h - 1, h], outs[-2:])
        if self.dbg == "gdn":
            return self.dump_X()
        o = [self.attn_pair(l, p, diff=False) for p in range(2)]
        self.wout_partial(l, [4, 5], o)
        if self.dbg == "dsw":
            return self.dump_X()
        o = [self.attn_pair(l, p, diff=True) for p in range(2)]
        self.wout_partial(l, [6, 7], o)
        if self.dbg == "mix":
            return self.dump_X()
        self.layernorm(l, 1, last=False)
        if self.dbg == "ln1":
            return self.dump_X()
        self.ffn(l)
        if self.dbg == "ffn":
            return self.dump_X()
        self.layernorm(l, 2, last=(l == self.nlayers - 1))

    def dump(self, idx, ap, reg, bf=False):
        n = 128 if ap.shape[-1] == 128 else 512
        dst = self.D["y"][(idx % 16) * 128:(idx % 16 + 1) * 128, 512 * (idx // 16):512 * (idx // 16) + n]
        self.S.dma("pool" if bf else "sp", dst, ap, reads=[reg], writes=[self.S.out_region])

    def dump_X(self):
        for t in range(NT):
            self.S.dma("sp", self.D["y"][t * 128:(t + 1) * 128, :], self.X[:, t, :], reads=[self.rX[t]], writes=[self.S.out_region])

    def lam_calc(self, lam_init):
        S = self.S
        lam, r = self.lam, self.rlam
        w, rw = self.W2.get()
        S.op("dve", lambda e: e.tensor_tensor(w[:, 0:32], self.lv[:, 0:32], self.lv[:, 32:64], ALU.mult), reads=[self.rLP], writes=[rw])
        S.op("dve", lambda e: e.tensor_tensor(w[:, 32:64], self.lv[:, 64:96], self.lv[:, 96:128], ALU.mult), reads=[self.rLP, rw], writes=[rw])
        S.op("dve", lambda e: e.tensor_reduce(lam[:, 0:2], w[:, 0:64].rearrange("p (a b) -> p a b", a=2), axis=AX.X, op=ALU.add), reads=[rw, r], writes=[r])
        S.op("act", lambda e: e.activation(lam[:, 2:4], lam[:, 0:2], AF.Exp), reads=[r], writes=[r])
        S.op("dve", lambda e: e.tensor_tensor(lam[:, 4:5], lam[:, 2:3], lam[:, 3:4], ALU.subtract), reads=[r], writes=[r])
        S.op("dve", lambda e: e.tensor_scalar(lam[:, 5:6], lam[:, 4:5], lam_init, -1.0, op0=ALU.add, op1=ALU.mult), reads=[r], writes=[r])
        S.op("dve", lambda e: e.tensor_scalar(lam[:, 6:7], self.dnw[:, 0:1], 1.0 - lam_init, None, op0=ALU.mult), reads=[r, self.rLP], writes=[r])
        S.op("act", lambda e: e.activation(self.aexp[:], self.alog[:], AF.Exp), reads=[self.rLP], writes=[self.raexp])

    def gdn_prep(self, l):
        S, SM, r = self.S, self.SM, self.rSM
        ps, rps = self.PA.get()
        for t in range(NT):
            cs = slice(PADX + t * 128, PADX + (t + 1) * 128)
            for c in range(8):
                S.op("pe", lambda e, c=c, cs=cs, t=t: e.matmul(ps[:, t * 8:(t + 1) * 8], lhsT=self.XT[:, c, cs], rhs=self.wbd[:, c, :],
                                                                 start=(c == 0), stop=(c == 7)),
                     reads=[self.rXT[t], self.rwbd], writes=[rps])
        psv = ps[:, 0:128].rearrange("p (t c) -> p t c", c=8)
        G, BE, GC, GL, EG, EGL, EKT, TM = [SM[:, k, :, :] for k in range(8)]
        dtb = self.dtb[:].unsqueeze(1).broadcast_to([128, NT, 4])
        aex = self.aexp[:].unsqueeze(1).broadcast_to([128, NT, 4])
        S.op("act", lambda e: e.activation(BE, psv[:, :, 0:4], AF.Exp, scale=-1.0), reads=[rps], writes=[r])
        S.op("dve", lambda e: e.tensor_scalar(BE, BE, 1.0, None, op0=ALU.add), reads=[r], writes=[r])
        S.op("dve", lambda e: e.reciprocal(BE, BE), reads=[r], writes=[r])
        S.op("dve", lambda e: e.tensor_tensor(TM, psv[:, :, 4:8], dtb, ALU.add), reads=[rps, self.rLP, r], writes=[r])
        S.op("act", lambda e: e.activation(TM, TM, AF.Exp), reads=[r], writes=[r])
        S.op("dve", lambda e: e.tensor_scalar(GC, TM, 2.0, None, op0=ALU.add), reads=[r], writes=[r])
        S.op("dve", lambda e: e.reciprocal(GC, GC), reads=[r], writes=[r])
        S.op("dve", lambda e: e.tensor_tensor(GC, GC, TM, ALU.mult), reads=[r], writes=[r])
        S.op("dve", lambda e: e.tensor_tensor(GL, GC, GC, ALU.mult), reads=[r], writes=[r])
        S.op("dve", lambda e: e.tensor_scalar(TM, GL, 1.0 / 11.0, None, op0=ALU.mult), reads=[r], writes=[r])
        for cst in (1.0 / 9.0, 1.0 / 7.0, 1.0 / 5.0, 1.0 / 3.0):
            S.op("dve", lambda e, cst=cst: e.scalar_tensor_tensor(TM, TM, cst, GL, op0=ALU.add, op1=ALU.mult), reads=[r], writes=[r])
        S.op("dve", lambda e: e.scalar_tensor_tensor(TM, TM, 1.0, GC, op0=ALU.add, op1=ALU.mult), reads=[r], writes=[r])
        S.op("dve", lambda e: e.scalar_tensor_tensor(G, TM, -2.0, aex, op0=ALU.mult, op1=ALU.mult), reads=[r, self.raexp], writes=[r])
        ps2, rps2 = self.PA.get()
        for t in range(NT):
            S.op("pe", lambda e, t=t: e.matmul(ps2[:, t * 4:(t + 1) * 4], lhsT=self.cF(CF_UTRI), rhs=SM[:, 0, t, :], start=True, stop=True),
                 reads=[r, self.rC], writes=[rps2])
            S.op("pe", lambda e, t=t: e.matmul(ps2[:, 64 + t * 4:64 + (t + 1) * 4], lhsT=self.cF(CF_ONES), rhs=SM[:, 0, t, :], start=True, stop=True),
                 reads=[r, self.rC], writes=[rps2])
        S.op("act", lambda e: e.activation(SM[:, 2:4, :, :].rearrange("p a t c -> p (a t c)"), ps2[:, 0:128], AF.Copy), reads=[rps2], writes=[r])
        S.op("act", lambda e: e.activation(SM[:, 4:6, :, :].rearrange("p a t c -> p (a t c)"),
                                           SM[:, 2:4, :, :].rearrange("p a t c -> p (a t c)"), AF.Exp), reads=[r], writes=[r])
        S.op("dve", lambda e: e.tensor_tensor(TM, GL, GC, ALU.subtract), reads=[r], writes=[r])
        S.op("act", lambda e: e.activation(EKT, TM, AF.Exp), reads=[r], writes=[r])

    def gdn_head(self, l, h):
        S = self.S
        Wq = self.load_wt(l, h)
        Wk = self.load_wt(l, 4 + h)
        Wv = self.load_wt(l, 8 + h)
        Wz = self.load_wt(l, 12 + h)
        out, rout = self.A4.get()
        S.op("pool", lambda e: e.memset(self.Sf[:], 0.0), writes=[self.rSf])
        sb0 = self.Sb.get()
        S.op("pool", lambda e: e.memset(sb0[0][:], 0.0), writes=[sb0[1]])
        self.curSb = sb0
        for g in range(4):
            self.gdn_group(l, h, g, Wq, Wk, Wv, Wz, out, rout)
            if self.dbg == "g0":
                gq = self.gq
                self.dump(0, self.SM[:].rearrange("p a t c -> p (a t c)"), self.rSM)
                for i, n in enumerate("qT kT vT vtok DU EGr Np Mp X4 Y4 kg kt wTn qkT qdec vn".split()):
                    self.dump(1 + i, gq[n][0][:], gq[n][1], bf=True)
                self.dump(17, out[:, 0:512], rout, bf=True)
                self.dump(18, self.Sf[:], self.rSf)
                self.dump(19, self.gacc["q"][0][:], self.gacc["q"][1])
                self.dump(20, self.X[:, 0, 0:512], self.rX[0])
                self.dump(21, self.XT[:, 0, PADX:PADX + 512], self.rXT[0], bf=True)
                return None
        return out, rout

    def gdn_group(self, l, h, g, Wq, Wk, Wv, Wz, out, rout):
        S, SM, gq = self.S, self.SM, self.gq
        rSM, rC, rCb, rLP = self.rSM, self.rC, self.rCb, self.rLP
        sl = slice(PADX + 512 * g, PADX + 512 * (g + 1))
        rxt = self.rXT[4 * g:4 * g + 4]
        t0 = 4 * g
        for wi, (nm, W, ci) in enumerate((("q", Wq, h), ("k", Wk, 4 + h), ("v", Wv, 8 + h))):
            ps, rps = self.PA.get()
            for c in range(8):
                S.op("pe", lambda e, c=c, W=W, ps=ps: e.matmul(ps[:], lhsT=W[0][:, c, :], rhs=self.XT[:, c, sl], start=(c == 0), stop=(c == 7)),
                     reads=[W[1]] + rxt, writes=[rps])
            st, rst = self.W2.get()
            S.op("act", lambda e, st=st, ps=ps: e.activation(st[:, 3:515], ps[:], AF.Copy), reads=[rps], writes=[rst])
            if g == 0:
                S.op("pool", lambda e, st=st: e.memset(st[:, 0:3], 0.0), reads=[rst], writes=[rst])
            else:
                S.op("dve", lambda e, st=st, wi=wi: e.tensor_copy(st[:, 0:3], self.HALO[:, wi, 0:3]), reads=[self.rHALO[wi], rst], writes=[rst])
            S.op("act", lambda e, ps=ps, wi=wi: e.activation(self.HALO[:, wi, 0:3], ps[:, 509:512], AF.Copy), reads=[rps], writes=[self.rHALO[wi]])
            acc, racc = self.gacc[nm]
            S.op("dve", lambda e, acc=acc, st=st, ci=ci: e.tensor_scalar(acc[:], st[:, 3:515], self.cwg[:, ci, 3:4], None, op0=ALU.mult),
                 reads=[rst, rLP], writes=[racc])
            for j in (2, 1, 0):
                S.op("dve", lambda e, acc=acc, st=st, ci=ci, j=j: e.scalar_tensor_tensor(acc[:], st[:, j:j + 512], self.cwg[:, ci, j:j + 1], acc[:],
                                                                                         op0=ALU.mult, op1=ALU.add),
                     reads=[rst, racc, rLP], writes=[racc])
        aq, raq = self.gacc["q"]
        ak, rak = self.gacc["k"]
        av, rav = self.gacc["v"]
        S.op("act", lambda e: e.activation(aq[:], aq[:], AF.Silu), reads=[raq], writes=[raq])
        S.op("act", lambda e: e.activation(ak[:], ak[:], AF.Silu), reads=[rak], writes=[rak])
        S.op("act", lambda e: e.activation(gq["vT"][0][:], av[:], AF.Silu), reads=[rav], writes=[gq["vT"][1]])
        for nm, acc, racc in (("q", aq, raq), ("k", ak, rak)):
            sq, rsq = self.B1.get()
            S.op("act", lambda e, sq=sq, acc=acc: e.activation(sq[:], acc[:], AF.Square), reads=[racc], writes=[rsq])
            ps, rps = self.PA.get()
            S.op("pe", lambda e, ps=ps, sq=sq: e.matmul(ps[:], lhsT=self.cB(CB_ONES), rhs=sq[:], start=True, stop=True), reads=[rsq, rCb], writes=[rps])
            S.op("act", lambda e, ps=ps: e.activation(ps[:], ps[:], AF.Ln, bias=self.eps5[:, 1:2]), reads=[rps, self.reps], writes=[rps])
            S.op("act", lambda e, ps=ps: e.activation(ps[:], ps[:], AF.Exp, scale=-0.5), reads=[rps], writes=[rps])
            if nm == "q":
                S.op("dve", lambda e, ps=ps, acc=acc: e.scalar_tensor_tensor(gq["qT"][0][:], acc[:], 128.0 ** -0.5, ps[:], op0=ALU.mult, op1=ALU.mult),
                     reads=[racc, rps], writes=[gq["qT"][1]])
            else:
                S.op("dve", lambda e, ps=ps, acc=acc: e.tensor_tensor(gq["kT"][0][:], acc[:], ps[:], ALU.mult), reads=[racc, rps], writes=[gq["kT"][1]])
        qT, rqT = gq["qT"]
        kT, rkT = gq["kT"]
        vT, rvT = gq["vT"]
        bc = lambda k: SM[:, k, t0:t0 + 4, h:h + 1].broadcast_to([128, 4, 128])
        ps, rps = self.PA.get()
        psb = ps[:].bitcast(BF16)
        for j in range(4):
            S.op("pe", lambda e, j=j, psb=psb: e.transpose(psb[:, j * 128:(j + 1) * 128], kT[:, j * 128:(j + 1) * 128], self.cB(CB_ID)),
                 reads=[rkT, rCb], writes=[rps])
        S.op("dve", lambda e, psb=psb: e.tensor_tensor(q4(gq["kg"][0][:]), q4(psb[:, 0:512]), bc(4), ALU.mult), reads=[rps, rSM], writes=[gq["kg"][1]])
        S.op("dve", lambda e, psb=psb: e.tensor_tensor(q4(gq["kt"][0][:]), q4(psb[:, 0:512]), bc(6), ALU.mult), reads=[rps, rSM], writes=[gq["kt"][1]])
        ps, rps = self.PA.get()
        psb2 = ps[:].bitcast(BF16)
        for j in range(4):
            S.op("pe", lambda e, j=j, psb2=psb2: e.transpose(psb2[:, j * 128:(j + 1) * 128], vT[:, j * 128:(j + 1) * 128], self.cB(CB_ID)),
                 reads=[rvT, rCb], writes=[rps])
        S.op("act", lambda e, psb2=psb2: e.activation(gq["vtok"][0][:], psb2[:, 0:512], AF.Copy), reads=[rps], writes=[gq["vtok"][1]])
        gcb, rgcb = self.W2.get()
        gcb4 = q4(gcb[:, 0:512])
        S.op("dve", lambda e: e.tensor_tensor(gcb4, q4(self.cF(CF_ONES, 512)), bc(2), ALU.mult), reads=[rC, rSM], writes=[rgcb])
        ps, rps = self.PA.get()
        for j in range(4):
            S.op("pe", lambda e, j=j, ps=ps: e.matmul(ps[:, j * 128:(j + 1) * 128], lhsT=gcb[:, j * 128:(j + 1) * 128], rhs=self.cF(CF_ID), start=True, stop=True),
                 reads=[rgcb, rC], writes=[rps])
        S.op("act", lambda e, ps=ps: e.activation(gq["EGr"][0][:], ps[:], AF.Exp), reads=[rps], writes=[gq["EGr"][1]])
        ps, rps = self.PA.get()
        for j in range(4):
            o = ps[:, j * 128:(j + 1) * 128]
            S.op("pe", lambda e, j=j, o=o: e.matmul(o, lhsT=gcb[:, j * 128:(j + 1) * 128], rhs=self.cF(CF_ID), start=True, stop=False), reads=[rgcb, rC], writes=[rps])
            S.op("pe", lambda e, j=j, o=o: e.matmul(o, lhsT=self.cF(CF_NID), rhs=gcb[:, j * 128:(j + 1) * 128], start=False, stop=False), reads=[rgcb, rC], writes=[rps])
            S.op("pe", lambda e, j=j, o=o: e.matmul(o, lhsT=self.cB(CB_ID), rhs=self.cB(CB_NEGL), start=False, stop=True), reads=[rCb], writes=[rps])
        DU, rDU = gq["DU"]
        S.op("act", lambda e, ps=ps: e.activation(DU[:], ps[:], AF.Exp), reads=[rps], writes=[rDU])
        Np, rNp = gq["Np"]
        Mp, rMp = gq["Mp"]
        ps, rps = self.PA.get()
        for j in range(4):
            S.op("pe", lambda e, j=j, ps=ps: e.matmul(ps[:, j * 128:(j + 1) * 128], lhsT=kT[:, j * 128:(j + 1) * 128], rhs=kT[:, j * 128:(j + 1) * 128], start=True, stop=True),
                 reads=[rkT], writes=[rps])
        S.op("dve", lambda e, ps=ps: e.tensor_tensor(Np[:], ps[:], DU[:], ALU.mult), reads=[rps, rDU], writes=[rNp])
        S.op("dve", lambda e: e.tensor_tensor(q4(Np[:]), q4(Np[:]), bc(1), ALU.mult), reads=[rNp, rSM], writes=[rNp])
        ps, rps = self.PA.get()
        psb3 = ps[:].bitcast(BF16)
        for j in range(4):
            S.op("pe", lambda e, j=j, psb3=psb3: e.transpose(psb3[:, j * 128:(j + 1) * 128], Np[:, j * 128:(j + 1) * 128], self.cB(CB_ID)),
                 reads=[rNp, rCb], writes=[rps])
        S.op("act", lambda e, psb3=psb3: e.activation(Mp[:], psb3[:, 0:512], AF.Copy), reads=[rps], writes=[rMp])
        X4, rX4 = gq["X4"]
        Y4, rY4 = gq["Y4"]
        idb = self.cB(CB_ID).unsqueeze(1).broadcast_to([128, 4, 128])
        S.op("pool", lambda e: e.tensor_copy(q4(X4[:]), idb), reads=[rCb], writes=[rX4])
        S.op("pool", lambda e: e.tensor_copy(q4(Y4[:]), idb), reads=[rCb], writes=[rY4])
        for L in range(NLEV):
            last = (L == NLEV - 1)
            p1, rp1 = self.PA.get()
            for j in range(4):
                S.op("pe", lambda e, j=j, p1=p1: e.matmul(p1[:, j * 128:(j + 1) * 128], lhsT=Mp[:, j * 128:(j + 1) * 128], rhs=X4[:, j * 128:(j + 1) * 128], start=True, stop=True),
                     reads=[rMp, rX4], writes=[rp1])
            t1, rt1 = self.TN.get()
            S.op("act", lambda e, t1=t1, p1=p1: e.activation(t1[:], p1[:], AF.Identity, scale=-1.0), reads=[rp1], writes=[rt1])
            if not last:
                p2, rp2 = self.PA.get()
                for j in range(4):
                    S.op("pe", lambda e, j=j, p2=p2: e.matmul(p2[:, j * 128:(j + 1) * 128], lhsT=Np[:, j * 128:(j + 1) * 128], rhs=Y4[:, j * 128:(j + 1) * 128], start=True, stop=True),
                         reads=[rNp, rY4], writes=[rp2])
                t2, rt2 = self.TN.get()
                S.op("act", lambda e, t2=t2, p2=p2: e.activation(t2[:], p2[:], AF.Identity, scale=-1.0), reads=[rp2], writes=[rt2])
            p3, rp3 = self.PA.get()
            for j in range(4):
                S.op("pe", lambda e, j=j, p3=p3, t1=t1: e.matmul(p3[:, j * 128:(j + 1) * 128], lhsT=Y4[:, j * 128:(j + 1) * 128], rhs=t1[:, j * 128:(j + 1) * 128], start=True, stop=True),
                     reads=[rY4, rt1], writes=[rp3])
            if not last:
                p4, rp4 = self.PA.get()
                for j in range(4):
                    S.op("pe", lambda e, j=j, p4=p4, t2=t2: e.matmul(p4[:, j * 128:(j + 1) * 128], lhsT=X4[:, j * 128:(j + 1) * 128], rhs=t2[:, j * 128:(j + 1) * 128], start=True, stop=True),
                         reads=[rX4, rt2], writes=[rp4])
            om = self.cB(CB_OM + 128 * L).bitcast(U16).unsqueeze(1).broadcast_to([128, 4, 128])
            S.op("dve", lambda e, om=om, p3=p3: e.copy_predicated(q4(X4[:]), om, q4(p3[:])), reads=[rp3, rCb, rX4], writes=[rX4])
            if not last:
                omt = self.cB(CB_OMT + 128 * L).bitcast(U16).unsqueeze(1).broadcast_to([128, 4, 128])
                S.op("dve", lambda e, omt=omt, p4=p4: e.copy_predicated(q4(Y4[:]), omt, q4(p4[:])), reads=[rp4, rCb, rY4], writes=[rY4])
        kg, rkg = gq["kg"]
        kt, rkt = gq["kt"]
        wTn, rwTn = gq["wTn"]
        qkT, rqkT = gq["qkT"]
        qdec, rqdec = gq["qdec"]
        vtok, rvtok = gq["vtok"]
        vn, rvn = gq["vn"]
        ps, rps = self.PA.get()
        for j in range(4):
            S.op("pe", lambda e, j=j, ps=ps: e.matmul(ps[:, j * 128:(j + 1) * 128], lhsT=kg[:, j * 128:(j + 1) * 128], rhs=X4[:, j * 128:(j + 1) * 128], start=True, stop=True),
                 reads=[rkg, rX4], writes=[rps])
        S.op("act", lambda e, ps=ps: e.activation(wTn[:], ps[:], AF.Identity, scale=-1.0), reads=[rps], writes=[rwTn])
        ps, rps = self.PA.get()
        for j in range(4):
            S.op("pe", lambda e, j=j, ps=ps: e.matmul(ps[:, j * 128:(j + 1) * 128], lhsT=kT[:, j * 128:(j + 1) * 128], rhs=qT[:, j * 128:(j + 1) * 128], start=True, stop=True),
                 reads=[rkT, rqT], writes=[rps])
        S.op("dve", lambda e, ps=ps: e.tensor_tensor(qkT[:], ps[:], DU[:], ALU.mult), reads=[rps, rDU], writes=[rqkT])
        S.op("dve", lambda e: e.tensor_tensor(qdec[:], qT[:], gq["EGr"][0][:], ALU.mult), reads=[rqT, gq["EGr"][1]], writes=[rqdec])
        pz, rpz = self.PA.get()
        for j in range(4):
            cs = slice(PADX + (t0 + j) * 128, PADX + (t0 + j + 1) * 128)
            for c in range(8):
                S.op("pe", lambda e, j=j, c=c, cs=cs, pz=pz: e.matmul(pz[:, j * 128:(j + 1) * 128], lhsT=self.XT[:, c, cs], rhs=Wz[0][:, c, :], start=(c == 0), stop=(c == 7)),
                     reads=[Wz[1], self.rXT[t0 + j]], writes=[rpz])
        sz, rsz = self.W2.get()
        S.op("act", lambda e, pz=pz: e.activation(sz[:, 0:512], pz[:], AF.Silu), reads=[rpz], writes=[rsz])
        gnb = self.gnw[:].unsqueeze(1).broadcast_to([128, 4, 128])
        S.op("pool", lambda e: e.tensor_tensor(q4(sz[:, 0:512]), q4(sz[:, 0:512]), gnb, ALU.mult), reads=[rsz, rLP], writes=[rsz])
        po, rpo = self.PB.get()
        for j in range(4):
            t = t0 + j
            js = slice(j * 128, (j + 1) * 128)
            sb, rsb = self.curSb
            pv, rpv = self.PA.get()
            S.op("pe", lambda e, js=js, pv=pv: e.matmul(pv[:, 0:128], lhsT=X4[:, js], rhs=vtok[:, js], start=True, stop=False), reads=[rX4, rvtok], writes=[rpv])
            S.op("pe", lambda e, js=js, pv=pv, sb=sb: e.matmul(pv[:, 0:128], lhsT=wTn[:, js], rhs=sb[:], start=False, stop=True), reads=[rwTn, rsb], writes=[rpv])
            S.op("act", lambda e, js=js, pv=pv, t=t: e.activation(vn[:, js], pv[:, 0:128], AF.Identity, scale=SM[:, 1, t, h:h + 1]), reads=[rpv, rSM, rvn], writes=[rvn])
            S.op("pe", lambda e, js=js, sb=sb: e.matmul(po[:, js], lhsT=qdec[:, js], rhs=sb[:], start=True, stop=False), reads=[rqdec, rsb], writes=[rpo])
            S.op("pe", lambda e, js=js: e.matmul(po[:, js], lhsT=qkT[:, js], rhs=vn[:, js], start=False, stop=True), reads=[rqkT, rvn], writes=[rpo])
            pS, rpS = self.PA.get()
            S.op("pe", lambda e, js=js, pS=pS: e.matmul(pS[:, 0:128], lhsT=kt[:, js], rhs=vn[:, js], start=True, stop=True), reads=[rkt, rvn], writes=[rpS])
            S.op("dve", lambda e, pS=pS, t=t: e.scalar_tensor_tensor(self.Sf[:], self.Sf[:], SM[:, 5, t, h:h + 1], pS[:, 0:128], op0=ALU.mult, op1=ALU.add),
                 reads=[rpS, rSM, self.rSf], writes=[self.rSf])
            nsb = self.Sb.get()
            S.op("act", lambda e, nsb=nsb: e.activation(nsb[0][:], self.Sf[:], AF.Copy), reads=[self.rSf], writes=[nsb[1]])
            self.curSb = nsb
        osq, rosq = self.B1.get()
        S.op("act", lambda e: e.activation(osq[:], po[:], AF.Square), reads=[rpo], writes=[rosq])
        S.op("dve", lambda e: e.tensor_reduce(self.ssq[:, 0:4], q4(osq[:]), axis=AX.X, op=ALU.add), reads=[rosq, self.rssq], writes=[self.rssq])
        S.op("act", lambda e: e.activation(self.ssq[:, 0:4], self.ssq[:, 0:4], AF.Ln, scale=1.0 / 128.0, bias=self.eps5[:, 0:1]), reads=[self.rssq, self.reps], writes=[self.rssq])
        S.op("act", lambda e: e.activation(self.ssq[:, 4:8], self.ssq[:, 0:4], AF.Exp, scale=-0.5), reads=[self.rssq], writes=[self.rssq])
        yt, ryt = self.W2.get()
        S.op("dve", lambda e: e.tensor_tensor(yt[:, 0:512], po[:], sz[:, 0:512], ALU.mult), reads=[rpo, rsz], writes=[ryt])
        yb, ryb = self.B1.get()
        S.op("dve", lambda e: e.tensor_tensor(q4(yb[:]), q4(yt[:, 0:512]), self.ssq[:, 4:8].unsqueeze(2).broadcast_to([128, 4, 128]), ALU.mult),
             reads=[ryt, self.rssq], writes=[ryb])
        ps, rps = self.PA.get()
        psb4 = ps[:].bitcast(BF16)
        for j in range(4):
            S.op("pe", lambda e, j=j, psb4=psb4: e.transpose(psb4[:, j * 128:(j + 1) * 128], yb[:, j * 128:(j + 1) * 128], self.cB(CB_ID)), reads=[ryb, rCb], writes=[rps])
        S.op("act", lambda e, psb4=psb4: e.activation(out[:, 512 * g:512 * (g + 1)], psb4[:, 0:512], AF.Copy), reads=[rps], writes=[rout])

    def wout_partial(self, l, chunks, outs):
        S = self.S
        ws = []
        for c in chunks:
            w, rw = self.Wrow.get()
            S.dma("pool", w[:], self.D["w_out"][l, c * 128:(c + 1) * 128, :], writes=[rw])
            ws.append((w, rw))
        n = len(chunks)
        for t in range(NT):
            for half in range(2):
                ps, rps = self.PA.get()
                for i in range(n):
                    S.op("pe", lambda e, i=i, ps=ps, t=t, half=half: e.matmul(ps[:], lhsT=outs[i][0][:, t * 128:(t + 1) * 128], rhs=ws[i][0][:, half * 512:(half + 1) * 512],
                                                                                   start=(i == 0), stop=(i == n - 1)),
                         reads=[outs[i][1], ws[i][1]], writes=[rps])
                xs = self.X[:, t, half * 512:(half + 1) * 512]
                S.op("dve", lambda e, xs=xs, ps=ps: e.tensor_tensor(xs, xs, ps[:], ALU.add), reads=[rps, self.rX[t]], writes=[self.rX[t]])

    def attn_pair(self, l, p, diff):
        S = self.S
        rCb, rC = self.rCb, self.rC
        base = 22 if diff else 16
        Wq = self.load_wt(l, base + p)
        Wk = self.load_wt(l, base + 2 + p)
        Wv = self.load_wt(l, base + 4 + p)
        qT, rqT = self.A4.get()
        kT, rkT = self.A4.get()
        out, rout = self.A4.get()
        vt, rvt = self.VT.get()
        S.op("pool", lambda e: e.memset(vt[:, :, 64:128], 0.0), writes=[rvt])
        S.op("pool", lambda e: e.memset(vt[:, :, 64:65], 1.0), reads=[rvt], writes=[rvt])
        for (W, dst, rdst) in ((Wq, qT, rqT), (Wk, kT, rkT)):
            for g in range(4):
                sl = slice(PADX + 512 * g, PADX + 512 * (g + 1))
                ps, rps = self.PA.get()
                for c in range(8):
                    S.op("pe", lambda e, c=c, W=W, ps=ps, sl=sl: e.matmul(ps[:], lhsT=W[0][:, c, :], rhs=self.XT[:, c, sl], start=(c == 0), stop=(c == 7)),
                         reads=[W[1]] + self.rXT[4 * g:4 * g + 4], writes=[rps])
                S.op("act", lambda e, ps=ps, dst=dst, g=g: e.activation(dst[:, 512 * g:512 * (g + 1)], ps[:], AF.Copy), reads=[rps], writes=[rdst])
        for t in range(NT):
            cs = slice(PADX + t * 128, PADX + (t + 1) * 128)
            ps, rps = self.PA.get()
            for c in range(8):
                S.op("pe", lambda e, c=c, ps=ps, cs=cs: e.matmul(ps[:, 0:128], lhsT=self.XT[:, c, cs], rhs=Wv[0][:, c, :], start=(c == 0), stop=(c == 7)),
                     reads=[Wv[1], self.rXT[t]], writes=[rps])
            S.op("dve", lambda e, ps=ps, t=t: e.tensor_copy(vt[:, t, 0:64], ps[:, 0:64]), reads=[rps, rvt], writes=[rvt])
            S.op("act", lambda e, ps=ps, t=t: e.activation(vt[:, t, 128:192], ps[:, 64:128], AF.Copy), reads=[rps, rvt], writes=[rvt])
        scale = (32.0 if diff else 64.0) ** -0.5
        for hh in range(2):
            rows = slice(0, 64) if hh == 0 else slice(64, 128)
            for g in range(4):
                nm = 4 * g + 4
                qs = slice(512 * g, 512 * (g + 1))
                nmaps = 2 if diff else 1
                accs = [self.PB.get() for _ in range(nmaps)]
                for m in range(nm):
                    ks = slice(128 * m, 128 * (m + 1))
                    for mm in range(nmaps):
                        acc, racc = accs[mm]
                        if diff:
                            kb = hh * 64 + mm * 32
                            kr = slice(kb, kb + 32)
                        else:
                            kb = hh * 64
                            kr = slice(kb, kb + 64)
                        stp, rstp = self.PA.get()
                        if kb == 96:
                            S.op("pe", lambda e, stp=stp, kr=kr, ks=ks: e.matmul(stp[:], lhsT=kT[kr, ks], rhs=qT[kr, qs], start=True, stop=True, tile_position=(96, 0)),
                                 reads=[rkT, rqT], writes=[rstp])
                        else:
                            S.op("pe", lambda e, stp=stp, kr=kr, ks=ks: e.matmul(stp[:], lhsT=kT[kr, ks], rhs=qT[kr, qs], start=True, stop=True),
                                 reads=[rkT, rqT], writes=[rstp])
                        P, rP = self.B1.get()
                        S.op("act", lambda e, P=P, stp=stp: e.activation(P[:], stp[:], AF.Exp, scale=scale), reads=[rstp], writes=[rP])
                        if not diff:
                            off = CB_WB + (min(4 * g - m, 5) + 3) * 128
                            S.op("dve", lambda e, P=P, off=off: e.tensor_tensor(P[:], P[:], self.cb[:, off:off + 512], ALU.mult), reads=[rP, rCb], writes=[rP])
                        elif m >= 4 * g:
                            off = CB_CB + (3 - (m - 4 * g)) * 128
                            S.op("dve", lambda e, P=P, off=off: e.tensor_tensor(P[:], P[:], self.cb[:, off:off + 512], ALU.mult), reads=[rP, rCb], writes=[rP])
                        if hh == 0:
                            S.op("pe", lambda e, acc=acc, P=P, m=m: e.matmul(acc[0:65, :], lhsT=vt[:, m, 0:65], rhs=P[:], start=(m == 0), stop=(m == nm - 1)),
                                 reads=[rvt, rP], writes=[racc])
                        else:
                            S.op("pe", lambda e, acc=acc, P=P, m=m: e.matmul(acc[:, :], lhsT=vt[:, m, 64:192], rhs=P[:], start=(m == 0), stop=(m == nm - 1)),
                                 reads=[rvt, rP], writes=[racc])
                rowp = 64 if hh == 0 else 0
                bcs = []
                for mm in range(nmaps):
                    acc, racc = accs[mm]
                    S.op("dve", lambda e, acc=acc: e.reciprocal(self.rrow[rowp:rowp + 1, :], acc[rowp:rowp + 1, :]), reads=[racc, self.rrrow], writes=[self.rrrow])
                    pbc, rpbc = self.PA.get()
                    if hh == 0:
                        S.op("pe", lambda e, pbc=pbc: e.matmul(pbc[0:64, :], lhsT=self.cf[64:96, CF_SEL:CF_SEL + 64], rhs=self.rrow[64:96, :], start=True, stop=True),
                             reads=[self.rrrow, rC], writes=[rpbc])
                    else:
                        S.op("pe", lambda e, pbc=pbc: e.matmul(pbc[:, :], lhsT=self.cf[0:32, CF_SEL:CF_SEL + 128], rhs=self.rrow[0:32, :], start=True, stop=True),
                             reads=[self.rrrow, rC], writes=[rpbc])
                    b_, rb_ = self.W2.get()
                    if mm == 0:
                        S.op("act", lambda e, b_=b_, pbc=pbc: e.activation(b_[rows, 0:512], pbc[rows, :], AF.Copy), reads=[rpbc], writes=[rb_])
                    else:
                        S.op("act", lambda e, b_=b_, pbc=pbc: e.activation(b_[rows, 0:512], pbc[rows, :], AF.Identity, scale=self.lam[rows, 5:6]),
                             reads=[rpbc, self.rlam], writes=[rb_])
                    bcs.append((b_, rb_))
                if not diff:
                    acc, racc = accs[0]
                    b_, rb_ = bcs[0]
                    S.op("dve", lambda e, acc=acc, b_=b_: e.tensor_tensor(out[rows, qs], acc[rows, :], b_[rows, 0:512], ALU.mult), reads=[racc, rb_], writes=[rout])
                else:
                    (b1, rb1), (b2, rb2) = bcs
                    S.op("dve", lambda e: e.tensor_tensor(b1[rows, 0:512], accs[0][0][rows, :], b1[rows, 0:512], ALU.mult), reads=[accs[0][1], rb1], writes=[rb1])
                    S.op("dve", lambda e: e.tensor_tensor(b2[rows, 0:512], accs[1][0][rows, :], b2[rows, 0:512], ALU.mult), reads=[accs[1][1], rb2], writes=[rb2])
                    S.op("pool", lambda e: e.tensor_tensor(b1[rows, 0:512], b1[rows, 0:512], b2[rows, 0:512], ALU.add), reads=[rb1, rb2], writes=[rb1])
                    osq, rosq = self.B1.get()
                    S.op("act", lambda e: e.activation(osq[rows, :], b1[rows, 0:512], AF.Square), reads=[rb1], writes=[rosq])
                    pss, rpss = self.PA.get()
                    if hh == 0:
                        S.op("pe", lambda e: e.matmul(pss[0:64, :], lhsT=self.cb[0:64, CB_ONES:CB_ONES + 64], rhs=osq[0:64, :], start=True, stop=True), reads=[rosq, rCb], writes=[rpss])
                    else:
                        S.op("pe", lambda e: e.matmul(pss[:, :], lhsT=self.cb[64:128, CB_ONES:CB_ONES + 128], rhs=osq[64:128, :], start=True, stop=True), reads=[rosq, rCb], writes=[rpss])
                    S.op("act", lambda e: e.activation(pss[rows, :], pss[rows, :], AF.Ln, scale=1.0 / 64.0, bias=self.eps5[rows, 0:1]), reads=[rpss, self.reps], writes=[rpss])
                    S.op("act", lambda e: e.activation(pss[rows, :], pss[rows, :], AF.Exp, scale=-0.5), reads=[rpss], writes=[rpss])
                    S.op("dve", lambda e: e.scalar_tensor_tensor(out[rows, qs], b1[rows, 0:512], self.lam[rows, 6:7], pss[rows, :], op0=ALU.mult, op1=ALU.mult),
                         reads=[rb1, rpss, self.rlam], writes=[rout])
        return out, rout

    def layernorm(self, l, which, last):
        S, D = self.S, self.D
        g_, rg = self.A4.get()
        b_, rb = self.A4.get()
        gv = g_[:].bitcast(F32)
        bv = b_[:].bitcast(F32)
        S.dma("sp", gv, D["ln%d_g" % which][l:l + 1, :].partition_broadcast(128), writes=[rg])
        S.dma("sp", bv, D["ln%d_b" % which][l:l + 1, :].partition_broadcast(128), writes=[rb])
        for t in range(NT):
            xt = self.X[:, t, :]
            rx = self.rX[t]
            st, rst = self.LNS.get()
            S.op("dve", lambda e, st=st, xt=xt: e.bn_stats(st[:, 0:6], xt[:, 0:512]), reads=[rx], writes=[rst])
            S.op("dve", lambda e, st=st, xt=xt: e.bn_stats(st[:, 6:12], xt[:, 512:1024]), reads=[rx, rst], writes=[rst])
            S.op("dve", lambda e, st=st: e.bn_aggr(st[:, 12:14], st[:, 0:12]), reads=[rst], writes=[rst])
            S.op("act", lambda e, st=st: e.activation(st[:, 14:15], st[:, 13:14], AF.Ln, bias=self.eps5[:, 0:1]), reads=[rst, self.reps], writes=[rst])
            S.op("act", lambda e, st=st: e.activation(st[:, 14:15], st[:, 14:15], AF.Exp, scale=-0.5), reads=[rst], writes=[rst])
            S.op("dve", lambda e, st=st: e.scalar_tensor_tensor(st[:, 15:16], st[:, 12:13], -1.0, st[:, 14:15], op0=ALU.mult, op1=ALU.mult), reads=[rst], writes=[rst])
            S.op("act", lambda e, st=st, xt=xt: e.activation(xt, xt, AF.Identity, scale=st[:, 14:15], bias=st[:, 15:16]), reads=[rst, rx], writes=[rx])
            S.op("dve", lambda e, xt=xt: e.tensor_tensor(xt, xt, gv, ALU.mult), reads=[rx, rg], writes=[rx])
            S.op("pool", lambda e, xt=xt: e.tensor_tensor(xt, xt, bv, ALU.add), reads=[rx, rb], writes=[rx])
            if last:
                S.dma("sp", D["y"][t * 128:(t + 1) * 128, :], xt, reads=[rx], writes=[S.out_region])
            else:
                self.x_tile_out(t, scale=True)

    def ffn(self, l):
        S, D = self.S, self.D
        rLP = self.rLP
        wins = []
        s = 0
        while s < T:
            n = min(510, T - s)
            wins.append((s, n))
            s += n
        for g0 in range(0, NFT, 4):
            grp = list(range(g0, min(g0 + 4, NFT)))
            hts = []
            for i in grp:
                cw = 128 if i < NFT - 1 else 64
                wu, rwu = self.Wup.get()
                S.dma("pool", wu[:], D["w_up_t"][l, i], writes=[rwu])
                ht, rht = self.A4.get()
                for (s, n) in wins:
                    w0 = PADX + s - 2
                    wd = n + 2
                    tiles = list(range(max(s - 2, 0) // 128, (s + n - 1) // 128 + 1))
                    rx = [self.rXT[t] for t in tiles] + ([self.rXTpad] if s == 0 else [])
                    pg, rpg = self.PA.get()
                    pv, rpv = self.PA.get()
                    for (pp, rpp, co) in ((pg, rpg, 0), (pv, rpv, 128)):
                        for c in range(8):
                            S.op("pe", lambda e, c=c, pp=pp, co=co, cw=cw, wu=wu, w0=w0, wd=wd: e.matmul(pp[0:cw, 0:wd], lhsT=wu[:, c, co:co + cw], rhs=self.XT[:, c, w0:w0 + wd],
                                                                                                       start=(c == 0), stop=(c == 7)),
                                 reads=[rwu] + rx, writes=[rpp])
                    ag, rag = self.W2.get()
                    av, rav = self.W2.get()
                    for (pp, rpp, aa, raa, gv) in ((pg, rpg, ag, rag, 0), (pv, rpv, av, rav, 1)):
                        S.op("act", lambda e, pp=pp, aa=aa, gv=gv, cw=cw, n=n, i=i: e.activation(aa[0:cw, 0:n], pp[0:cw, 2:2 + n], AF.Identity, scale=self.cwf[0:cw, i, gv, 2:3]),
                             reads=[rpp, rLP], writes=[raa])
                        for j in (1, 0):
                            S.op("dve", lambda e, pp=pp, aa=aa, gv=gv, cw=cw, n=n, i=i, j=j: e.scalar_tensor_tensor(aa[0:cw, 0:n], pp[0:cw, j:j + n], self.cwf[0:cw, i, gv, j:j + 1], aa[0:cw, 0:n],
                                                                                                                    op0=ALU.mult, op1=ALU.add),
                                 reads=[rpp, raa, rLP], writes=[raa])
                    S.op("act", lambda e, ag=ag, cw=cw, n=n: e.activation(ag[0:cw, 0:n], ag[0:cw, 0:n], AF.Silu), reads=[rag], writes=[rag])
                    S.op("dve", lambda e, ag=ag, av=av, ht=ht, cw=cw, n=n, s=s: e.tensor_tensor(ht[0:cw, s:s + n], ag[0:cw, 0:n], av[0:cw, 0:n], ALU.mult),
                         reads=[rag, rav], writes=[rht])
                hts.append((ht, rht, cw))
            wds = []
            for i in grp:
                w, rw = self.Wrow.get()
                S.dma("pool", w[:], D["w_down_t"][l, i], writes=[rw])
                wds.append((w, rw))
            nk = len(hts)
            for t in range(NT):
                for half in range(2):
                    ps, rps = self.PA.get()
                    for k in range(nk):
                        ht, rht, cw = hts[k]
                        S.op("pe", lambda e, k=k, ht=ht, cw=cw, ps=ps, t=t, half=half: e.matmul(ps[:], lhsT=ht[0:cw, t * 128:(t + 1) * 128], rhs=wds[k][0][0:cw, half * 512:(half + 1) * 512],
                                                                                              start=(k == 0), stop=(k == nk - 1)),
                             reads=[rht, wds[k][1]], writes=[rps])
                    xs = self.X[:, t, half * 512:(half + 1) * 512]
                    S.op("dve", lambda e, xs=xs, ps=ps: e.tensor_tensor(xs, xs, ps[:], ALU.add), reads=[rps, self.rX[t]], writes=[self.rX[t]])


def _host_layout(inputs):
    L = DEPTH
    f = lambda a: np.ascontiguousarray(np.asarray(a, dtype=np.float32))
    w_in = np.asarray(inputs["w_in"], dtype=np.float32)
    wi = np.concatenate([w_in[:, :, :2048], w_in[:, :, 2056:]], axis=2)
    w_in_t = f(wi.reshape(L, 8, 128, 28, 128).transpose(0, 3, 2, 1, 4))
    w_bd = f(w_in[:, :, 2048:2056].reshape(L, 8, 128, 8).transpose(0, 2, 1, 3))
    w_up = np.asarray(inputs["w_up"], dtype=np.float32)
    pad = NFT * 128 - DFF
    gate = np.pad(w_up[:, :, :DFF], ((0, 0), (0, 0), (0, pad))).reshape(L, 8, 128, NFT, 128)
    val = np.pad(w_up[:, :, DFF:], ((0, 0), (0, 0), (0, pad))).reshape(L, 8, 128, NFT, 128)
    w_up_t = f(np.stack([gate, val], axis=4).transpose(0, 3, 2, 1, 4, 5).reshape(L, NFT, 128, 8, 256))
    w_down = np.asarray(inputs["w_down"], dtype=np.float32)
    w_down_t = f(np.pad(w_down, ((0, 0), (0, pad), (0, 0))).reshape(L, NFT, 128, DM))
    gconv = f(np.asarray(inputs["gdn_conv"], np.float32).reshape(L, 4, 12, 128).transpose(0, 3, 2, 1))
    fc = np.asarray(inputs["ffn_conv"], np.float32)
    fg = np.pad(fc[:, :, :DFF], ((0, 0), (0, 0), (0, pad))).reshape(L, 3, NFT, 128)
    fv = np.pad(fc[:, :, DFF:], ((0, 0), (0, 0), (0, pad))).reshape(L, 3, NFT, 128)
    fconv = f(np.stack([fg, fv], axis=2).transpose(0, 4, 3, 2, 1))
    cf, cbf = host_consts()
    dn = np.asarray(inputs["diff_norm"], np.float32)
    shared = {
        "w_in_t": w_in_t, "w_bd": w_bd, "w_out": f(inputs["w_out"]), "w_up_t": w_up_t, "w_down_t": w_down_t,
        "gconv": gconv, "fconv": fconv,
        "ln1_g": f(inputs["ln1_g"]), "ln1_b": f(inputs["ln1_b"]), "ln2_g": f(inputs["ln2_g"]), "ln2_b": f(inputs["ln2_b"]),
        "a_log": f(inputs["gdn_a_log"]), "dt_bias": f(inputs["gdn_dt_bias"]), "gnorm": f(inputs["gdn_norm"]),
        "dlam": f(np.asarray(inputs["diff_lambda"], np.float32).reshape(L, 128)),
        "dnorm": f(np.concatenate([dn, dn], axis=1).reshape(L, 128, 1)),
        "cf": f(cf), "cbf": f(cbf),
    }
    return shared


_PROG_CACHE = {}


def run_cores(inputs, nlayers=DEPTH, dbg=None, ncores=8):
    key = (nlayers, dbg)
    if key not in _PROG_CACHE:
        _PROG_CACHE[key] = Prog(nlayers, dbg)
    prog = _PROG_CACHE[key]
    shared = _host_layout(inputs)
    x = np.asarray(inputs["x"], dtype=np.float32)
    in_maps = []
    for b in range(ncores):
        m = dict(shared)
        m["x"] = np.ascontiguousarray(x[b])
        in_maps.append(m)
    res = run_bass_kernel_spmd(prog.nc, in_maps, core_ids=list(range(ncores)))
    return np.stack([np.asarray(r["y"], dtype=np.float32) for r in res.results], axis=0)


def kernel(**inputs):
    return run_cores(inputs)
```

```python
import math
import numpy as np
from contextlib import ExitStack
import concourse.bass as bass
import concourse.mybir as mybir
from concourse.bass_utils import run_bass_kernel_spmd

F32 = mybir.dt.float32
BF16 = mybir.dt.bfloat16
U16 = mybir.dt.uint16
AF = mybir.ActivationFunctionType
ALU = mybir.AluOpType
AX = mybir.AxisListType


class Region:
    __slots__ = ("name", "w", "rs", "dsem", "dcnt")

    def __init__(self, name, dsem=None):
        self.name = name
        self.w = None
        self.rs = {}
        self.dsem = dsem
        self.dcnt = 0


class _Rec:
    def __init__(self):
        self.calls = []

    def __getattr__(self, name):
        def f(*a, **k):
            self.calls.append((name, a, k))
            return self
        return f


class _Eng:
    def __init__(self, name, eng, sem):
        self.name = name
        self.eng = eng
        self.sem = sem
        self.count = 0
        self.seen = {}
        self.ops = []
        self.waited = set()
        self.rank = None


class Sched:
    def __init__(self, nc, st):
        self.nc = nc
        self.st = st
        self.E = {}
        for name, eng in (("pe", nc.tensor), ("act", nc.scalar), ("dve", nc.vector),
                          ("pool", nc.gpsimd), ("sp", nc.sync)):
            sem = st.enter_context(nc.semaphore("s_" + name))
            self.E[name] = _Eng(name, eng, sem)
        self.n_ops = 0
        self._uid = 0
        self.out_region = self.region("out", dma=True)

    def sb(self, name, shape, dtype):
        return self.st.enter_context(self.nc.sbuf_tensor("sb_" + name, shape, dtype))

    def ps(self, name, shape, dtype):
        return self.st.enter_context(self.nc.psum_tensor("ps_" + name, shape, dtype))

    def region(self, name, dma=False):
        dsem = None
        if dma:
            self._uid += 1
            dsem = self.st.enter_context(self.nc.semaphore("d%d_%s" % (self._uid, name)))
        return Region(name, dsem)

    def _need(self, me, tok, waits, raw):
        if tok is None:
            return
        kind, key, sem, cnt = tok
        if kind == "e" and key == me.name:
            if not raw or me.name in ("pe", "sp"):
                return
        if me.seen.get(key, 0) >= cnt:
            return
        me.seen[key] = cnt
        waits[key] = (kind, key, sem, cnt)
        if kind == "e":
            self.E[key].waited.add(cnt)

    def _collect(self, me, reads, writes):
        waits = {}
        for r in reads:
            self._need(me, r.w, waits, True)
        for r in writes:
            self._need(me, r.w, waits, False)
            for t in r.rs.values():
                self._need(me, t, waits, False)
        return list(waits.values())

    def op(self, engname, fn, reads=(), writes=()):
        me = self.E[engname]
        waits = self._collect(me, reads, writes)
        me.count += 1
        tok = ("e", me.name, me.sem, me.count)
        for r in reads:
            r.rs[me.name] = tok
        for r in writes:
            r.w = tok
            r.rs = {}
        sem = me.sem
        rec = _Rec()
        fn(rec)
        assert len(rec.calls) == 1
        name, a, k = rec.calls[0]

        idx = me.count

        def emit(e, waits=waits, name=name, a=a, k=k, sem=sem, idx=idx, me=me):
            self._emit_waits(e, waits)
            ins = getattr(e, name)(*a, **k)
            if idx in me.rank:
                ins.then_inc(sem, 1)
        me.ops.append(emit)
        self.n_ops += 1

    def dma(self, queue, out_ap, in_ap, reads=(), writes=()):
        me = self.E[queue]
        waits = self._collect(me, reads, writes)
        tgt = None
        for r in writes:
            if r.dsem is not None:
                tgt = r
        assert tgt is not None, "dma needs a dma-region target"
        tgt.dcnt += 16
        tok = ("d", "d:" + tgt.name + str(id(tgt)), tgt.dsem, tgt.dcnt)
        for r in reads:
            r.rs[tok[1]] = tok
        for r in writes:
            r.w = tok
            r.rs = {}
        sem = tgt.dsem

        def emit(e, waits=waits, sem=sem, out_ap=out_ap, in_ap=in_ap):
            self._emit_waits(e, waits)
            e.dma_start(out=out_ap, in_=in_ap).then_inc(sem, 16)
        me.ops.append(emit)
        self.n_ops += 1

    def _emit_waits(self, e, waits):
        for kind, key, sem, cnt in waits:
            if kind == "e":
                e.wait_ge(sem, self.E[key].rank[cnt])
            else:
                e.wait_ge(sem, cnt)

    def finish(self):
        for eng in self.E.values():
            eng.rank = {c: i + 1 for i, c in enumerate(sorted(eng.waited))}
        outr = self.out_region
        sp = self.E["sp"]
        total = outr.dcnt

        def fin(e, s=outr.dsem, v=total):
            e.wait_ge(s, v)
        sp.ops.append(fin)
        with self.nc.Block() as block:
            @block.tensor
            def _(e):
                for f in self.E["pe"].ops:
                    f(e)

            @block.scalar
            def _(e):
                for f in self.E["act"].ops:
                    f(e)

            @block.vector
            def _(e):
                for f in self.E["dve"].ops:
                    f(e)

            @block.gpsimd
            def _(e):
                for f in self.E["pool"].ops:
                    f(e)

            @block.sync
            def _(e):
                for f in self.E["sp"].ops:
                    f(e)


T = 2048
DM = 1024
NT = 16
PADX = 8
DEPTH = 4
DFF = 2752
NFT = 22
ALPHA = (2 * DEPTH) ** 0.25
EPS = 1e-5
NLEV = 7


class Flat3:
    def __init__(self, t, n, w):
        self.t, self.n, self.w = t, n, w

    def __getitem__(self, key):
        p, c, sl = key
        a = 0 if sl.start is None else sl.start
        b = self.w if sl.stop is None else sl.stop
        return self.t[p, c * self.w + a:c * self.w + b]


class Ring:
    def __init__(self, S, name, n, shape, dtype, psum=False, dma=False):
        self.b = []
        for i in range(n):
            nm = "%s%d" % (name, i)
            t = (S.ps if psum else S.sb)(nm, shape, dtype)
            self.b.append((t, S.region(nm, dma=dma)))
        self.i = 0

    def get(self):
        x = self.b[self.i % len(self.b)]
        self.i += 1
        return x


def q4(ap):
    return ap.rearrange("p (j f) -> p j f", j=4)


def host_consts():
    f = {}
    I = np.eye(128, dtype=np.float32)
    j = np.arange(128)[:, None]
    i = np.arange(128)[None, :]
    sel = np.zeros((128, 128), np.float32)
    sel[0, :] = 1.0
    sel[64, :] = 1.0
    cf = np.concatenate([I, -I, (j <= i).astype(np.float32), sel, np.ones((128, 640), np.float32)], axis=1)
    negl = np.where(i < j, -30000.0, 0.0).astype(np.float32)
    om, omt = [], []
    for L in range(NLEV):
        b = 1 << L
        m = ((j // (2 * b) == i // (2 * b)) & ((j // b) % 2 == 0) & ((i // b) % 2 == 1)).astype(np.float32)
        om.append(m)
        omt.append(m.T.copy())
    ik = np.arange(128)[:, None]
    wb = []
    for dl in range(-3, 9):
        d = 128 * dl + np.arange(128)[None, :] - ik
        w = ((d >= 0) & (d <= 128)).astype(np.float32) + ((d >= 0) & (d <= 512) & (d % 4 == 0)) + ((d >= 0) & (d % 16 == 0))
        wb.append(w.astype(np.float32))
    cb = []
    for dl in range(-3, 4):
        d = 128 * dl + np.arange(128)[None, :] - ik
        cb.append((d >= 0).astype(np.float32))
    cbf = np.concatenate([I, negl] + om + omt + wb + cb + [np.ones((128, 128), np.float32)], axis=1)
    return cf.astype(np.float32), cbf.astype(np.float32)


CF_ID, CF_NID, CF_UTRI, CF_SEL, CF_ONES = 0, 128, 256, 384, 512
CB_ID, CB_NEGL, CB_OM, CB_OMT, CB_WB, CB_CB, CB_ONES = 0, 128, 256, 256 + 896, 256 + 1792, 256 + 1792 + 1536, 256 + 1792 + 1536 + 896
NCF = 512 + 640
NCB = CB_ONES + 128


class Prog:
    def __init__(self, nlayers=DEPTH, dbg=None):
        self.nlayers = nlayers
        self.dbg = dbg
        nc = self.nc = bass.Bass("TRN2", target_bir_lowering=False)
        L = DEPTH
        D = {}

        def inp(name, shape, dt=F32):
            D[name] = nc.dram_tensor(name, shape, dt, kind="ExternalInput").ap()
        inp("x", [T, DM])
        inp("w_in_t", [L, 28, 128, 8, 128])
        inp("w_bd", [L, 128, 8, 8])
        inp("w_out", [L, DM, DM])
        inp("w_up_t", [L, NFT, 128, 8, 256])
        inp("w_down_t", [L, NFT, 128, DM])
        inp("gconv", [L, 128, 12, 4])
        inp("fconv", [L, 128, NFT, 2, 3])
        inp("ln1_g", [L, DM]); inp("ln1_b", [L, DM]); inp("ln2_g", [L, DM]); inp("ln2_b", [L, DM])
        inp("a_log", [L, 4]); inp("dt_bias", [L, 4]); inp("gnorm", [L, 128])
        inp("dlam", [L, 128]); inp("dnorm", [L, 128, 1])
        inp("cf", [128, NCF]); inp("cbf", [128, NCB])
        D["y"] = nc.dram_tensor("y", [T, DM], F32, kind="ExternalOutput").ap()
        self.D = D
        with ExitStack() as st:
            self.S = S = Sched(nc, st)
            self.alloc()
            self.prologue()
            if dbg == "pro":
                w, rw = self.W2.get()
                S.op("dve", lambda e: e.tensor_copy(w[:, 0:512], self.XT[:, 0, PADX:PADX + 512]), reads=self.rXT[0:4], writes=[rw])
                self.dump(0, w[:, 0:512], rw)
                self.dump(1, self.XT[:, 0, PADX:PADX + 512], self.rXT[0], bf=True)
                w2, rw2 = self.W2.get()
                S.op("dve", lambda e: e.tensor_copy(w2[:, 0:512], self.cb[:, 0:512]), reads=[self.rCb], writes=[rw2])
                self.dump(2, w2[:, 0:512], rw2)
                self.dump(3, self.X[:, 0, 0:512], self.rX[0])
                xb, rxb = self.XB.b[0]
                w3, rw3 = self.W2.get()
                S.op("dve", lambda e: e.tensor_copy(w3[:, 0:512], xb[:, 0:512]), reads=[rxb], writes=[rw3])
                self.dump(4, w3[:, 0:512], rw3)
                ps, rps = self.PA.get()
                psb = ps[:].bitcast(BF16)
                S.op("pe", lambda e: e.transpose(psb[:, 0:128], xb[:, 0:128], self.cB(CB_ID)), reads=[rxb, self.rCb], writes=[rps])
                w4, rw4 = self.W2.get()
                S.op("dve", lambda e: e.tensor_copy(w4[:, 0:128], psb[:, 0:128]), reads=[rps], writes=[rw4])
                S.op("act", lambda e: e.activation(w4[:, 128:256], psb[:, 0:128], AF.Copy), reads=[rps, rw4], writes=[rw4])
                S.op("dve", lambda e: e.tensor_copy(w4[:, 256:384], self.XT[:, 0, PADX + 1920:PADX + 2048]), reads=[self.rXT[15], rw4], writes=[rw4])
                self.dump(5, w4[:, 0:512], rw4)
                ps2, rps2 = self.PA.get()
                psb2 = ps2[:].bitcast(BF16)
                for c4 in range(4):
                    S.op("pe", lambda e, c4=c4: e.transpose(psb2[:, c4 * 128:(c4 + 1) * 128], xb[:, c4 * 128:(c4 + 1) * 128], self.cB(CB_ID)), reads=[rxb, self.rCb], writes=[rps2])
                b1, rb1 = self.B1.get()
                b2, rb2 = self.B1.get()
                b3, rb3 = self.B1.get()
                S.op("dve", lambda e: e.tensor_copy(b1[:], psb2[:, 0:512]), reads=[rps2], writes=[rb1])
                S.op("dve", lambda e: e.tensor_scalar(b2[:], psb2[:, 0:512], 1.0, None, op0=ALU.mult), reads=[rps2], writes=[rb2])
                S.op("act", lambda e: e.activation(b3[:], psb2[:, 0:512], AF.Identity), reads=[rps2], writes=[rb3])
                for k, (bb, rbb) in enumerate(((b1, rb1), (b2, rb2), (b3, rb3))):
                    wk, rwk = self.W2.get()
                    S.op("dve", lambda e, wk=wk, bb=bb: e.tensor_copy(wk[:, 0:512], bb[:]), reads=[rbb], writes=[rwk])
                    self.dump(6 + k, wk[:, 0:512], rwk)
            else:
                for l in range(nlayers):
                    self.layer(l)
            S.finish()

    def alloc(self):
        S = self.S
        self.X = S.sb("X", [128, NT, DM], F32)
        self.rX = [S.region("X%d" % t) for t in range(NT)]
        self.XT = Flat3(S.sb("XT", [128, 8 * (PADX + T)], BF16), 8, PADX + T)
        self.rXT = [S.region("XT%d" % t) for t in range(NT)]
        self.rXTpad = S.region("XTpad")
        self.PA = Ring(S, "pa", 6, [128, 512], F32, psum=True)
        self.PB = Ring(S, "pb", 2, [128, 512], F32, psum=True)
        self.W2 = Ring(S, "w2", 4, [128, 516], F32)
        self.B1 = Ring(S, "b1", 5, [128, 512], BF16)
        self.A4 = Ring(S, "a4", 5, [128, T], BF16, dma=True)
        self.VT = Ring(S, "vt", 1, [128, NT, 192], BF16)
        self.Wt = Ring(S, "wt", 4, [128, 8, 128], BF16, dma=True)
        self.Wup = Ring(S, "wup", 2, [128, 8, 256], BF16, dma=True)
        self.Wrow = Ring(S, "wrow", 4, [128, DM], BF16, dma=True)
        self.TN = Ring(S, "tn", 2, [128, 512], BF16)
        self.XB = Ring(S, "xb", 1, [128, DM], BF16)
        self.LNS = Ring(S, "lns", 3, [128, 16], F32)
        self.gq = {}
        for n in "qT kT vT vtok DU EGr Np Mp X4 Y4 kg kt wTn qkT qdec vn".split():
            self.gq[n] = (S.sb("gq_" + n, [128, 512], BF16), S.region("gq_" + n))
        self.gacc = {n: (S.sb("gacc_" + n, [128, 512], F32), S.region("gacc_" + n)) for n in "qkv"}
        self.HALO = S.sb("halo", [128, 3, 4], F32)
        self.rHALO = [S.region("halo%d" % i) for i in range(3)]
        self.SM = S.sb("SM", [128, 8, NT, 4], F32)
        self.rSM = S.region("SM")
        self.aexp = S.sb("aexp", [128, 4], F32)
        self.raexp = S.region("aexp")
        self.Sf = S.sb("Sf", [128, 128], F32)
        self.rSf = S.region("Sf")
        self.Sb = Ring(S, "Sb", 2, [128, 128], BF16)
        self.ssq = S.sb("ssq", [128, 8], F32)
        self.rssq = S.region("ssq")
        self.cwg = S.sb("cwg", [128, 12, 4], F32)
        self.cwf = S.sb("cwf", [128, NFT, 2, 3], F32)
        self.alog = S.sb("alog", [128, 4], F32)
        self.dtb = S.sb("dtb", [128, 4], F32)
        self.gnw = S.sb("gnw", [128, 128], F32)
        self.lv = S.sb("lv", [128, 128], F32)
        self.dnw = S.sb("dnw", [128, 1], F32)
        self.rLP = S.region("LP", dma=True)
        self.wbd = S.sb("wbd", [128, 8, 8], BF16)
        self.rwbd = S.region("wbd", dma=True)
        self.lam = S.sb("lam", [128, 8], F32)
        self.rlam = S.region("lam")
        self.rrow = S.sb("rrow", [128, 512], F32)
        self.rrrow = S.region("rrow")
        self.cf = S.sb("cf", [128, NCF], F32)
        self.cb = S.sb("cb", [128, NCB], BF16)
        self.rC = S.region("C", dma=True)
        self.rCb = S.region("Cb", dma=True)
        self.eps5 = S.sb("eps5", [128, 2], F32)
        self.reps = S.region("eps")
        self.rXload = [S.region("xl%d" % i, dma=True) for i in range(4)]

    def cF(self, off, n=128):
        return self.cf[:, off:off + n]

    def cB(self, off, n=128):
        return self.cb[:, off:off + n]

    def prologue(self):
        S, D = self.S, self.D
        S.dma("sp", self.cf[:], D["cf"], writes=[self.rC])
        S.dma("pool", self.cb[:], D["cbf"], writes=[self.rCb])
        for i in range(4):
            for t in range(4 * i, 4 * i + 4):
                S.dma("sp", self.X[:, t, :], D["x"][t * 128:(t + 1) * 128, :], writes=[self.rXload[i]])
        for c in range(8):
            S.op("pool", lambda e, c=c: e.memset(self.XT[:, c, 0:PADX], 0.0), writes=[self.rXTpad])
        S.op("pool", lambda e: e.memset(self.eps5[:, 0:1], EPS), writes=[self.reps])
        S.op("pool", lambda e: e.memset(self.eps5[:, 1:2], 1e-6), reads=[self.reps], writes=[self.reps])
        S.op("pool", lambda e: e.memset(self.rrow[:], 0.0), writes=[self.rrrow])
        for t in range(NT):
            self.x_tile_out(t, extra_reads=[self.rXload[t // 4]], scale=True)

    def x_tile_out(self, t, extra_reads=(), scale=True):
        S = self.S
        xb, rxb = self.XB.get()
        xbv = xb[:, 0:DM]
        S.op("act", lambda e: e.activation(xbv, self.X[:, t, :], AF.Copy), reads=[self.rX[t]] + list(extra_reads), writes=[rxb])
        for half in range(2):
            ps, rps = self.PA.get()
            psb = ps[:].bitcast(BF16)
            for c4 in range(4):
                c = half * 4 + c4
                S.op("pe", lambda e, c=c, c4=c4: e.transpose(psb[:, c4 * 128:(c4 + 1) * 128], xb[:, c * 128:(c + 1) * 128], self.cB(CB_ID)),
                     reads=[rxb, self.rCb], writes=[rps])
            for c4 in range(4):
                c = half * 4 + c4
                dst = self.XT[:, c, PADX + t * 128:PADX + (t + 1) * 128]
                src = psb[:, c4 * 128:(c4 + 1) * 128]
                if c4 % 2 == 0:
                    S.op("dve", lambda e, dst=dst, src=src: e.tensor_copy(dst, src), reads=[rps, self.rXT[t]], writes=[self.rXT[t]])
                else:
                    S.op("act", lambda e, dst=dst, src=src: e.activation(dst, src, AF.Copy), reads=[rps, self.rXT[t]], writes=[self.rXT[t]])
        if scale:
            S.op("act", lambda e: e.activation(self.X[:, t, :], self.X[:, t, :], AF.Identity, scale=ALPHA),
                 reads=[self.rX[t]] + list(extra_reads), writes=[self.rX[t]])

    def load_wt(self, l, ti):
        w, rw = self.Wt.get()
        self.S.dma("pool", w[:], self.D["w_in_t"][l, ti], writes=[rw])
        return w, rw

    def layer(self, l):
        S, D = self.S, self.D
        lam_init = 0.8 - 0.6 * math.exp(-0.3 * l)
        rLP = self.rLP
        S.dma("sp", self.cwg[:], D["gconv"][l], writes=[rLP])
        S.dma("sp", self.cwf[:], D["fconv"][l], writes=[rLP])
        S.dma("sp", self.alog[:], D["a_log"][l:l + 1, :].partition_broadcast(128), writes=[rLP])
        S.dma("sp", self.dtb[:], D["dt_bias"][l:l + 1, :].partition_broadcast(128), writes=[rLP])
        S.dma("sp", self.gnw[:], D["gnorm"][l:l + 1, :].partition_broadcast(128), writes=[rLP])
        S.dma("sp", self.lv[:], D["dlam"][l:l + 1, :].partition_broadcast(128), writes=[rLP])
        S.dma("sp", self.dnw[:], D["dnorm"][l], writes=[rLP])
        S.dma("pool", self.wbd[:], D["w_bd"][l], writes=[self.rwbd])
        self.lam_calc(lam_init)
        self.gdn_prep(l)
        outs = []
        for h in range(4):
            outs.append(self.gdn_head(l, h))
            if self.dbg == "g0":
                return
            if h % 2 == 1:
                self.wout_partial(l, [h - 1, h], outs[-2:])
        if self.dbg == "gdn":
            return self.dump_X()
        o = [self.attn_pair(l, p, diff=False) for p in range(2)]
        self.wout_partial(l, [4, 5], o)
        if self.dbg == "dsw":
            return self.dump_X()
        o = [self.attn_pair(l, p, diff=True) for p in range(2)]
        self.wout_partial(l, [6, 7], o)
        if self.dbg == "mix":
            return self.dump_X()
        self.layernorm(l, 1, last=False)
        if self.dbg == "ln1":
            return self.dump_X()
        self.ffn(l)
        if self.dbg == "ffn":
            return self.dump_X()
        self.layernorm(l, 2, last=(l == self.nlayers - 1))

    def dump(self, idx, ap, reg, bf=False):
        n = 128 if ap.shape[-1] == 128 else 512
        dst = self.D["y"][(idx % 16) * 128:(idx % 16 + 1) * 128, 512 * (idx // 16):512 * (idx // 16) + n]
        self.S.dma("pool" if bf else "sp", dst, ap, reads=[reg], writes=[self.S.out_region])

    def dump_X(self):
        for t in range(NT):
            self.S.dma("sp", self.D["y"][t * 128:(t + 1) * 128, :], self.X[:, t, :], reads=[self.rX[t]], writes=[self.S.out_region])

    def lam_calc(self, lam_init):
        S = self.S
        lam, r = self.lam, self.rlam
        w, rw = self.W2.get()
        S.op("dve", lambda e: e.tensor_tensor(w[:, 0:32], self.lv[:, 0:32], self.lv[:, 32:64], ALU.mult), reads=[self.rLP], writes=[rw])
        S.op("dve", lambda e: e.tensor_tensor(w[:, 32:64], self.lv[:, 64:96], self.lv[:, 96:128], ALU.mult), reads=[self.rLP, rw], writes=[rw])
        S.op("dve", lambda e: e.tensor_reduce(lam[:, 0:2], w[:, 0:64].rearrange("p (a b) -> p a b", a=2), axis=AX.X, op=ALU.add), reads=[rw, r], writes=[r])
        S.op("act", lambda e: e.activation(lam[:, 2:4], lam[:, 0:2], AF.Exp), reads=[r], writes=[r])
        S.op("dve", lambda e: e.tensor_tensor(lam[:, 4:5], lam[:, 2:3], lam[:, 3:4], ALU.subtract), reads=[r], writes=[r])
        S.op("dve", lambda e: e.tensor_scalar(lam[:, 5:6], lam[:, 4:5], lam_init, -1.0, op0=ALU.add, op1=ALU.mult), reads=[r], writes=[r])
        S.op("dve", lambda e: e.tensor_scalar(lam[:, 6:7], self.dnw[:, 0:1], 1.0 - lam_init, None, op0=ALU.mult), reads=[r, self.rLP], writes=[r])
        S.op("act", lambda e: e.activation(self.aexp[:], self.alog[:], AF.Exp), reads=[self.rLP], writes=[self.raexp])

    def gdn_prep(self, l):
        S, SM, r = self.S, self.SM, self.rSM
        ps, rps = self.PA.get()
        for t in range(NT):
            cs = slice(PADX + t * 128, PADX + (t + 1) * 128)
            for c in range(8):
                S.op("pe", lambda e, c=c, cs=cs, t=t: e.matmul(ps[:, t * 8:(t + 1) * 8], lhsT=self.XT[:, c, cs], rhs=self.wbd[:, c, :],
                                                                 start=(c == 0), stop=(c == 7)),
                     reads=[self.rXT[t], self.rwbd], writes=[rps])
        psv = ps[:, 0:128].rearrange("p (t c) -> p t c", c=8)
        G, BE, GC, GL, EG, EGL, EKT, TM = [SM[:, k, :, :] for k in range(8)]
        dtb = self.dtb[:].unsqueeze(1).broadcast_to([128, NT, 4])
        aex = self.aexp[:].unsqueeze(1).broadcast_to([128, NT, 4])
        S.op("act", lambda e: e.activation(BE, psv[:, :, 0:4], AF.Exp, scale=-1.0), reads=[rps], writes=[r])
        S.op("dve", lambda e: e.tensor_scalar(BE, BE, 1.0, None, op0=ALU.add), reads=[r], writes=[r])
        S.op("dve", lambda e: e.reciprocal(BE, BE), reads=[r], writes=[r])
        S.op("dve", lambda e: e.tensor_tensor(TM, psv[:, :, 4:8], dtb, ALU.add), reads=[rps, self.rLP, r], writes=[r])
        S.op("act", lambda e: e.activation(TM, TM, AF.Exp), reads=[r], writes=[r])
        S.op("dve", lambda e: e.tensor_scalar(GC, TM, 2.0, None, op0=ALU.add), reads=[r], writes=[r])
        S.op("dve", lambda e: e.reciprocal(GC, GC), reads=[r], writes=[r])
        S.op("dve", lambda e: e.tensor_tensor(GC, GC, TM, ALU.mult), reads=[r], writes=[r])
        S.op("dve", lambda e: e.tensor_tensor(GL, GC, GC, ALU.mult), reads=[r], writes=[r])
        S.op("dve", lambda e: e.tensor_scalar(TM, GL, 1.0 / 11.0, None, op0=ALU.mult), reads=[r], writes=[r])
        for cst in (1.0 / 9.0, 1.0 / 7.0, 1.0 / 5.0, 1.0 / 3.0):
            S.op("dve", lambda e, cst=cst: e.scalar_tensor_tensor(TM, TM, cst, GL, op0=ALU.add, op1=ALU.mult), reads=[r], writes=[r])
        S.op("dve", lambda e: e.scalar_tensor_tensor(TM, TM, 1.0, GC, op0=ALU.add, op1=ALU.mult), reads=[r], writes=[r])
        S.op("dve", lambda e: e.scalar_tensor_tensor(G, TM, -2.0, aex, op0=ALU.mult, op1=ALU.mult), reads=[r, self.raexp], writes=[r])
        ps2, rps2 = self.PA.get()
        for t in range(NT):
            S.op("pe", lambda e, t=t: e.matmul(ps2[:, t * 4:(t + 1) * 4], lhsT=self.cF(CF_UTRI), rhs=SM[:, 0, t, :], start=True, stop=True),
                 reads=[r, self.rC], writes=[rps2])
            S.op("pe", lambda e, t=t: e.matmul(ps2[:, 64 + t * 4:64 + (t + 1) * 4], lhsT=self.cF(CF_ONES), rhs=SM[:, 0, t, :], start=True, stop=True),
                 reads=[r, self.rC], writes=[rps2])
        S.op("act", lambda e: e.activation(SM[:, 2:4, :, :].rearrange("p a t c -> p (a t c)"), ps2[:, 0:128], AF.Copy), reads=[rps2], writes=[r])
        S.op("act", lambda e: e.activation(SM[:, 4:6, :, :].rearrange("p a t c -> p (a t c)"),
                                           SM[:, 2:4, :, :].rearrange("p a t c -> p (a t c)"), AF.Exp), reads=[r], writes=[r])
        S.op("dve", lambda e: e.tensor_tensor(TM, GL, GC, ALU.subtract), reads=[r], writes=[r])
        S.op("act", lambda e: e.activation(EKT, TM, AF.Exp), reads=[r], writes=[r])

    def gdn_head(self, l, h):
        S = self.S
        Wq = self.load_wt(l, h)
        Wk = self.load_wt(l, 4 + h)
        Wv = self.load_wt(l, 8 + h)
        Wz = self.load_wt(l, 12 + h)
        out, rout = self.A4.get()
        S.op("pool", lambda e: e.memset(self.Sf[:], 0.0), writes=[self.rSf])
        sb0 = self.Sb.get()
        S.op("pool", lambda e: e.memset(sb0[0][:], 0.0), writes=[sb0[1]])
        self.curSb = sb0
        for g in range(4):
            self.gdn_group(l, h, g, Wq, Wk, Wv, Wz, out, rout)
            if self.dbg == "g0":
                gq = self.gq
                self.dump(0, self.SM[:].rearrange("p a t c -> p (a t c)"), self.rSM)
                for i, n in enumerate("qT kT vT vtok DU EGr Np Mp X4 Y4 kg kt wTn qkT qdec vn".split()):
                    self.dump(1 + i, gq[n][0][:], gq[n][1], bf=True)
                self.dump(17, out[:, 0:512], rout, bf=True)
                self.dump(18, self.Sf[:], self.rSf)
                self.dump(19, self.gacc["q"][0][:], self.gacc["q"][1])
                self.dump(20, self.X[:, 0, 0:512], self.rX[0])
                self.dump(21, self.XT[:, 0, PADX:PADX + 512], self.rXT[0], bf=True)
                return None
        return out, rout

    def gdn_group(self, l, h, g, Wq, Wk, Wv, Wz, out, rout):
        S, SM, gq = self.S, self.SM, self.gq
        rSM, rC, rCb, rLP = self.rSM, self.rC, self.rCb, self.rLP
        sl = slice(PADX + 512 * g, PADX + 512 * (g + 1))
        rxt = self.rXT[4 * g:4 * g + 4]
        t0 = 4 * g
        for wi, (nm, W, ci) in enumerate((("q", Wq, h), ("k", Wk, 4 + h), ("v", Wv, 8 + h))):
            ps, rps = self.PA.get()
            for c in range(8):
                S.op("pe", lambda e, c=c, W=W, ps=ps: e.matmul(ps[:], lhsT=W[0][:, c, :], rhs=self.XT[:, c, sl], start=(c == 0), stop=(c == 7)),
                     reads=[W[1]] + rxt, writes=[rps])
            st, rst = self.W2.get()
            S.op("act", lambda e, st=st, ps=ps: e.activation(st[:, 3:515], ps[:], AF.Copy), reads=[rps], writes=[rst])
            if g == 0:
                S.op("pool", lambda e, st=st: e.memset(st[:, 0:3], 0.0), reads=[rst], writes=[rst])
            else:
                S.op("dve", lambda e, st=st, wi=wi: e.tensor_copy(st[:, 0:3], self.HALO[:, wi, 0:3]), reads=[self.rHALO[wi], rst], writes=[rst])
            S.op("act", lambda e, ps=ps, wi=wi: e.activation(self.HALO[:, wi, 0:3], ps[:, 509:512], AF.Copy), reads=[rps], writes=[self.rHALO[wi]])
            acc, racc = self.gacc[nm]
            S.op("dve", lambda e, acc=acc, st=st, ci=ci: e.tensor_scalar(acc[:], st[:, 3:515], self.cwg[:, ci, 3:4], None, op0=ALU.mult),
                 reads=[rst, rLP], writes=[racc])
            for j in (2, 1, 0):
                S.op("dve", lambda e, acc=acc, st=st, ci=ci, j=j: e.scalar_tensor_tensor(acc[:], st[:, j:j + 512], self.cwg[:, ci, j:j + 1], acc[:],
                                                                                         op0=ALU.mult, op1=ALU.add),
                     reads=[rst, racc, rLP], writes=[racc])
        aq, raq = self.gacc["q"]
        ak, rak = self.gacc["k"]
        av, rav = self.gacc["v"]
        S.op("act", lambda e: e.activation(aq[:], aq[:], AF.Silu), reads=[raq], writes=[raq])
        S.op("act", lambda e: e.activation(ak[:], ak[:], AF.Silu), reads=[rak], writes=[rak])
        S.op("act", lambda e: e.activation(gq["vT"][0][:], av[:], AF.Silu), reads=[rav], writes=[gq["vT"][1]])
        for nm, acc, racc in (("q", aq, raq), ("k", ak, rak)):
            sq, rsq = self.B1.get()
            S.op("act", lambda e, sq=sq, acc=acc: e.activation(sq[:], acc[:], AF.Square), reads=[racc], writes=[rsq])
            ps, rps = self.PA.get()
            S.op("pe", lambda e, ps=ps, sq=sq: e.matmul(ps[:], lhsT=self.cB(CB_ONES), rhs=sq[:], start=True, stop=True), reads=[rsq, rCb], writes=[rps])
            S.op("act", lambda e, ps=ps: e.activation(ps[:], ps[:], AF.Ln, bias=self.eps5[:, 1:2]), reads=[rps, self.reps], writes=[rps])
            S.op("act", lambda e, ps=ps: e.activation(ps[:], ps[:], AF.Exp, scale=-0.5), reads=[rps], writes=[rps])
            if nm == "q":
                S.op("dve", lambda e, ps=ps, acc=acc: e.scalar_tensor_tensor(gq["qT"][0][:], acc[:], 128.0 ** -0.5, ps[:], op0=ALU.mult, op1=ALU.mult),
                     reads=[racc, rps], writes=[gq["qT"][1]])
            else:
                S.op("dve", lambda e, ps=ps, acc=acc: e.tensor_tensor(gq["kT"][0][:], acc[:], ps[:], ALU.mult), reads=[racc, rps], writes=[gq["kT"][1]])
        qT, rqT = gq["qT"]
        kT, rkT = gq["kT"]
        vT, rvT = gq["vT"]
        bc = lambda k: SM[:, k, t0:t0 + 4, h:h + 1].broadcast_to([128, 4, 128])
        ps, rps = self.PA.get()
        psb = ps[:].bitcast(BF16)
        for j in range(4):
            S.op("pe", lambda e, j=j, psb=psb: e.transpose(psb[:, j * 128:(j + 1) * 128], kT[:, j * 128:(j + 1) * 128], self.cB(CB_ID)),
                 reads=[rkT, rCb], writes=[rps])
        S.op("dve", lambda e, psb=psb: e.tensor_tensor(q4(gq["kg"][0][:]), q4(psb[:, 0:512]), bc(4), ALU.mult), reads=[rps, rSM], writes=[gq["kg"][1]])
        S.op("dve", lambda e, psb=psb: e.tensor_tensor(q4(gq["kt"][0][:]), q4(psb[:, 0:512]), bc(6), ALU.mult), reads=[rps, rSM], writes=[gq["kt"][1]])
        ps, rps = self.PA.get()
        psb2 = ps[:].bitcast(BF16)
        for j in range(4):
            S.op("pe", lambda e, j=j, psb2=psb2: e.transpose(psb2[:, j * 128:(j + 1) * 128], vT[:, j * 128:(j + 1) * 128], self.cB(CB_ID)),
                 reads=[rvT, rCb], writes=[rps])
        S.op("act", lambda e, psb2=psb2: e.activation(gq["vtok"][0][:], psb2[:, 0:512], AF.Copy), reads=[rps], writes=[gq["vtok"][1]])
        gcb, rgcb = self.W2.get()
        gcb4 = q4(gcb[:, 0:512])
        S.op("dve", lambda e: e.tensor_tensor(gcb4, q4(self.cF(CF_ONES, 512)), bc(2), ALU.mult), reads=[rC, rSM], writes=[rgcb])
        ps, rps = self.PA.get()
        for j in range(4):
            S.op("pe", lambda e, j=j, ps=ps: e.matmul(ps[:, j * 128:(j + 1) * 128], lhsT=gcb[:, j * 128:(j + 1) * 128], rhs=self.cF(CF_ID), start=True, stop=True),
                 reads=[rgcb, rC], writes=[rps])
        S.op("act", lambda e, ps=ps: e.activation(gq["EGr"][0][:], ps[:], AF.Exp), reads=[rps], writes=[gq["EGr"][1]])
        ps, rps = self.PA.get()
        for j in range(4):
            o = ps[:, j * 128:(j + 1) * 128]
            S.op("pe", lambda e, j=j, o=o: e.matmul(o, lhsT=gcb[:, j * 128:(j + 1) * 128], rhs=self.cF(CF_ID), start=True, stop=False), reads=[rgcb, rC], writes=[rps])
            S.op("pe", lambda e, j=j, o=o: e.matmul(o, lhsT=self.cF(CF_NID), rhs=gcb[:, j * 128:(j + 1) * 128], start=False, stop=False), reads=[rgcb, rC], writes=[rps])
            S.op("pe", lambda e, j=j, o=o: e.matmul(o, lhsT=self.cB(CB_ID), rhs=self.cB(CB_NEGL), start=False, stop=True), reads=[rCb], writes=[rps])
        DU, rDU = gq["DU"]
        S.op("act", lambda e, ps=ps: e.activation(DU[:], ps[:], AF.Exp), reads=[rps], writes=[rDU])
        Np, rNp = gq["Np"]
        Mp, rMp = gq["Mp"]
        ps, rps = self.PA.get()
        for j in range(4):
            S.op("pe", lambda e, j=j, ps=ps: e.matmul(ps[:, j * 128:(j + 1) * 128], lhsT=kT[:, j * 128:(j + 1) * 128], rhs=kT[:, j * 128:(j + 1) * 128], start=True, stop=True),
                 reads=[rkT], writes=[rps])
        S.op("dve", lambda e, ps=ps: e.tensor_tensor(Np[:], ps[:], DU[:], ALU.mult), reads=[rps, rDU], writes=[rNp])
        S.op("dve", lambda e: e.tensor_tensor(q4(Np[:]), q4(Np[:]), bc(1), ALU.mult), reads=[rNp, rSM], writes=[rNp])
        ps, rps = self.PA.get()
        psb3 = ps[:].bitcast(BF16)
        for j in range(4):
            S.op("pe", lambda e, j=j, psb3=psb3: e.transpose(psb3[:, j * 128:(j + 1) * 128], Np[:, j * 128:(j + 1) * 128], self.cB(CB_ID)),
                 reads=[rNp, rCb], writes=[rps])
        S.op("act", lambda e, psb3=psb3: e.activation(Mp[:], psb3[:, 0:512], AF.Copy), reads=[rps], writes=[rMp])
        X4, rX4 = gq["X4"]
        Y4, rY4 = gq["Y4"]
        idb = self.cB(CB_ID).unsqueeze(1).broadcast_to([128, 4, 128])
        S.op("pool", lambda e: e.tensor_copy(q4(X4[:]), idb), reads=[rCb], writes=[rX4])
        S.op("pool", lambda e: e.tensor_copy(q4(Y4[:]), idb), reads=[rCb], writes=[rY4])
        for L in range(NLEV):
            last = (L == NLEV - 1)
            p1, rp1 = self.PA.get()
            for j in range(4):
                S.op("pe", lambda e, j=j, p1=p1: e.matmul(p1[:, j * 128:(j + 1) * 128], lhsT=Mp[:, j * 128:(j + 1) * 128], rhs=X4[:, j * 128:(j + 1) * 128], start=True, stop=True),
                     reads=[rMp, rX4], writes=[rp1])
            t1, rt1 = self.TN.get()
            S.op("act", lambda e, t1=t1, p1=p1: e.activation(t1[:], p1[:], AF.Identity, scale=-1.0), reads=[rp1], writes=[rt1])
            if not last:
                p2, rp2 = self.PA.get()
                for j in range(4):
                    S.op("pe", lambda e, j=j, p2=p2: e.matmul(p2[:, j * 128:(j + 1) * 128], lhsT=Np[:, j * 128:(j + 1) * 128], rhs=Y4[:, j * 128:(j + 1) * 128], start=True, stop=True),
                         reads=[rNp, rY4], writes=[rp2])
                t2, rt2 = self.TN.get()
                S.op("act", lambda e, t2=t2, p2=p2: e.activation(t2[:], p2[:], AF.Identity, scale=-1.0), reads=[rp2], writes=[rt2])
            p3, rp3 = self.PA.get()
            for j in range(4):
                S.op("pe", lambda e, j=j, p3=p3, t1=t1: e.matmul(p3[:, j * 128:(j + 1) * 128], lhsT=Y4[:, j * 128:(j + 1) * 128], rhs=t1[:, j * 128:(j + 1) * 128], start=True, stop=True),
                     reads=[rY4, rt1], writes=[rp3])
            if not last:
                p4, rp4 = self.PA.get()
                for j in range(4):
                    S.op("pe", lambda e, j=j, p4=p4, t2=t2: e.matmul(p4[:, j * 128:(j + 1) * 128], lhsT=X4[:, j * 128:(j + 1) * 128], rhs=t2[:, j * 128:(j + 1) * 128], start=True, stop=True),
                         reads=[rX4, rt2], writes=[rp4])
            om = self.cB(CB_OM + 128 * L).bitcast(U16).unsqueeze(1).broadcast_to([128, 4, 128])
            S.op("dve", lambda e, om=om, p3=p3: e.copy_predicated(q4(X4[:]), om, q4(p3[:])), reads=[rp3, rCb, rX4], writes=[rX4])
            if not last:
                omt = self.cB(CB_OMT + 128 * L).bitcast(U16).unsqueeze(1).broadcast_to([128, 4, 128])
                S.op("dve", lambda e, omt=omt, p4=p4: e.copy_predicated(q4(Y4[:]), omt, q4(p4[:])), reads=[rp4, rCb, rY4], writes=[rY4])
        kg, rkg = gq["kg"]
        kt, rkt = gq["kt"]
        wTn, rwTn = gq["wTn"]
        qkT, rqkT = gq["qkT"]
        qdec, rqdec = gq["qdec"]
        vtok, rvtok = gq["vtok"]
        vn, rvn = gq["vn"]
        ps, rps = self.PA.get()
        for j in range(4):
            S.op("pe", lambda e, j=j, ps=ps: e.matmul(ps[:, j * 128:(j + 1) * 128], lhsT=kg[:, j * 128:(j + 1) * 128], rhs=X4[:, j * 128:(j + 1) * 128], start=True, stop=True),
                 reads=[rkg, rX4], writes=[rps])
        S.op("act", lambda e, ps=ps: e.activation(wTn[:], ps[:], AF.Identity, scale=-1.0), reads=[rps], writes=[rwTn])
        ps, rps = self.PA.get()
        for j in range(4):
            S.op("pe", lambda e, j=j, ps=ps: e.matmul(ps[:, j * 128:(j + 1) * 128], lhsT=kT[:, j * 128:(j + 1) * 128], rhs=qT[:, j * 128:(j + 1) * 128], start=True, stop=True),
                 reads=[rkT, rqT], writes=[rps])
        S.op("dve", lambda e, ps=ps: e.tensor_tensor(qkT[:], ps[:], DU[:], ALU.mult), reads=[rps, rDU], writes=[rqkT])
        S.op("dve", lambda e: e.tensor_tensor(qdec[:], qT[:], gq["EGr"][0][:], ALU.mult), reads=[rqT, gq["EGr"][1]], writes=[rqdec])
        pz, rpz = self.PA.get()
        for j in range(4):
            cs = slice(PADX + (t0 + j) * 128, PADX + (t0 + j + 1) * 128)
            for c in range(8):
                S.op("pe", lambda e, j=j, c=c, cs=cs, pz=pz: e.matmul(pz[:, j * 128:(j + 1) * 128], lhsT=self.XT[:, c, cs], rhs=Wz[0][:, c, :], start=(c == 0), stop=(c == 7)),
                     reads=[Wz[1], self.rXT[t0 + j]], writes=[rpz])
        sz, rsz = self.W2.get()
        S.op("act", lambda e, pz=pz: e.activation(sz[:, 0:512], pz[:], AF.Silu), reads=[rpz], writes=[rsz])
        gnb = self.gnw[:].unsqueeze(1).broadcast_to([128, 4, 128])
        S.op("pool", lambda e: e.tensor_tensor(q4(sz[:, 0:512]), q4(sz[:, 0:512]), gnb, ALU.mult), reads=[rsz, rLP], writes=[rsz])
        po, rpo = self.PB.get()
        for j in range(4):
            t = t0 + j
            js = slice(j * 128, (j + 1) * 128)
            sb, rsb = self.curSb
            pv, rpv = self.PA.get()
            S.op("pe", lambda e, js=js, pv=pv: e.matmul(pv[:, 0:128], lhsT=X4[:, js], rhs=vtok[:, js], start=True, stop=False), reads=[rX4, rvtok], writes=[rpv])
            S.op("pe", lambda e, js=js, pv=pv, sb=sb: e.matmul(pv[:, 0:128], lhsT=wTn[:, js], rhs=sb[:], start=False, stop=True), reads=[rwTn, rsb], writes=[rpv])
            S.op("act", lambda e, js=js, pv=pv, t=t: e.activation(vn[:, js], pv[:, 0:128], AF.Identity, scale=SM[:, 1, t, h:h + 1]), reads=[rpv, rSM, rvn], writes=[rvn])
            S.op("pe", lambda e, js=js, sb=sb: e.matmul(po[:, js], lhsT=qdec[:, js], rhs=sb[:], start=True, stop=False), reads=[rqdec, rsb], writes=[rpo])
            S.op("pe", lambda e, js=js: e.matmul(po[:, js], lhsT=qkT[:, js], rhs=vn[:, js], start=False, stop=True), reads=[rqkT, rvn], writes=[rpo])
            pS, rpS = self.PA.get()
            S.op("pe", lambda e, js=js, pS=pS: e.matmul(pS[:, 0:128], lhsT=kt[:, js], rhs=vn[:, js], start=True, stop=True), reads=[rkt, rvn], writes=[rpS])
            S.op("dve", lambda e, pS=pS, t=t: e.scalar_tensor_tensor(self.Sf[:], self.Sf[:], SM[:, 5, t, h:h + 1], pS[:, 0:128], op0=ALU.mult, op1=ALU.add),
                 reads=[rpS, rSM, self.rSf], writes=[self.rSf])
            nsb = self.Sb.get()
            S.op("act", lambda e, nsb=nsb: e.activation(nsb[0][:], self.Sf[:], AF.Copy), reads=[self.rSf], writes=[nsb[1]])
            self.curSb = nsb
        osq, rosq = self.B1.get()
        S.op("act", lambda e: e.activation(osq[:], po[:], AF.Square), reads=[rpo], writes=[rosq])
        S.op("dve", lambda e: e.tensor_reduce(self.ssq[:, 0:4], q4(osq[:]), axis=AX.X, op=ALU.add), reads=[rosq, self.rssq], writes=[self.rssq])
        S.op("act", lambda e: e.activation(self.ssq[:, 0:4], self.ssq[:, 0:4], AF.Ln, scale=1.0 / 128.0, bias=self.eps5[:, 0:1]), reads=[self.rssq, self.reps], writes=[self.rssq])
        S.op("act", lambda e: e.activation(self.ssq[:, 4:8], self.ssq[:, 0:4], AF.Exp, scale=-0.5), reads=[self.rssq], writes=[self.rssq])
        yt, ryt = self.W2.get()
        S.op("dve", lambda e: e.tensor_tensor(yt[:, 0:512], po[:], sz[:, 0:512], ALU.mult), reads=[rpo, rsz], writes=[ryt])
        yb, ryb = self.B1.get()
        S.op("dve", lambda e: e.tensor_tensor(q4(yb[:]), q4(yt[:, 0:512]), self.ssq[:, 4:8].unsqueeze(2).broadcast_to([128, 4, 128]), ALU.mult),
             reads=[ryt, self.rssq], writes=[ryb])
        ps, rps = self.PA.get()
        psb4 = ps[:].bitcast(BF16)
        for j in range(4):
            S.op("pe", lambda e, j=j, psb4=psb4: e.transpose(psb4[:, j * 128:(j + 1) * 128], yb[:, j * 128:(j + 1) * 128], self.cB(CB_ID)), reads=[ryb, rCb], writes=[rps])
        S.op("act", lambda e, psb4=psb4: e.activation(out[:, 512 * g:512 * (g + 1)], psb4[:, 0:512], AF.Copy), reads=[rps], writes=[rout])

    def wout_partial(self, l, chunks, outs):
        S = self.S
        ws = []
        for c in chunks:
            w, rw = self.Wrow.get()
            S.dma("pool", w[:], self.D["w_out"][l, c * 128:(c + 1) * 128, :], writes=[rw])
            ws.append((w, rw))
        n = len(chunks)
        for t in range(NT):
            for half in range(2):
                ps, rps = self.PA.get()
                for i in range(n):
                    S.op("pe", lambda e, i=i, ps=ps, t=t, half=half: e.matmul(ps[:], lhsT=outs[i][0][:, t * 128:(t + 1) * 128], rhs=ws[i][0][:, half * 512:(half + 1) * 512],
                                                                                   start=(i == 0), stop=(i == n - 1)),
                         reads=[outs[i][1], ws[i][1]], writes=[rps])
                xs = self.X[:, t, half * 512:(half + 1) * 512]
                S.op("dve", lambda e, xs=xs, ps=ps: e.tensor_tensor(xs, xs, ps[:], ALU.add), reads=[rps, self.rX[t]], writes=[self.rX[t]])

    def attn_pair(self, l, p, diff):
        S = self.S
        rCb, rC = self.rCb, self.rC
        base = 22 if diff else 16
        Wq = self.load_wt(l, base + p)
        Wk = self.load_wt(l, base + 2 + p)
        Wv = self.load_wt(l, base + 4 + p)
        qT, rqT = self.A4.get()
        kT, rkT = self.A4.get()
        out, rout = self.A4.get()
        vt, rvt = self.VT.get()
        S.op("pool", lambda e: e.memset(vt[:, :, 64:128], 0.0), writes=[rvt])
        S.op("pool", lambda e: e.memset(vt[:, :, 64:65], 1.0), reads=[rvt], writes=[rvt])
        for (W, dst, rdst) in ((Wq, qT, rqT), (Wk, kT, rkT)):
            for g in range(4):
                sl = slice(PADX + 512 * g, PADX + 512 * (g + 1))
                ps, rps = self.PA.get()
                for c in range(8):
                    S.op("pe", lambda e, c=c, W=W, ps=ps, sl=sl: e.matmul(ps[:], lhsT=W[0][:, c, :], rhs=self.XT[:, c, sl], start=(c == 0), stop=(c == 7)),
                         reads=[W[1]] + self.rXT[4 * g:4 * g + 4], writes=[rps])
                S.op("act", lambda e, ps=ps, dst=dst, g=g: e.activation(dst[:, 512 * g:512 * (g + 1)], ps[:], AF.Copy), reads=[rps], writes=[rdst])
        for t in range(NT):
            cs = slice(PADX + t * 128, PADX + (t + 1) * 128)
            ps, rps = self.PA.get()
            for c in range(8):
                S.op("pe", lambda e, c=c, ps=ps, cs=cs: e.matmul(ps[:, 0:128], lhsT=self.XT[:, c, cs], rhs=Wv[0][:, c, :], start=(c == 0), stop=(c == 7)),
                     reads=[Wv[1], self.rXT[t]], writes=[rps])
            S.op("dve", lambda e, ps=ps, t=t: e.tensor_copy(vt[:, t, 0:64], ps[:, 0:64]), reads=[rps, rvt], writes=[rvt])
            S.op("act", lambda e, ps=ps, t=t: e.activation(vt[:, t, 128:192], ps[:, 64:128], AF.Copy), reads=[rps, rvt], writes=[rvt])
        scale = (32.0 if diff else 64.0) ** -0.5
        for hh in range(2):
            rows = slice(0, 64) if hh == 0 else slice(64, 128)
            for g in range(4):
                nm = 4 * g + 4
                qs = slice(512 * g, 512 * (g + 1))
                nmaps = 2 if diff else 1
                accs = [self.PB.get() for _ in range(nmaps)]
                for m in range(nm):
                    ks = slice(128 * m, 128 * (m + 1))
                    for mm in range(nmaps):
                        acc, racc = accs[mm]
                        if diff:
                            kb = hh * 64 + mm * 32
                            kr = slice(kb, kb + 32)
                        else:
                            kb = hh * 64
                            kr = slice(kb, kb + 64)
                        stp, rstp = self.PA.get()
                        if kb == 96:
                            S.op("pe", lambda e, stp=stp, kr=kr, ks=ks: e.matmul(stp[:], lhsT=kT[kr, ks], rhs=qT[kr, qs], start=True, stop=True, tile_position=(96, 0)),
                                 reads=[rkT, rqT], writes=[rstp])
                        else:
                            S.op("pe", lambda e, stp=stp, kr=kr, ks=ks: e.matmul(stp[:], lhsT=kT[kr, ks], rhs=qT[kr, qs], start=True, stop=True),
                                 reads=[rkT, rqT], writes=[rstp])
                        P, rP = self.B1.get()
                        S.op("act", lambda e, P=P, stp=stp: e.activation(P[:], stp[:], AF.Exp, scale=scale), reads=[rstp], writes=[rP])
                        if not diff:
                            off = CB_WB + (min(4 * g - m, 5) + 3) * 128
                            S.op("dve", lambda e, P=P, off=off: e.tensor_tensor(P[:], P[:], self.cb[:, off:off + 512], ALU.mult), reads=[rP, rCb], writes=[rP])
                        elif m >= 4 * g:
                            off = CB_CB + (3 - (m - 4 * g)) * 128
                            S.op("dve", lambda e, P=P, off=off: e.tensor_tensor(P[:], P[:], self.cb[:, off:off + 512], ALU.mult), reads=[rP, rCb], writes=[rP])
                        if hh == 0:
                            S.op("pe", lambda e, acc=acc, P=P, m=m: e.matmul(acc[0:65, :], lhsT=vt[:, m, 0:65], rhs=P[:], start=(m == 0), stop=(m == nm - 1)),
                                 reads=[rvt, rP], writes=[racc])
                        else:
                            S.op("pe", lambda e, acc=acc, P=P, m=m: e.matmul(acc[:, :], lhsT=vt[:, m, 64:192], rhs=P[:], start=(m == 0), stop=(m == nm - 1)),
                                 reads=[rvt, rP], writes=[racc])
                rowp = 64 if hh == 0 else 0
                bcs = []
                for mm in range(nmaps):
                    acc, racc = accs[mm]
                    S.op("dve", lambda e, acc=acc: e.reciprocal(self.rrow[rowp:rowp + 1, :], acc[rowp:rowp + 1, :]), reads=[racc, self.rrrow], writes=[self.rrrow])
                    pbc, rpbc = self.PA.get()
                    if hh == 0:
                        S.op("pe", lambda e, pbc=pbc: e.matmul(pbc[0:64, :], lhsT=self.cf[64:96, CF_SEL:CF_SEL + 64], rhs=self.rrow[64:96, :], start=True, stop=True),
                             reads=[self.rrrow, rC], writes=[rpbc])
                    else:
                        S.op("pe", lambda e, pbc=pbc: e.matmul(pbc[:, :], lhsT=self.cf[0:32, CF_SEL:CF_SEL + 128], rhs=self.rrow[0:32, :], start=True, stop=True),
                             reads=[self.rrrow, rC], writes=[rpbc])
                    b_, rb_ = self.W2.get()
                    if mm == 0:
                        S.op("act", lambda e, b_=b_, pbc=pbc: e.activation(b_[rows, 0:512], pbc[rows, :], AF.Copy), reads=[rpbc], writes=[rb_])
                    else:
                        S.op("act", lambda e, b_=b_, pbc=pbc: e.activation(b_[rows, 0:512], pbc[rows, :], AF.Identity, scale=self.lam[rows, 5:6]),
                             reads=[rpbc, self.rlam], writes=[rb_])
                    bcs.append((b_, rb_))
                if not diff:
                    acc, racc = accs[0]
                    b_, rb_ = bcs[0]
                    S.op("dve", lambda e, acc=acc, b_=b_: e.tensor_tensor(out[rows, qs], acc[rows, :], b_[rows, 0:512], ALU.mult), reads=[racc, rb_], writes=[rout])
                else:
                    (b1, rb1), (b2, rb2) = bcs
                    S.op("dve", lambda e: e.tensor_tensor(b1[rows, 0:512], accs[0][0][rows, :], b1[rows, 0:512], ALU.mult), reads=[accs[0][1], rb1], writes=[rb1])
                    S.op("dve", lambda e: e.tensor_tensor(b2[rows, 0:512], accs[1][0][rows, :], b2[rows, 0:512], ALU.mult), reads=[accs[1][1], rb2], writes=[rb2])
                    S.op("pool", lambda e: e.tensor_tensor(b1[rows, 0:512], b1[rows, 0:512], b2[rows, 0:512], ALU.add), reads=[rb1, rb2], writes=[rb1])
                    osq, rosq = self.B1.get()
                    S.op("act", lambda e: e.activation(osq[rows, :], b1[rows, 0:512], AF.Square), reads=[rb1], writes=[rosq])
                    pss, rpss = self.PA.get()
                    if hh == 0:
                        S.op("pe", lambda e: e.matmul(pss[0:64, :], lhsT=self.cb[0:64, CB_ONES:CB_ONES + 64], rhs=osq[0:64, :], start=True, stop=True), reads=[rosq, rCb], writes=[rpss])
                    else:
                        S.op("pe", lambda e: e.matmul(pss[:, :], lhsT=self.cb[64:128, CB_ONES:CB_ONES + 128], rhs=osq[64:128, :], start=True, stop=True), reads=[rosq, rCb], writes=[rpss])
                    S.op("act", lambda e: e.activation(pss[rows, :], pss[rows, :], AF.Ln, scale=1.0 / 64.0, bias=self.eps5[rows, 0:1]), reads=[rpss, self.reps], writes=[rpss])
                    S.op("act", lambda e: e.activation(pss[rows, :], pss[rows, :], AF.Exp, scale=-0.5), reads=[rpss], writes=[rpss])
                    S.op("dve", lambda e: e.scalar_tensor_tensor(out[rows, qs], b1[rows, 0:512], self.lam[rows, 6:7], pss[rows, :], op0=ALU.mult, op1=ALU.mult),
                         reads=[rb1, rpss, self.rlam], writes=[rout])
        return out, rout

    def layernorm(self, l, which, last):
        S, D = self.S, self.D
        g_, rg = self.A4.get()
        b_, rb = self.A4.get()
        gv = g_[:].bitcast(F32)
        bv = b_[:].bitcast(F32)
        S.dma("sp", gv, D["ln%d_g" % which][l:l + 1, :].partition_broadcast(128), writes=[rg])
        S.dma("sp", bv, D["ln%d_b" % which][l:l + 1, :].partition_broadcast(128), writes=[rb])
        for t in range(NT):
            xt = self.X[:, t, :]
            rx = self.rX[t]
            st, rst = self.LNS.get()
            S.op("dve", lambda e, st=st, xt=xt: e.bn_stats(st[:, 0:6], xt[:, 0:512]), reads=[rx], writes=[rst])
            S.op("dve", lambda e, st=st, xt=xt: e.bn_stats(st[:, 6:12], xt[:, 512:1024]), reads=[rx, rst], writes=[rst])
            S.op("dve", lambda e, st=st: e.bn_aggr(st[:, 12:14], st[:, 0:12]), reads=[rst], writes=[rst])
            S.op("act", lambda e, st=st: e.activation(st[:, 14:15], st[:, 13:14], AF.Ln, bias=self.eps5[:, 0:1]), reads=[rst, self.reps], writes=[rst])
            S.op("act", lambda e, st=st: e.activation(st[:, 14:15], st[:, 14:15], AF.Exp, scale=-0.5), reads=[rst], writes=[rst])
            S.op("dve", lambda e, st=st: e.scalar_tensor_tensor(st[:, 15:16], st[:, 12:13], -1.0, st[:, 14:15], op0=ALU.mult, op1=ALU.mult), reads=[rst], writes=[rst])
            S.op("act", lambda e, st=st, xt=xt: e.activation(xt, xt, AF.Identity, scale=st[:, 14:15], bias=st[:, 15:16]), reads=[rst, rx], writes=[rx])
            S.op("dve", lambda e, xt=xt: e.tensor_tensor(xt, xt, gv, ALU.mult), reads=[rx, rg], writes=[rx])
            S.op("pool", lambda e, xt=xt: e.tensor_tensor(xt, xt, bv, ALU.add), reads=[rx, rb], writes=[rx])
            if last:
                S.dma("sp", D["y"][t * 128:(t + 1) * 128, :], xt, reads=[rx], writes=[S.out_region])
            else:
                self.x_tile_out(t, scale=True)

    def ffn(self, l):
        S, D = self.S, self.D
        rLP = self.rLP
        wins = []
        s = 0
        while s < T:
            n = min(510, T - s)
            wins.append((s, n))
            s += n
        for g0 in range(0, NFT, 4):
            grp = list(range(g0, min(g0 + 4, NFT)))
            hts = []
            for i in grp:
                cw = 128 if i < NFT - 1 else 64
                wu, rwu = self.Wup.get()
                S.dma("pool", wu[:], D["w_up_t"][l, i], writes=[rwu])
                ht, rht = self.A4.get()
                for (s, n) in wins:
                    w0 = PADX + s - 2
                    wd = n + 2
                    tiles = list(range(max(s - 2, 0) // 128, (s + n - 1) // 128 + 1))
                    rx = [self.rXT[t] for t in tiles] + ([self.rXTpad] if s == 0 else [])
                    pg, rpg = self.PA.get()
                    pv, rpv = self.PA.get()
                    for (pp, rpp, co) in ((pg, rpg, 0), (pv, rpv, 128)):
                        for c in range(8):
                            S.op("pe", lambda e, c=c, pp=pp, co=co, cw=cw, wu=wu, w0=w0, wd=wd: e.matmul(pp[0:cw, 0:wd], lhsT=wu[:, c, co:co + cw], rhs=self.XT[:, c, w0:w0 + wd],
                                                                                                       start=(c == 0), stop=(c == 7)),
                                 reads=[rwu] + rx, writes=[rpp])
                    ag, rag = self.W2.get()
                    av, rav = self.W2.get()
                    for (pp, rpp, aa, raa, gv) in ((pg, rpg, ag, rag, 0), (pv, rpv, av, rav, 1)):
                        S.op("act", lambda e, pp=pp, aa=aa, gv=gv, cw=cw, n=n, i=i: e.activation(aa[0:cw, 0:n], pp[0:cw, 2:2 + n], AF.Identity, scale=self.cwf[0:cw, i, gv, 2:3]),
                             reads=[rpp, rLP], writes=[raa])
                        for j in (1, 0):
                            S.op("dve", lambda e, pp=pp, aa=aa, gv=gv, cw=cw, n=n, i=i, j=j: e.scalar_tensor_tensor(aa[0:cw, 0:n], pp[0:cw, j:j + n], self.cwf[0:cw, i, gv, j:j + 1], aa[0:cw, 0:n],
                                                                                                                    op0=ALU.mult, op1=ALU.add),
                                 reads=[rpp, raa, rLP], writes=[raa])
                    S.op("act", lambda e, ag=ag, cw=cw, n=n: e.activation(ag[0:cw, 0:n], ag[0:cw, 0:n], AF.Silu), reads=[rag], writes=[rag])
                    S.op("dve", lambda e, ag=ag, av=av, ht=ht, cw=cw, n=n, s=s: e.tensor_tensor(ht[0:cw, s:s + n], ag[0:cw, 0:n], av[0:cw, 0:n], ALU.mult),
                         reads=[rag, rav], writes=[rht])
                hts.append((ht, rht, cw))
            wds = []
            for i in grp:
                w, rw = self.Wrow.get()
                S.dma("pool", w[:], D["w_down_t"][l, i], writes=[rw])
                wds.append((w, rw))
            nk = len(hts)
            for t in range(NT):
                for half in range(2):
                    ps, rps = self.PA.get()
                    for k in range(nk):
                        ht, rht, cw = hts[k]
                        S.op("pe", lambda e, k=k, ht=ht, cw=cw, ps=ps, t=t, half=half: e.matmul(ps[:], lhsT=ht[0:cw, t * 128:(t + 1) * 128], rhs=wds[k][0][0:cw, half * 512:(half + 1) * 512],
                                                                                              start=(k == 0), stop=(k == nk - 1)),
                             reads=[rht, wds[k][1]], writes=[rps])
                    xs = self.X[:, t, half * 512:(half + 1) * 512]
                    S.op("dve", lambda e, xs=xs, ps=ps: e.tensor_tensor(xs, xs, ps[:], ALU.add), reads=[rps, self.rX[t]], writes=[self.rX[t]])


def _host_layout(inputs):
    L = DEPTH
    f = lambda a: np.ascontiguousarray(np.asarray(a, dtype=np.float32))
    w_in = np.asarray(inputs["w_in"], dtype=np.float32)
    wi = np.concatenate([w_in[:, :, :2048], w_in[:, :, 2056:]], axis=2)
    w_in_t = f(wi.reshape(L, 8, 128, 28, 128).transpose(0, 3, 2, 1, 4))
    w_bd = f(w_in[:, :, 2048:2056].reshape(L, 8, 128, 8).transpose(0, 2, 1, 3))
    w_up = np.asarray(inputs["w_up"], dtype=np.float32)
    pad = NFT * 128 - DFF
    gate = np.pad(w_up[:, :, :DFF], ((0, 0), (0, 0), (0, pad))).reshape(L, 8, 128, NFT, 128)
    val = np.pad(w_up[:, :, DFF:], ((0, 0), (0, 0), (0, pad))).reshape(L, 8, 128, NFT, 128)
    w_up_t = f(np.stack([gate, val], axis=4).transpose(0, 3, 2, 1, 4, 5).reshape(L, NFT, 128, 8, 256))
    w_down = np.asarray(inputs["w_down"], dtype=np.float32)
    w_down_t = f(np.pad(w_down, ((0, 0), (0, pad), (0, 0))).reshape(L, NFT, 128, DM))
    gconv = f(np.asarray(inputs["gdn_conv"], np.float32).reshape(L, 4, 12, 128).transpose(0, 3, 2, 1))
    fc = np.asarray(inputs["ffn_conv"], np.float32)
    fg = np.pad(fc[:, :, :DFF], ((0, 0), (0, 0), (0, pad))).reshape(L, 3, NFT, 128)
    fv = np.pad(fc[:, :, DFF:], ((0, 0), (0, 0), (0, pad))).reshape(L, 3, NFT, 128)
    fconv = f(np.stack([fg, fv], axis=2).transpose(0, 4, 3, 2, 1))
    cf, cbf = host_consts()
    dn = np.asarray(inputs["diff_norm"], np.float32)
    shared = {
        "w_in_t": w_in_t, "w_bd": w_bd, "w_out": f(inputs["w_out"]), "w_up_t": w_up_t, "w_down_t": w_down_t,
        "gconv": gconv, "fconv": fconv,
        "ln1_g": f(inputs["ln1_g"]), "ln1_b": f(inputs["ln1_b"]), "ln2_g": f(inputs["ln2_g"]), "ln2_b": f(inputs["ln2_b"]),
        "a_log": f(inputs["gdn_a_log"]), "dt_bias": f(inputs["gdn_dt_bias"]), "gnorm": f(inputs["gdn_norm"]),
        "dlam": f(np.asarray(inputs["diff_lambda"], np.float32).reshape(L, 128)),
        "dnorm": f(np.concatenate([dn, dn], axis=1).reshape(L, 128, 1)),
        "cf": f(cf), "cbf": f(cbf),
    }
    return shared


_PROG_CACHE = {}


def run_cores(inputs, nlayers=DEPTH, dbg=None, ncores=8):
    key = (nlayers, dbg)
    if key not in _PROG_CACHE:
        _PROG_CACHE[key] = Prog(nlayers, dbg)
    prog = _PROG_CACHE[key]
    shared = _host_layout(inputs)
    x = np.asarray(inputs["x"], dtype=np.float32)
    in_maps = []
    for b in range(ncores):
        m = dict(shared)
        m["x"] = np.ascontiguousarray(x[b])
        in_maps.append(m)
    res = run_bass_kernel_spmd(prog.nc, in_maps, core_ids=list(range(ncores)))
    return np.stack([np.asarray(r["y"], dtype=np.float32) for r in res.results], axis=0)


def kernel(**inputs):
    return run_cores(inputs)
```

```python
import math
import numpy as np
from contextlib import ExitStack
import concourse.bass as bass
import concourse.mybir as mybir
from concourse.bass_utils import run_bass_kernel_spmd

F32 = mybir.dt.float32
BF16 = mybir.dt.bfloat16
U16 = mybir.dt.uint16
AF = mybir.ActivationFunctionType
ALU = mybir.AluOpType
AX = mybir.AxisListType


class Region:
    __slots__ = ("name", "w", "rs", "dsem", "dcnt")

    def __init__(self, name, dsem=None):
        self.name = name
        self.w = None
        self.rs = {}
        self.dsem = dsem
        self.dcnt = 0


class _Rec:
    def __init__(self):
        self.calls = []

    def __getattr__(self, name):
        def f(*a, **k):
            self.calls.append((name, a, k))
            return self
        return f


class _Eng:
    def __init__(self, name, eng, sem):
        self.name = name
        self.eng = eng
        self.sem = sem
        self.count = 0
        self.seen = {}
        self.ops = []
        self.waited = set()
        self.rank = None


class Sched:
    def __init__(self, nc, st):
        self.nc = nc
        self.st = st
        self.E = {}
        for name, eng in (("pe", nc.tensor), ("act", nc.scalar), ("dve", nc.vector),
                          ("pool", nc.gpsimd), ("sp", nc.sync)):
            sem = st.enter_context(nc.semaphore("s_" + name))
            self.E[name] = _Eng(name, eng, sem)
        self.n_ops = 0
        self._uid = 0
        self.out_region = self.region("out", dma=True)

    def sb(self, name, shape, dtype):
        return self.st.enter_context(self.nc.sbuf_tensor("sb_" + name, shape, dtype))

    def ps(self, name, shape, dtype):
        return self.st.enter_context(self.nc.psum_tensor("ps_" + name, shape, dtype))

    def region(self, name, dma=False):
        dsem = None
        if dma:
            self._uid += 1
            dsem = self.st.enter_context(self.nc.semaphore("d%d_%s" % (self._uid, name)))
        return Region(name, dsem)

    def _need(self, me, tok, waits, raw):
        if tok is None:
            return
        kind, key, sem, cnt = tok
        if kind == "e" and key == me.name:
            if not raw or me.name in ("pe", "sp"):
                return
        if me.seen.get(key, 0) >= cnt:
            return
        me.seen[key] = cnt
        waits[key] = (kind, key, sem, cnt)
        if kind == "e":
            self.E[key].waited.add(cnt)

    def _collect(self, me, reads, writes):
        waits = {}
        for r in reads:
            self._need(me, r.w, waits, True)
        for r in writes:
            self._need(me, r.w, waits, False)
            for t in r.rs.values():
                self._need(me, t, waits, False)
        return list(waits.values())

    def op(self, engname, fn, reads=(), writes=()):
        me = self.E[engname]
        waits = self._collect(me, reads, writes)
        me.count += 1
        tok = ("e", me.name, me.sem, me.count)
        for r in reads:
            r.rs[me.name] = tok
        for r in writes:
            r.w = tok
            r.rs = {}
        sem = me.sem
        rec = _Rec()
        fn(rec)
        assert len(rec.calls) == 1
        name, a, k = rec.calls[0]

        idx = me.count

        def emit(e, waits=waits, name=name, a=a, k=k, sem=sem, idx=idx, me=me):
            self._emit_waits(e, waits)
            ins = getattr(e, name)(*a, **k)
            if idx in me.rank:
                ins.then_inc(sem, 1)
        me.ops.append(emit)
        self.n_ops += 1

    def dma(self, queue, out_ap, in_ap, reads=(), writes=()):
        me = self.E[queue]
        waits = self._collect(me, reads, writes)
        tgt = None
        for r in writes:
            if r.dsem is not None:
                tgt = r
        assert tgt is not None, "dma needs a dma-region target"
        tgt.dcnt += 16
        tok = ("d", "d:" + tgt.name + str(id(tgt)), tgt.dsem, tgt.dcnt)
        for r in reads:
            r.rs[tok[1]] = tok
        for r in writes:
            r.w = tok
            r.rs = {}
        sem = tgt.dsem

        def emit(e, waits=waits, sem=sem, out_ap=out_ap, in_ap=in_ap):
            self._emit_waits(e, waits)
            e.dma_start(out=out_ap, in_=in_ap).then_inc(sem, 16)
        me.ops.append(emit)
        self.n_ops += 1

    def _emit_waits(self, e, waits):
        for kind, key, sem, cnt in waits:
            if kind == "e":
                e.wait_ge(sem, self.E[key].rank[cnt])
            else:
                e.wait_ge(sem, cnt)

    def finish(self):
        for eng in self.E.values():
            eng.rank = {c: i + 1 for i, c in enumerate(sorted(eng.waited))}
        outr = self.out_region
        sp = self.E["sp"]
        total = outr.dcnt

        def fin(e, s=outr.dsem, v=total):
            e.wait_ge(s, v)
        sp.ops.append(fin)
        with self.nc.Block() as block:
            @block.tensor
            def _(e):
                for f in self.E["pe"].ops:
                    f(e)

            @block.scalar
            def _(e):
                for f in self.E["act"].ops:
                    f(e)

            @block.vector
            def _(e):
                for f in self.E["dve"].ops:
                    f(e)

            @block.gpsimd
            def _(e):
                for f in self.E["pool"].ops:
                    f(e)

            @block.sync
            def _(e):
                for f in self.E["sp"].ops:
                    f(e)


T = 2048
DM = 1024
NT = 16
PADX = 8
DEPTH = 4
DFF = 2752
NFT = 22
ALPHA = (2 * DEPTH) ** 0.25
EPS = 1e-5
NLEV = 7


class Flat3:
    def __init__(self, t, n, w):
        self.t, self.n, self.w = t, n, w

    def __getitem__(self, key):
        p, c, sl = key
        a = 0 if sl.start is None else sl.start
        b = self.w if sl.stop is None else sl.stop
        return self.t[p, c * self.w + a:c * self.w + b]


class Ring:
    def __init__(self, S, name, n, shape, dtype, psum=False, dma=False):
        self.b = []
        for i in range(n):
            nm = "%s%d" % (name, i)
            t = (S.ps if psum else S.sb)(nm, shape, dtype)
            self.b.append((t, S.region(nm, dma=dma)))
        self.i = 0

    def get(self):
        x = self.b[self.i % len(self.b)]
        self.i += 1
        return x


def q4(ap):
    return ap.rearrange("p (j f) -> p j f", j=4)


def host_consts():
    f = {}
    I = np.eye(128, dtype=np.float32)
    j = np.arange(128)[:, None]
    i = np.arange(128)[None, :]
    sel = np.zeros((128, 128), np.float32)
    sel[0, :] = 1.0
    sel[64, :] = 1.0
    cf = np.concatenate([I, -I, (j <= i).astype(np.float32), sel, np.ones((128, 640), np.float32)], axis=1)
    negl = np.where(i < j, -30000.0, 0.0).astype(np.float32)
    om, omt = [], []
    for L in range(NLEV):
        b = 1 << L
        m = ((j // (2 * b) == i // (2 * b)) & ((j // b) % 2 == 0) & ((i // b) % 2 == 1)).astype(np.float32)
        om.append(m)
        omt.append(m.T.copy())
    ik = np.arange(128)[:, None]
    wb = []
    for dl in range(-3, 9):
        d = 128 * dl + np.arange(128)[None, :] - ik
        w = ((d >= 0) & (d <= 128)).astype(np.float32) + ((d >= 0) & (d <= 512) & (d % 4 == 0)) + ((d >= 0) & (d % 16 == 0))
        wb.append(w.astype(np.float32))
    cb = []
    for dl in range(-3, 4):
        d = 128 * dl + np.arange(128)[None, :] - ik
        cb.append((d >= 0).astype(np.float32))
    cbf = np.concatenate([I, negl] + om + omt + wb + cb + [np.ones((128, 128), np.float32)], axis=1)
    return cf.astype(np.float32), cbf.astype(np.float32)


CF_ID, CF_NID, CF_UTRI, CF_SEL, CF_ONES = 0, 128, 256, 384, 512
CB_ID, CB_NEGL, CB_OM, CB_OMT, CB_WB, CB_CB, CB_ONES = 0, 128, 256, 256 + 896, 256 + 1792, 256 + 1792 + 1536, 256 + 1792 + 1536 + 896
NCF = 512 + 640
NCB = CB_ONES + 128


class Prog:
    def __init__(self, nlayers=DEPTH, dbg=None):
        self.nlayers = nlayers
        self.dbg = dbg
        nc = self.nc = bass.Bass("TRN2", target_bir_lowering=False)
        L = DEPTH
        D = {}

        def inp(name, shape, dt=F32):
            D[name] = nc.dram_tensor(name, shape, dt, kind="ExternalInput").ap()
        inp("x", [T, DM])
        inp("w_in_t", [L, 28, 128, 8, 128])
        inp("w_bd", [L, 128, 8, 8])
        inp("w_out", [L, DM, DM])
        inp("w_up_t", [L, NFT, 128, 8, 256])
        inp("w_down_t", [L, NFT, 128, DM])
        inp("gconv", [L, 128, 12, 4])
        inp("fconv", [L, 128, NFT, 2, 3])
        inp("ln1_g", [L, DM]); inp("ln1_b", [L, DM]); inp("ln2_g", [L, DM]); inp("ln2_b", [L, DM])
        inp("a_log", [L, 4]); inp("dt_bias", [L, 4]); inp("gnorm", [L, 128])
        inp("dlam", [L, 128]); inp("dnorm", [L, 128, 1])
        inp("cf", [128, NCF]); inp("cbf", [128, NCB])
        D["y"] = nc.dram_tensor("y", [T, DM], F32, kind="ExternalOutput").ap()
        self.D = D
        with ExitStack() as st:
            self.S = S = Sched(nc, st)
            self.alloc()
            self.prologue()
            if dbg == "pro":
                w, rw = self.W2.get()
                S.op("dve", lambda e: e.tensor_copy(w[:, 0:512], self.XT[:, 0, PADX:PADX + 512]), reads=self.rXT[0:4], writes=[rw])
                self.dump(0, w[:, 0:512], rw)
                self.dump(1, self.XT[:, 0, PADX:PADX + 512], self.rXT[0], bf=True)
                w2, rw2 = self.W2.get()
                S.op("dve", lambda e: e.tensor_copy(w2[:, 0:512], self.cb[:, 0:512]), reads=[self.rCb], writes=[rw2])
                self.dump(2, w2[:, 0:512], rw2)
                self.dump(3, self.X[:, 0, 0:512], self.rX[0])
                xb, rxb = self.XB.b[0]
                w3, rw3 = self.W2.get()
                S.op("dve", lambda e: e.tensor_copy(w3[:, 0:512], xb[:, 0:512]), reads=[rxb], writes=[rw3])
                self.dump(4, w3[:, 0:512], rw3)
                ps, rps = self.PA.get()
                psb = ps[:].bitcast(BF16)
                S.op("pe", lambda e: e.transpose(psb[:, 0:128], xb[:, 0:128], self.cB(CB_ID)), reads=[rxb, self.rCb], writes=[rps])
                w4, rw4 = self.W2.get()
                S.op("dve", lambda e: e.tensor_copy(w4[:, 0:128], psb[:, 0:128]), reads=[rps], writes=[rw4])
                S.op("act", lambda e: e.activation(w4[:, 128:256], psb[:, 0:128], AF.Copy), reads=[rps, rw4], writes=[rw4])
                S.op("dve", lambda e: e.tensor_copy(w4[:, 256:384], self.XT[:, 0, PADX + 1920:PADX + 2048]), reads=[self.rXT[15], rw4], writes=[rw4])
                self.dump(5, w4[:, 0:512], rw4)
                ps2, rps2 = self.PA.get()
                psb2 = ps2[:].bitcast(BF16)
                for c4 in range(4):
                    S.op("pe", lambda e, c4=c4: e.transpose(psb2[:, c4 * 128:(c4 + 1) * 128], xb[:, c4 * 128:(c4 + 1) * 128], self.cB(CB_ID)), reads=[rxb, self.rCb], writes=[rps2])
                b1, rb1 = self.B1.get()
                b2, rb2 = self.B1.get()
                b3, rb3 = self.B1.get()
                S.op("dve", lambda e: e.tensor_copy(b1[:], psb2[:, 0:512]), reads=[rps2], writes=[rb1])
                S.op("dve", lambda e: e.tensor_scalar(b2[:], psb2[:, 0:512], 1.0, None, op0=ALU.mult), reads=[rps2], writes=[rb2])
                S.op("act", lambda e: e.activation(b3[:], psb2[:, 0:512], AF.Identity), reads=[rps2], writes=[rb3])
                for k, (bb, rbb) in enumerate(((b1, rb1), (b2, rb2), (b3, rb3))):
                    wk, rwk = self.W2.get()
                    S.op("dve", lambda e, wk=wk, bb=bb: e.tensor_copy(wk[:, 0:512], bb[:]), reads=[rbb], writes=[rwk])
                    self.dump(6 + k, wk[:, 0:512], rwk)
            else:
                for l in range(nlayers):
                    self.layer(l)
            S.finish()

    def alloc(self):
        S = self.S
        self.X = S.sb("X", [128, NT, DM], F32)
        self.rX = [S.region("X%d" % t) for t in range(NT)]
        self.XT = Flat3(S.sb("XT", [128, 8 * (PADX + T)], BF16), 8, PADX + T)
        self.rXT = [S.region("XT%d" % t) for t in range(NT)]
        self.rXTpad = S.region("XTpad")
        self.PA = Ring(S, "pa", 6, [128, 512], F32, psum=True)
        self.PB = Ring(S, "pb", 2, [128, 512], F32, psum=True)
        self.W2 = Ring(S, "w2", 4, [128, 516], F32)
        self.B1 = Ring(S, "b1", 5, [128, 512], BF16)
        self.A4 = Ring(S, "a4", 5, [128, T], BF16, dma=True)
        self.VT = Ring(S, "vt", 1, [128, NT, 192], BF16)
        self.Wt = Ring(S, "wt", 4, [128, 8, 128], BF16, dma=True)
        self.Wup = Ring(S, "wup", 2, [128, 8, 256], BF16, dma=True)
        self.Wrow = Ring(S, "wrow", 4, [128, DM], BF16, dma=True)
        self.TN = Ring(S, "tn", 2, [128, 512], BF16)
        self.XB = Ring(S, "xb", 1, [128, DM], BF16)
        self.LNS = Ring(S, "lns", 3, [128, 16], F32)
        self.gq = {}
        for n in "qT kT vT vtok DU EGr Np Mp X4 Y4 kg kt wTn qkT qdec vn".split():
            self.gq[n] = (S.sb("gq_" + n, [128, 512], BF16), S.region("gq_" + n))
        self.gacc = {n: (S.sb("gacc_" + n, [128, 512], F32), S.region("gacc_" + n)) for n in "qkv"}
        self.HALO = S.sb("halo", [128, 3, 4], F32)
        self.rHALO = [S.region("halo%d" % i) for i in range(3)]
        self.SM = S.sb("SM", [128, 8, NT, 4], F32)
        self.rSM = S.region("SM")
        self.aexp = S.sb("aexp", [128, 4], F32)
        self.raexp = S.region("aexp")
        self.Sf = S.sb("Sf", [128, 128], F32)
        self.rSf = S.region("Sf")
        self.Sb = Ring(S, "Sb", 2, [128, 128], BF16)
        self.ssq = S.sb("ssq", [128, 8], F32)
        self.rssq = S.region("ssq")
        self.cwg = S.sb("cwg", [128, 12, 4], F32)
        self.cwf = S.sb("cwf", [128, NFT, 2, 3], F32)
        self.alog = S.sb("alog", [128, 4], F32)
        self.dtb = S.sb("dtb", [128, 4], F32)
        self.gnw = S.sb("gnw", [128, 128], F32)
        self.lv = S.sb("lv", [128, 128], F32)
        self.dnw = S.sb("dnw", [128, 1], F32)
        self.rLP = S.region("LP", dma=True)
        self.wbd = S.sb("wbd", [128, 8, 8], BF16)
        self.rwbd = S.region("wbd", dma=True)
        self.lam = S.sb("lam", [128, 8], F32)
        self.rlam = S.region("lam")
        self.rrow = S.sb("rrow", [128, 512], F32)
        self.rrrow = S.region("rrow")
        self.cf = S.sb("cf", [128, NCF], F32)
        self.cb = S.sb("cb", [128, NCB], BF16)
        self.rC = S.region("C", dma=True)
        self.rCb = S.region("Cb", dma=True)
        self.eps5 = S.sb("eps5", [128, 2], F32)
        self.reps = S.region("eps")
        self.rXload = [S.region("xl%d" % i, dma=True) for i in range(4)]

    def cF(self, off, n=128):
        return self.cf[:, off:off + n]

    def cB(self, off, n=128):
        return self.cb[:, off:off + n]

    def prologue(self):
        S, D = self.S, self.D
        S.dma("sp", self.cf[:], D["cf"], writes=[self.rC])
        S.dma("pool", self.cb[:], D["cbf"], writes=[self.rCb])
        for i in range(4):
            for t in range(4 * i, 4 * i + 4):
                S.dma("sp", self.X[:, t, :], D["x"][t * 128:(t + 1) * 128, :], writes=[self.rXload[i]])
        for c in range(8):
            S.op("pool", lambda e, c=c: e.memset(self.XT[:, c, 0:PADX], 0.0), writes=[self.rXTpad])
        S.op("pool", lambda e: e.memset(self.eps5[:, 0:1], EPS), writes=[self.reps])
        S.op("pool", lambda e: e.memset(self.eps5[:, 1:2], 1e-6), reads=[self.reps], writes=[self.reps])
        S.op("pool", lambda e: e.memset(self.rrow[:], 0.0), writes=[self.rrrow])
        for t in range(NT):
            self.x_tile_out(t, extra_reads=[self.rXload[t // 4]], scale=True)

    def x_tile_out(self, t, extra_reads=(), scale=True):
        S = self.S
        xb, rxb = self.XB.get()
        xbv = xb[:, 0:DM]
        S.op("act", lambda e: e.activation(xbv, self.X[:, t, :], AF.Copy), reads=[self.rX[t]] + list(extra_reads), writes=[rxb])
        for half in range(2):
            ps, rps = self.PA.get()
            psb = ps[:].bitcast(BF16)
            for c4 in range(4):
                c = half * 4 + c4
                S.op("pe", lambda e, c=c, c4=c4: e.transpose(psb[:, c4 * 128:(c4 + 1) * 128], xb[:, c * 128:(c + 1) * 128], self.cB(CB_ID)),
                     reads=[rxb, self.rCb], writes=[rps])
            for c4 in range(4):
                c = half * 4 + c4
                dst = self.XT[:, c, PADX + t * 128:PADX + (t + 1) * 128]
                src = psb[:, c4 * 128:(c4 + 1) * 128]
                if c4 % 2 == 0:
                    S.op("dve", lambda e, dst=dst, src=src: e.tensor_copy(dst, src), reads=[rps, self.rXT[t]], writes=[self.rXT[t]])
                else:
                    S.op("act", lambda e, dst=dst, src=src: e.activation(dst, src, AF.Copy), reads=[rps, self.rXT[t]], writes=[self.rXT[t]])
        if scale:
            S.op("act", lambda e: e.activation(self.X[:, t, :], self.X[:, t, :], AF.Identity, scale=ALPHA),
                 reads=[self.rX[t]] + list(extra_reads), writes=[self.rX[t]])

    def load_wt(self, l, ti):
        w, rw = self.Wt.get()
        self.S.dma("pool", w[:], self.D["w_in_t"][l, ti], writes=[rw])
        return w, rw

    def mark(self, name):
        if not hasattr(self, "marks"):
            self.marks = []
        self.marks.append((name, self.S.E["pe"].count))

    def layer(self, l):
        S, D = self.S, self.D
        self.mark("L%d start" % l)
        lam_init = 0.8 - 0.6 * math.exp(-0.3 * l)
        rLP = self.rLP
        S.dma("sp", self.cwg[:], D["gconv"][l], writes=[rLP])
        S.dma("sp", self.cwf[:], D["fconv"][l], writes=[rLP])
        S.dma("sp", self.alog[:], D["a_log"][l:l + 1, :].partition_broadcast(128), writes=[rLP])
        S.dma("sp", self.dtb[:], D["dt_bias"][l:l + 1, :].partition_broadcast(128), writes=[rLP])
        S.dma("sp", self.gnw[:], D["gnorm"][l:l + 1, :].partition_broadcast(128), writes=[rLP])
        S.dma("sp", self.lv[:], D["dlam"][l:l + 1, :].partition_broadcast(128), writes=[rLP])
        S.dma("sp", self.dnw[:], D["dnorm"][l], writes=[rLP])
        S.dma("pool", self.wbd[:], D["w_bd"][l], writes=[self.rwbd])
        self.lam_calc(lam_init)
        self.gdn_prep(l)
        self.mark("L%d prep done" % l)
        outs = []
        for h in range(4):
            outs.append(self.gdn_head(l, h))
            self.mark("L%d gdn head %d done" % (l, h))
            if self.dbg == "g0":
                return
            if h % 2 == 1:
                self.wout_partial(l, [h - 1, h], outs[-2:])
        if self.dbg == "gdn":
            return self.dump_X()
        self.mark("L%d gdn+wout done" % l)
        o = [self.attn_pair(l, p, diff=False) for p in range(2)]
        self.mark("L%d dsw done" % l)
        self.wout_partial(l, [4, 5], o)
        if self.dbg == "dsw":
            return self.dump_X()
        o = [self.attn_pair(l, p, diff=True) for p in range(2)]
        self.mark("L%d diff done" % l)
        self.wout_partial(l, [6, 7], o)
        self.mark("L%d wout done" % l)
        if self.dbg == "mix":
            return self.dump_X()
        self.layernorm(l, 1, last=False)
        self.mark("L%d ln1 done" % l)
        if self.dbg == "ln1":
            return self.dump_X()
        self.ffn(l)
        self.mark("L%d ffn done" % l)
        if self.dbg == "ffn":
            return self.dump_X()
        self.layernorm(l, 2, last=(l == self.nlayers - 1))

    def dump(self, idx, ap, reg, bf=False):
        n = 128 if ap.shape[-1] == 128 else 512
        dst = self.D["y"][(idx % 16) * 128:(idx % 16 + 1) * 128, 512 * (idx // 16):512 * (idx // 16) + n]
        self.S.dma("pool" if bf else "sp", dst, ap, reads=[reg], writes=[self.S.out_region])

    def dump_X(self):
        for t in range(NT):
            self.S.dma("sp", self.D["y"][t * 128:(t + 1) * 128, :], self.X[:, t, :], reads=[self.rX[t]], writes=[self.S.out_region])

    def lam_calc(self, lam_init):
        S = self.S
        lam, r = self.lam, self.rlam
        w, rw = self.W2.get()
        S.op("dve", lambda e: e.tensor_tensor(w[:, 0:32], self.lv[:, 0:32], self.lv[:, 32:64], ALU.mult), reads=[self.rLP], writes=[rw])
        S.op("dve", lambda e: e.tensor_tensor(w[:, 32:64], self.lv[:, 64:96], self.lv[:, 96:128], ALU.mult), reads=[self.rLP, rw], writes=[rw])
        S.op("dve", lambda e: e.tensor_reduce(lam[:, 0:2], w[:, 0:64].rearrange("p (a b) -> p a b", a=2), axis=AX.X, op=ALU.add), reads=[rw, r], writes=[r])
        S.op("act", lambda e: e.activation(lam[:, 2:4], lam[:, 0:2], AF.Exp), reads=[r], writes=[r])
        S.op("dve", lambda e: e.tensor_tensor(lam[:, 4:5], lam[:, 2:3], lam[:, 3:4], ALU.subtract), reads=[r], writes=[r])
        S.op("dve", lambda e: e.tensor_scalar(lam[:, 5:6], lam[:, 4:5], lam_init, -1.0, op0=ALU.add, op1=ALU.mult), reads=[r], writes=[r])
        S.op("dve", lambda e: e.tensor_scalar(lam[:, 6:7], self.dnw[:, 0:1], 1.0 - lam_init, None, op0=ALU.mult), reads=[r, self.rLP], writes=[r])
        S.op("act", lambda e: e.activation(self.aexp[:], self.alog[:], AF.Exp), reads=[self.rLP], writes=[self.raexp])

    def gdn_prep(self, l):
        S, SM, r = self.S, self.SM, self.rSM
        ps, rps = self.PA.get()
        for t in range(NT):
            cs = slice(PADX + t * 128, PADX + (t + 1) * 128)
            for c in range(8):
                S.op("pe", lambda e, c=c, cs=cs, t=t: e.matmul(ps[:, t * 8:(t + 1) * 8], lhsT=self.XT[:, c, cs], rhs=self.wbd[:, c, :],
                                                                 start=(c == 0), stop=(c == 7)),
                     reads=[self.rXT[t], self.rwbd], writes=[rps])
        psv = ps[:, 0:128].rearrange("p (t c) -> p t c", c=8)
        G, BE, GC, GL, EG, EGL, EKT, TM = [SM[:, k, :, :] for k in range(8)]
        dtb = self.dtb[:].unsqueeze(1).broadcast_to([128, NT, 4])
        aex = self.aexp[:].unsqueeze(1).broadcast_to([128, NT, 4])
        S.op("act", lambda e: e.activation(BE, psv[:, :, 0:4], AF.Exp, scale=-1.0), reads=[rps], writes=[r])
        S.op("dve", lambda e: e.tensor_scalar(BE, BE, 1.0, None, op0=ALU.add), reads=[r], writes=[r])
        S.op("dve", lambda e: e.reciprocal(BE, BE), reads=[r], writes=[r])
        S.op("dve", lambda e: e.tensor_tensor(TM, psv[:, :, 4:8], dtb, ALU.add), reads=[rps, self.rLP, r], writes=[r])
        S.op("act", lambda e: e.activation(TM, TM, AF.Exp), reads=[r], writes=[r])
        S.op("dve", lambda e: e.tensor_scalar(GC, TM, 2.0, None, op0=ALU.add), reads=[r], writes=[r])
        S.op("dve", lambda e: e.reciprocal(GC, GC), reads=[r], writes=[r])
        S.op("dve", lambda e: e.tensor_tensor(GC, GC, TM, ALU.mult), reads=[r], writes=[r])
        S.op("dve", lambda e: e.tensor_tensor(GL, GC, GC, ALU.mult), reads=[r], writes=[r])
        S.op("dve", lambda e: e.tensor_scalar(TM, GL, 1.0 / 11.0, None, op0=ALU.mult), reads=[r], writes=[r])
        for cst in (1.0 / 9.0, 1.0 / 7.0, 1.0 / 5.0, 1.0 / 3.0):
            S.op("dve", lambda e, cst=cst: e.scalar_tensor_tensor(TM, TM, cst, GL, op0=ALU.add, op1=ALU.mult), reads=[r], writes=[r])
        S.op("dve", lambda e: e.scalar_tensor_tensor(TM, TM, 1.0, GC, op0=ALU.add, op1=ALU.mult), reads=[r], writes=[r])
        S.op("dve", lambda e: e.scalar_tensor_tensor(G, TM, -2.0, aex, op0=ALU.mult, op1=ALU.mult), reads=[r, self.raexp], writes=[r])
        ps2, rps2 = self.PA.get()
        for t in range(NT):
            S.op("pe", lambda e, t=t: e.matmul(ps2[:, t * 4:(t + 1) * 4], lhsT=self.cF(CF_UTRI), rhs=SM[:, 0, t, :], start=True, stop=True),
                 reads=[r, self.rC], writes=[rps2])
            S.op("pe", lambda e, t=t: e.matmul(ps2[:, 64 + t * 4:64 + (t + 1) * 4], lhsT=self.cF(CF_ONES), rhs=SM[:, 0, t, :], start=True, stop=True),
                 reads=[r, self.rC], writes=[rps2])
        S.op("act", lambda e: e.activation(SM[:, 2:4, :, :].rearrange("p a t c -> p (a t c)"), ps2[:, 0:128], AF.Copy), reads=[rps2], writes=[r])
        S.op("act", lambda e: e.activation(SM[:, 4:6, :, :].rearrange("p a t c -> p (a t c)"),
                                           SM[:, 2:4, :, :].rearrange("p a t c -> p (a t c)"), AF.Exp), reads=[r], writes=[r])
        S.op("dve", lambda e: e.tensor_tensor(TM, GL, GC, ALU.subtract), reads=[r], writes=[r])
        S.op("act", lambda e: e.activation(EKT, TM, AF.Exp), reads=[r], writes=[r])

    def gdn_head(self, l, h):
        S = self.S
        Wq = self.load_wt(l, h)
        Wk = self.load_wt(l, 4 + h)
        Wv = self.load_wt(l, 8 + h)
        Wz = self.load_wt(l, 12 + h)
        out, rout = self.A4.get()
        S.op("pool", lambda e: e.memset(self.Sf[:], 0.0), writes=[self.rSf])
        sb0 = self.Sb.get()
        S.op("pool", lambda e: e.memset(sb0[0][:], 0.0), writes=[sb0[1]])
        self.curSb = sb0
        for g in range(4):
            self.gdn_group(l, h, g, Wq, Wk, Wv, Wz, out, rout)
            if self.dbg == "g0":
                gq = self.gq
                self.dump(0, self.SM[:].rearrange("p a t c -> p (a t c)"), self.rSM)
                for i, n in enumerate("qT kT vT vtok DU EGr Np Mp X4 Y4 kg kt wTn qkT qdec vn".split()):
                    self.dump(1 + i, gq[n][0][:], gq[n][1], bf=True)
                self.dump(17, out[:, 0:512], rout, bf=True)
                self.dump(18, self.Sf[:], self.rSf)
                self.dump(19, self.gacc["q"][0][:], self.gacc["q"][1])
                self.dump(20, self.X[:, 0, 0:512], self.rX[0])
                self.dump(21, self.XT[:, 0, PADX:PADX + 512], self.rXT[0], bf=True)
                return None
        return out, rout

    def gdn_group(self, l, h, g, Wq, Wk, Wv, Wz, out, rout):
        S, SM, gq = self.S, self.SM, self.gq
        rSM, rC, rCb, rLP = self.rSM, self.rC, self.rCb, self.rLP
        sl = slice(PADX + 512 * g, PADX + 512 * (g + 1))
        rxt = self.rXT[4 * g:4 * g + 4]
        t0 = 4 * g
        for wi, (nm, W, ci) in enumerate((("q", Wq, h), ("k", Wk, 4 + h), ("v", Wv, 8 + h))):
            ps, rps = self.PA.get()
            for c in range(8):
                S.op("pe", lambda e, c=c, W=W, ps=ps: e.matmul(ps[:], lhsT=W[0][:, c, :], rhs=self.XT[:, c, sl], start=(c == 0), stop=(c == 7)),
                     reads=[W[1]] + rxt, writes=[rps])
            st, rst = self.W2.get()
            S.op("act", lambda e, st=st, ps=ps: e.activation(st[:, 3:515], ps[:], AF.Copy), reads=[rps], writes=[rst])
            if g == 0:
                S.op("pool", lambda e, st=st: e.memset(st[:, 0:3], 0.0), reads=[rst], writes=[rst])
            else:
                S.op("dve", lambda e, st=st, wi=wi: e.tensor_copy(st[:, 0:3], self.HALO[:, wi, 0:3]), reads=[self.rHALO[wi], rst], writes=[rst])
            S.op("act", lambda e, ps=ps, wi=wi: e.activation(self.HALO[:, wi, 0:3], ps[:, 509:512], AF.Copy), reads=[rps], writes=[self.rHALO[wi]])
            acc, racc = self.gacc[nm]
            S.op("dve", lambda e, acc=acc, st=st, ci=ci: e.tensor_scalar(acc[:], st[:, 3:515], self.cwg[:, ci, 3:4], None, op0=ALU.mult),
                 reads=[rst, rLP], writes=[racc])
            for j in (2, 1, 0):
                S.op("dve", lambda e, acc=acc, st=st, ci=ci, j=j: e.scalar_tensor_tensor(acc[:], st[:, j:j + 512], self.cwg[:, ci, j:j + 1], acc[:],
                                                                                         op0=ALU.mult, op1=ALU.add),
                     reads=[rst, racc, rLP], writes=[racc])
        aq, raq = self.gacc["q"]
        ak, rak = self.gacc["k"]
        av, rav = self.gacc["v"]
        S.op("act", lambda e: e.activation(aq[:], aq[:], AF.Silu), reads=[raq], writes=[raq])
        S.op("act", lambda e: e.activation(ak[:], ak[:], AF.Silu), reads=[rak], writes=[rak])
        S.op("act", lambda e: e.activation(gq["vT"][0][:], av[:], AF.Silu), reads=[rav], writes=[gq["vT"][1]])
        for nm, acc, racc in (("q", aq, raq), ("k", ak, rak)):
            sq, rsq = self.B1.get()
            S.op("act", lambda e, sq=sq, acc=acc: e.activation(sq[:], acc[:], AF.Square), reads=[racc], writes=[rsq])
            ps, rps = self.PA.get()
            S.op("pe", lambda e, ps=ps, sq=sq: e.matmul(ps[:], lhsT=self.cB(CB_ONES), rhs=sq[:], start=True, stop=True), reads=[rsq, rCb], writes=[rps])
            S.op("act", lambda e, ps=ps: e.activation(ps[:], ps[:], AF.Ln, bias=self.eps5[:, 1:2]), reads=[rps, self.reps], writes=[rps])
            S.op("act", lambda e, ps=ps: e.activation(ps[:], ps[:], AF.Exp, scale=-0.5), reads=[rps], writes=[rps])
            if nm == "q":
                S.op("dve", lambda e, ps=ps, acc=acc: e.scalar_tensor_tensor(gq["qT"][0][:], acc[:], 128.0 ** -0.5, ps[:], op0=ALU.mult, op1=ALU.mult),
                     reads=[racc, rps], writes=[gq["qT"][1]])
            else:
                S.op("dve", lambda e, ps=ps, acc=acc: e.tensor_tensor(gq["kT"][0][:], acc[:], ps[:], ALU.mult), reads=[racc, rps], writes=[gq["kT"][1]])
        qT, rqT = gq["qT"]
        kT, rkT = gq["kT"]
        vT, rvT = gq["vT"]
        bc = lambda k: SM[:, k, t0:t0 + 4, h:h + 1].broadcast_to([128, 4, 128])
        ps, rps = self.PA.get()
        psb = ps[:].bitcast(BF16)
        for j in range(4):
            S.op("pe", lambda e, j=j, psb=psb: e.transpose(psb[:, j * 128:(j + 1) * 128], kT[:, j * 128:(j + 1) * 128], self.cB(CB_ID)),
                 reads=[rkT, rCb], writes=[rps])
        S.op("dve", lambda e, psb=psb: e.tensor_tensor(q4(gq["kg"][0][:]), q4(psb[:, 0:512]), bc(4), ALU.mult), reads=[rps, rSM], writes=[gq["kg"][1]])
        S.op("dve", lambda e, psb=psb: e.tensor_tensor(q4(gq["kt"][0][:]), q4(psb[:, 0:512]), bc(6), ALU.mult), reads=[rps, rSM], writes=[gq["kt"][1]])
        ps, rps = self.PA.get()
        psb2 = ps[:].bitcast(BF16)
        for j in range(4):
            S.op("pe", lambda e, j=j, psb2=psb2: e.transpose(psb2[:, j * 128:(j + 1) * 128], vT[:, j * 128:(j + 1) * 128], self.cB(CB_ID)),
                 reads=[rvT, rCb], writes=[rps])
        S.op("act", lambda e, psb2=psb2: e.activation(gq["vtok"][0][:], psb2[:, 0:512], AF.Copy), reads=[rps], writes=[gq["vtok"][1]])
        gcb, rgcb = self.W2.get()
        gcb4 = q4(gcb[:, 0:512])
        S.op("dve", lambda e: e.tensor_tensor(gcb4, q4(self.cF(CF_ONES, 512)), bc(2), ALU.mult), reads=[rC, rSM], writes=[rgcb])
        ps, rps = self.PA.get()
        for j in range(4):
            S.op("pe", lambda e, j=j, ps=ps: e.matmul(ps[:, j * 128:(j + 1) * 128], lhsT=gcb[:, j * 128:(j + 1) * 128], rhs=self.cF(CF_ID), start=True, stop=True),
                 reads=[rgcb, rC], writes=[rps])
        S.op("act", lambda e, ps=ps: e.activation(gq["EGr"][0][:], ps[:], AF.Exp), reads=[rps], writes=[gq["EGr"][1]])
        ps, rps = self.PA.get()
        for j in range(4):
            o = ps[:, j * 128:(j + 1) * 128]
            S.op("pe", lambda e, j=j, o=o: e.matmul(o, lhsT=gcb[:, j * 128:(j + 1) * 128], rhs=self.cF(CF_ID), start=True, stop=False), reads=[rgcb, rC], writes=[rps])
            S.op("pe", lambda e, j=j, o=o: e.matmul(o, lhsT=self.cF(CF_NID), rhs=gcb[:, j * 128:(j + 1) * 128], start=False, stop=False), reads=[rgcb, rC], writes=[rps])
            S.op("pe", lambda e, j=j, o=o: e.matmul(o, lhsT=self.cB(CB_ID), rhs=self.cB(CB_NEGL), start=False, stop=True), reads=[rCb], writes=[rps])
        DU, rDU = gq["DU"]
        S.op("act", lambda e, ps=ps: e.activation(DU[:], ps[:], AF.Exp), reads=[rps], writes=[rDU])
        Np, rNp = gq["Np"]
        Mp, rMp = gq["Mp"]
        ps, rps = self.PA.get()
        for j in range(4):
            S.op("pe", lambda e, j=j, ps=ps: e.matmul(ps[:, j * 128:(j + 1) * 128], lhsT=kT[:, j * 128:(j + 1) * 128], rhs=kT[:, j * 128:(j + 1) * 128], start=True, stop=True),
                 reads=[rkT], writes=[rps])
        S.op("dve", lambda e, ps=ps: e.tensor_tensor(Np[:], ps[:], DU[:], ALU.mult), reads=[rps, rDU], writes=[rNp])
        S.op("dve", lambda e: e.tensor_tensor(q4(Np[:]), q4(Np[:]), bc(1), ALU.mult), reads=[rNp, rSM], writes=[rNp])
        ps, rps = self.PA.get()
        psb3 = ps[:].bitcast(BF16)
        for j in range(4):
            S.op("pe", lambda e, j=j, psb3=psb3: e.transpose(psb3[:, j * 128:(j + 1) * 128], Np[:, j * 128:(j + 1) * 128], self.cB(CB_ID)),
                 reads=[rNp, rCb], writes=[rps])
        S.op("act", lambda e, psb3=psb3: e.activation(Mp[:], psb3[:, 0:512], AF.Copy), reads=[rps], writes=[rMp])
        kg, rkg = gq["kg"]
        kt, rkt = gq["kt"]
        wTn, rwTn = gq["wTn"]
        qkT, rqkT = gq["qkT"]
        qdec, rqdec = gq["qdec"]
        vtok, rvtok = gq["vtok"]
        vn, rvn = gq["vn"]
        ps, rps = self.PA.get()
        for j in range(4):
            S.op("pe", lambda e, j=j, ps=ps: e.matmul(ps[:, j * 128:(j + 1) * 128], lhsT=kT[:, j * 128:(j + 1) * 128], rhs=qT[:, j * 128:(j + 1) * 128], start=True, stop=True),
                 reads=[rkT, rqT], writes=[rps])
        S.op("dve", lambda e, ps=ps: e.tensor_tensor(qkT[:], ps[:], DU[:], ALU.mult), reads=[rps, rDU], writes=[rqkT])
        S.op("dve", lambda e: e.tensor_tensor(qdec[:], qT[:], gq["EGr"][0][:], ALU.mult), reads=[rqT, gq["EGr"][1]], writes=[rqdec])
        pz, rpz = self.PA.get()
        for j in range(4):
            cs = slice(PADX + (t0 + j) * 128, PADX + (t0 + j + 1) * 128)
            for c in range(8):
                S.op("pe", lambda e, j=j, c=c, cs=cs, pz=pz: e.matmul(pz[:, j * 128:(j + 1) * 128], lhsT=self.XT[:, c, cs], rhs=Wz[0][:, c, :], start=(c == 0), stop=(c == 7)),
                     reads=[Wz[1], self.rXT[t0 + j]], writes=[rpz])
        sz, rsz = self.W2.get()
        S.op("act", lambda e, pz=pz: e.activation(sz[:, 0:512], pz[:], AF.Silu), reads=[rpz], writes=[rsz])
        gnb = self.gnw[:].unsqueeze(1).broadcast_to([128, 4, 128])
        S.op("pool", lambda e: e.tensor_tensor(q4(sz[:, 0:512]), q4(sz[:, 0:512]), gnb, ALU.mult), reads=[rsz, rLP], writes=[rsz])
        X4, rX4 = gq["X4"]
        Y4, rY4 = gq["Y4"]
        idb = self.cB(CB_ID).unsqueeze(1).broadcast_to([128, 4, 128])
        S.op("pool", lambda e: e.tensor_copy(q4(X4[:]), idb), reads=[rCb], writes=[rX4])
        S.op("pool", lambda e: e.tensor_copy(q4(Y4[:]), idb), reads=[rCb], writes=[rY4])
        for L in range(NLEV):
            last = (L == NLEV - 1)
            p1, rp1 = self.PA.get()
            for j in range(4):
                S.op("pe", lambda e, j=j, p1=p1: e.matmul(p1[:, j * 128:(j + 1) * 128], lhsT=Mp[:, j * 128:(j + 1) * 128], rhs=X4[:, j * 128:(j + 1) * 128], start=True, stop=True),
                     reads=[rMp, rX4], writes=[rp1])
            t1, rt1 = self.TN.get()
            S.op("act", lambda e, t1=t1, p1=p1: e.activation(t1[:], p1[:], AF.Identity, scale=-1.0), reads=[rp1], writes=[rt1])
            if not last:
                p2, rp2 = self.PA.get()
                for j in range(4):
                    S.op("pe", lambda e, j=j, p2=p2: e.matmul(p2[:, j * 128:(j + 1) * 128], lhsT=Np[:, j * 128:(j + 1) * 128], rhs=Y4[:, j * 128:(j + 1) * 128], start=True, stop=True),
                         reads=[rNp, rY4], writes=[rp2])
                t2, rt2 = self.TN.get()
                S.op("act", lambda e, t2=t2, p2=p2: e.activation(t2[:], p2[:], AF.Identity, scale=-1.0), reads=[rp2], writes=[rt2])
            p3, rp3 = self.PA.get()
            for j in range(4):
                S.op("pe", lambda e, j=j, p3=p3, t1=t1: e.matmul(p3[:, j * 128:(j + 1) * 128], lhsT=Y4[:, j * 128:(j + 1) * 128], rhs=t1[:, j * 128:(j + 1) * 128], start=True, stop=True),
                     reads=[rY4, rt1], writes=[rp3])
            if not last:
                p4, rp4 = self.PA.get()
                for j in range(4):
                    S.op("pe", lambda e, j=j, p4=p4, t2=t2: e.matmul(p4[:, j * 128:(j + 1) * 128], lhsT=X4[:, j * 128:(j + 1) * 128], rhs=t2[:, j * 128:(j + 1) * 128], start=True, stop=True),
                         reads=[rX4, rt2], writes=[rp4])
            om = self.cB(CB_OM + 128 * L).bitcast(U16).unsqueeze(1).broadcast_to([128, 4, 128])
            S.op("dve", lambda e, om=om, p3=p3: e.copy_predicated(q4(X4[:]), om, q4(p3[:])), reads=[rp3, rCb, rX4], writes=[rX4])
            if not last:
                omt = self.cB(CB_OMT + 128 * L).bitcast(U16).unsqueeze(1).broadcast_to([128, 4, 128])
                S.op("dve", lambda e, omt=omt, p4=p4: e.copy_predicated(q4(Y4[:]), omt, q4(p4[:])), reads=[rp4, rCb, rY4], writes=[rY4])
        ps, rps = self.PA.get()
        for j in range(4):
            S.op("pe", lambda e, j=j, ps=ps: e.matmul(ps[:, j * 128:(j + 1) * 128], lhsT=kg[:, j * 128:(j + 1) * 128], rhs=X4[:, j * 128:(j + 1) * 128], start=True, stop=True),
                 reads=[rkg, rX4], writes=[rps])
        S.op("act", lambda e, ps=ps: e.activation(wTn[:], ps[:], AF.Identity, scale=-1.0), reads=[rps], writes=[rwTn])
        po, rpo = self.PB.get()
        for j in range(4):
            t = t0 + j
            js = slice(j * 128, (j + 1) * 128)
            sb, rsb = self.curSb
            pv, rpv = self.PA.get()
            S.op("pe", lambda e, js=js, pv=pv: e.matmul(pv[:, 0:128], lhsT=X4[:, js], rhs=vtok[:, js], start=True, stop=False), reads=[rX4, rvtok], writes=[rpv])
            S.op("pe", lambda e, js=js, pv=pv, sb=sb: e.matmul(pv[:, 0:128], lhsT=wTn[:, js], rhs=sb[:], start=False, stop=True), reads=[rwTn, rsb], writes=[rpv])
            S.op("act", lambda e, js=js, pv=pv, t=t: e.activation(vn[:, js], pv[:, 0:128], AF.Identity, scale=SM[:, 1, t, h:h + 1]), reads=[rpv, rSM, rvn], writes=[rvn])
            S.op("pe", lambda e, js=js, sb=sb: e.matmul(po[:, js], lhsT=qdec[:, js], rhs=sb[:], start=True, stop=False), reads=[rqdec, rsb], writes=[rpo])
            S.op("pe", lambda e, js=js: e.matmul(po[:, js], lhsT=qkT[:, js], rhs=vn[:, js], start=False, stop=True), reads=[rqkT, rvn], writes=[rpo])
            pS, rpS = self.PA.get()
            S.op("pe", lambda e, js=js, pS=pS: e.matmul(pS[:, 0:128], lhsT=kt[:, js], rhs=vn[:, js], start=True, stop=True), reads=[rkt, rvn], writes=[rpS])
            S.op("dve", lambda e, pS=pS, t=t: e.scalar_tensor_tensor(self.Sf[:], self.Sf[:], SM[:, 5, t, h:h + 1], pS[:, 0:128], op0=ALU.mult, op1=ALU.add),
                 reads=[rpS, rSM, self.rSf], writes=[self.rSf])
            nsb = self.Sb.get()
            S.op("act", lambda e, nsb=nsb: e.activation(nsb[0][:], self.Sf[:], AF.Copy), reads=[self.rSf], writes=[nsb[1]])
            self.curSb = nsb
        osq, rosq = self.B1.get()
        S.op("act", lambda e: e.activation(osq[:], po[:], AF.Square), reads=[rpo], writes=[rosq])
        S.op("dve", lambda e: e.tensor_reduce(self.ssq[:, 0:4], q4(osq[:]), axis=AX.X, op=ALU.add), reads=[rosq, self.rssq], writes=[self.rssq])
        S.op("act", lambda e: e.activation(self.ssq[:, 0:4], self.ssq[:, 0:4], AF.Ln, scale=1.0 / 128.0, bias=self.eps5[:, 0:1]), reads=[self.rssq, self.reps], writes=[self.rssq])
        S.op("act", lambda e: e.activation(self.ssq[:, 4:8], self.ssq[:, 0:4], AF.Exp, scale=-0.5), reads=[self.rssq], writes=[self.rssq])
        yt, ryt = self.W2.get()
        S.op("dve", lambda e: e.tensor_tensor(yt[:, 0:512], po[:], sz[:, 0:512], ALU.mult), reads=[rpo, rsz], writes=[ryt])
        yb, ryb = self.B1.get()
        S.op("dve", lambda e: e.tensor_tensor(q4(yb[:]), q4(yt[:, 0:512]), self.ssq[:, 4:8].unsqueeze(2).broadcast_to([128, 4, 128]), ALU.mult),
             reads=[ryt, self.rssq], writes=[ryb])
        ps, rps = self.PA.get()
        psb4 = ps[:].bitcast(BF16)
        for j in range(4):
            S.op("pe", lambda e, j=j, psb4=psb4: e.transpose(psb4[:, j * 128:(j + 1) * 128], yb[:, j * 128:(j + 1) * 128], self.cB(CB_ID)), reads=[ryb, rCb], writes=[rps])
        S.op("act", lambda e, psb4=psb4: e.activation(out[:, 512 * g:512 * (g + 1)], psb4[:, 0:512], AF.Copy), reads=[rps], writes=[rout])

    def wout_partial(self, l, chunks, outs):
        S = self.S
        ws = []
        for c in chunks:
            w, rw = self.Wrow.get()
            S.dma("pool", w[:], self.D["w_out"][l, c * 128:(c + 1) * 128, :], writes=[rw])
            ws.append((w, rw))
        n = len(chunks)
        for t in range(NT):
            for half in range(2):
                ps, rps = self.PA.get()
                for i in range(n):
                    S.op("pe", lambda e, i=i, ps=ps, t=t, half=half: e.matmul(ps[:], lhsT=outs[i][0][:, t * 128:(t + 1) * 128], rhs=ws[i][0][:, half * 512:(half + 1) * 512],
                                                                                   start=(i == 0), stop=(i == n - 1)),
                         reads=[outs[i][1], ws[i][1]], writes=[rps])
                xs = self.X[:, t, half * 512:(half + 1) * 512]
                S.op("dve", lambda e, xs=xs, ps=ps: e.tensor_tensor(xs, xs, ps[:], ALU.add), reads=[rps, self.rX[t]], writes=[self.rX[t]])

    def attn_pair(self, l, p, diff):
        S = self.S
        rCb, rC = self.rCb, self.rC
        base = 22 if diff else 16
        Wq = self.load_wt(l, base + p)
        Wk = self.load_wt(l, base + 2 + p)
        Wv = self.load_wt(l, base + 4 + p)
        qT, rqT = self.A4.get()
        kT, rkT = self.A4.get()
        out, rout = self.A4.get()
        vt, rvt = self.VT.get()
        S.op("pool", lambda e: e.memset(vt[:, :, 64:128], 0.0), writes=[rvt])
        S.op("pool", lambda e: e.memset(vt[:, :, 64:65], 1.0), reads=[rvt], writes=[rvt])
        for (W, dst, rdst) in ((Wq, qT, rqT), (Wk, kT, rkT)):
            for g in range(4):
                sl = slice(PADX + 512 * g, PADX + 512 * (g + 1))
                ps, rps = self.PA.get()
                for c in range(8):
                    S.op("pe", lambda e, c=c, W=W, ps=ps, sl=sl: e.matmul(ps[:], lhsT=W[0][:, c, :], rhs=self.XT[:, c, sl], start=(c == 0), stop=(c == 7)),
                         reads=[W[1]] + self.rXT[4 * g:4 * g + 4], writes=[rps])
                S.op("act", lambda e, ps=ps, dst=dst, g=g: e.activation(dst[:, 512 * g:512 * (g + 1)], ps[:], AF.Copy), reads=[rps], writes=[rdst])
        for t in range(NT):
            cs = slice(PADX + t * 128, PADX + (t + 1) * 128)
            ps, rps = self.PA.get()
            for c in range(8):
                S.op("pe", lambda e, c=c, ps=ps, cs=cs: e.matmul(ps[:, 0:128], lhsT=self.XT[:, c, cs], rhs=Wv[0][:, c, :], start=(c == 0), stop=(c == 7)),
                     reads=[Wv[1], self.rXT[t]], writes=[rps])
            S.op("dve", lambda e, ps=ps, t=t: e.tensor_copy(vt[:, t, 0:64], ps[:, 0:64]), reads=[rps, rvt], writes=[rvt])
            S.op("act", lambda e, ps=ps, t=t: e.activation(vt[:, t, 128:192], ps[:, 64:128], AF.Copy), reads=[rps, rvt], writes=[rvt])
        scale = (32.0 if diff else 64.0) ** -0.5
        nmaps = 2 if diff else 1
        for hh in range(2):
            rows = slice(0, 64) if hh == 0 else slice(64, 128)
            rowp = 64 if hh == 0 else 0
            for g in range(4):
                nm = 4 * g + 4
                qs = slice(512 * g, 512 * (g + 1))
                parts = []
                for mm in range(nmaps):
                    acc, racc = self.PB.get()
                    if diff:
                        kb = hh * 64 + mm * 32
                        kr = slice(kb, kb + 32)
                    else:
                        kb = hh * 64
                        kr = slice(kb, kb + 64)

                    def st_issue(m):
                        ks = slice(128 * m, 128 * (m + 1))
                        stp, rstp = self.PA.get()
                        if kb == 96:
                            S.op("pe", lambda e: e.matmul(stp[:], lhsT=kT[kr, ks], rhs=qT[kr, qs], start=True, stop=True, tile_position=(96, 0)),
                                 reads=[rkT, rqT], writes=[rstp])
                        else:
                            S.op("pe", lambda e: e.matmul(stp[:], lhsT=kT[kr, ks], rhs=qT[kr, qs], start=True, stop=True),
                                 reads=[rkT, rqT], writes=[rstp])
                        return stp, rstp

                    def rest(m, stp, rstp):
                        P, rP = self.B1.get()
                        S.op("act", lambda e: e.activation(P[:], stp[:], AF.Exp, scale=scale), reads=[rstp], writes=[rP])
                        if not diff:
                            off = CB_WB + (min(4 * g - m, 5) + 3) * 128
                            S.op("dve", lambda e: e.tensor_tensor(P[:], P[:], self.cb[:, off:off + 512], ALU.mult), reads=[rP, rCb], writes=[rP])
                        elif m >= 4 * g:
                            off = CB_CB + (3 - (m - 4 * g)) * 128
                            S.op("dve", lambda e: e.tensor_tensor(P[:], P[:], self.cb[:, off:off + 512], ALU.mult), reads=[rP, rCb], writes=[rP])
                        if hh == 0:
                            S.op("pe", lambda e: e.matmul(acc[0:65, :], lhsT=vt[:, m, 0:65], rhs=P[:], start=(m == 0), stop=(m == nm - 1)),
                                 reads=[rvt, rP], writes=[racc])
                        else:
                            S.op("pe", lambda e: e.matmul(acc[:, :], lhsT=vt[:, m, 64:192], rhs=P[:], start=(m == 0), stop=(m == nm - 1)),
                                 reads=[rvt, rP], writes=[racc])

                    cur = st_issue(0)
                    for m in range(nm):
                        nxt = st_issue(m + 1) if m + 1 < nm else None
                        rest(m, *cur)
                        cur = nxt
                    S.op("act", lambda e: e.activation(self.rrow[rowp:rowp + 1, :], acc[rowp:rowp + 1, :], AF.Ln), reads=[racc, self.rrrow], writes=[self.rrrow])
                    S.op("act", lambda e: e.activation(self.rrow[rowp:rowp + 1, :], self.rrow[rowp:rowp + 1, :], AF.Exp, scale=-1.0), reads=[self.rrrow], writes=[self.rrrow])
                    pbc, rpbc = self.PA.get()
                    if hh == 0:
                        S.op("pe", lambda e: e.matmul(pbc[0:64, :], lhsT=self.cf[64:96, CF_SEL:CF_SEL + 64], rhs=self.rrow[64:96, :], start=True, stop=True),
                             reads=[self.rrrow, rC], writes=[rpbc])
                    else:
                        S.op("pe", lambda e: e.matmul(pbc[:, :], lhsT=self.cf[0:32, CF_SEL:CF_SEL + 128], rhs=self.rrow[0:32, :], start=True, stop=True),
                             reads=[self.rrrow, rC], writes=[rpbc])
                    b_, rb_ = self.W2.get()
                    if mm == 0:
                        S.op("act", lambda e: e.activation(b_[rows, 0:512], pbc[rows, :], AF.Copy), reads=[rpbc], writes=[rb_])
                    else:
                        S.op("act", lambda e: e.activation(b_[rows, 0:512], pbc[rows, :], AF.Identity, scale=self.lam[rows, 5:6]),
                             reads=[rpbc, self.rlam], writes=[rb_])
                    if not diff:
                        S.op("dve", lambda e: e.tensor_tensor(out[rows, qs], acc[rows, :], b_[rows, 0:512], ALU.mult), reads=[racc, rb_], writes=[rout])
                    else:
                        S.op("dve", lambda e: e.tensor_tensor(b_[rows, 0:512], acc[rows, :], b_[rows, 0:512], ALU.mult), reads=[racc, rb_], writes=[rb_])
                    parts.append((b_, rb_))
                if diff:
                    (b1, rb1), (b2, rb2) = parts
                    S.op("pool", lambda e: e.tensor_tensor(b1[rows, 0:512], b1[rows, 0:512], b2[rows, 0:512], ALU.add), reads=[rb1, rb2], writes=[rb1])
                    osq, rosq = self.B1.get()
                    S.op("act", lambda e: e.activation(osq[rows, :], b1[rows, 0:512], AF.Square), reads=[rb1], writes=[rosq])
                    pss, rpss = self.PA.get()
                    if hh == 0:
                        S.op("pe", lambda e: e.matmul(pss[0:64, :], lhsT=self.cb[0:64, CB_ONES:CB_ONES + 64], rhs=osq[0:64, :], start=True, stop=True), reads=[rosq, rCb], writes=[rpss])
                    else:
                        S.op("pe", lambda e: e.matmul(pss[:, :], lhsT=self.cb[64:128, CB_ONES:CB_ONES + 128], rhs=osq[64:128, :], start=True, stop=True), reads=[rosq, rCb], writes=[rpss])
                    S.op("act", lambda e: e.activation(pss[rows, :], pss[rows, :], AF.Ln, scale=1.0 / 64.0, bias=self.eps5[rows, 0:1]), reads=[rpss, self.reps], writes=[rpss])
                    S.op("act", lambda e: e.activation(pss[rows, :], pss[rows, :], AF.Exp, scale=-0.5), reads=[rpss], writes=[rpss])
                    S.op("dve", lambda e: e.scalar_tensor_tensor(out[rows, qs], b1[rows, 0:512], self.lam[rows, 6:7], pss[rows, :], op0=ALU.mult, op1=ALU.mult),
                         reads=[rb1, rpss, self.rlam], writes=[rout])
        return out, rout

    def layernorm(self, l, which, last):
        S, D = self.S, self.D
        g_, rg = self.A4.get()
        b_, rb = self.A4.get()
        gv = g_[:].bitcast(F32)
        bv = b_[:].bitcast(F32)
        S.dma("sp", gv, D["ln%d_g" % which][l:l + 1, :].partition_broadcast(128), writes=[rg])
        S.dma("sp", bv, D["ln%d_b" % which][l:l + 1, :].partition_broadcast(128), writes=[rb])
        for t in range(NT):
            xt = self.X[:, t, :]
            rx = self.rX[t]
            st, rst = self.LNS.get()
            S.op("dve", lambda e, st=st, xt=xt: e.bn_stats(st[:, 0:6], xt[:, 0:512]), reads=[rx], writes=[rst])
            S.op("dve", lambda e, st=st, xt=xt: e.bn_stats(st[:, 6:12], xt[:, 512:1024]), reads=[rx, rst], writes=[rst])
            S.op("dve", lambda e, st=st: e.bn_aggr(st[:, 12:14], st[:, 0:12]), reads=[rst], writes=[rst])
            S.op("act", lambda e, st=st: e.activation(st[:, 14:15], st[:, 13:14], AF.Ln, bias=self.eps5[:, 0:1]), reads=[rst, self.reps], writes=[rst])
            S.op("act", lambda e, st=st: e.activation(st[:, 14:15], st[:, 14:15], AF.Exp, scale=-0.5), reads=[rst], writes=[rst])
            S.op("dve", lambda e, st=st: e.scalar_tensor_tensor(st[:, 15:16], st[:, 12:13], -1.0, st[:, 14:15], op0=ALU.mult, op1=ALU.mult), reads=[rst], writes=[rst])
            S.op("act", lambda e, st=st, xt=xt: e.activation(xt, xt, AF.Identity, scale=st[:, 14:15], bias=st[:, 15:16]), reads=[rst, rx], writes=[rx])
            S.op("dve", lambda e, xt=xt: e.tensor_tensor(xt, xt, gv, ALU.mult), reads=[rx, rg], writes=[rx])
            S.op("pool", lambda e, xt=xt: e.tensor_tensor(xt, xt, bv, ALU.add), reads=[rx, rb], writes=[rx])
            if last:
                S.dma("sp", D["y"][t * 128:(t + 1) * 128, :], xt, reads=[rx], writes=[S.out_region])
            else:
                self.x_tile_out(t, scale=True)

    def ffn(self, l):
        S, D = self.S, self.D
        rLP = self.rLP
        wins = []
        s = 0
        while s < T:
            n = min(510, T - s)
            wins.append((s, n))
            s += n
        for g0 in range(0, NFT, 4):
            grp = list(range(g0, min(g0 + 4, NFT)))
            hts = []
            for i in grp:
                cw = 128 if i < NFT - 1 else 64
                wu, rwu = self.Wup.get()
                S.dma("pool", wu[:], D["w_up_t"][l, i], writes=[rwu])
                ht, rht = self.A4.get()
                for (s, n) in wins:
                    w0 = PADX + s - 2
                    wd = n + 2
                    tiles = list(range(max(s - 2, 0) // 128, (s + n - 1) // 128 + 1))
                    rx = [self.rXT[t] for t in tiles] + ([self.rXTpad] if s == 0 else [])
                    pg, rpg = self.PA.get()
                    pv, rpv = self.PA.get()
                    for (pp, rpp, co) in ((pg, rpg, 0), (pv, rpv, 128)):
                        for c in range(8):
                            S.op("pe", lambda e, c=c, pp=pp, co=co, cw=cw, wu=wu, w0=w0, wd=wd: e.matmul(pp[0:cw, 0:wd], lhsT=wu[:, c, co:co + cw], rhs=self.XT[:, c, w0:w0 + wd],
                                                                                                       start=(c == 0), stop=(c == 7)),
                                 reads=[rwu] + rx, writes=[rpp])
                    ag, rag = self.W2.get()
                    av, rav = self.W2.get()
                    for (pp, rpp, aa, raa, gv) in ((pg, rpg, ag, rag, 0), (pv, rpv, av, rav, 1)):
                        S.op("act", lambda e, pp=pp, aa=aa, gv=gv, cw=cw, n=n, i=i: e.activation(aa[0:cw, 0:n], pp[0:cw, 2:2 + n], AF.Identity, scale=self.cwf[0:cw, i, gv, 2:3]),
                             reads=[rpp, rLP], writes=[raa])
                        for j in (1, 0):
                            S.op("dve", lambda e, pp=pp, aa=aa, gv=gv, cw=cw, n=n, i=i, j=j: e.scalar_tensor_tensor(aa[0:cw, 0:n], pp[0:cw, j:j + n], self.cwf[0:cw, i, gv, j:j + 1], aa[0:cw, 0:n],
                                                                                                                    op0=ALU.mult, op1=ALU.add),
                                 reads=[rpp, raa, rLP], writes=[raa])
                    S.op("act", lambda e, ag=ag, cw=cw, n=n: e.activation(ag[0:cw, 0:n], ag[0:cw, 0:n], AF.Silu), reads=[rag], writes=[rag])
                    S.op("dve", lambda e, ag=ag, av=av, ht=ht, cw=cw, n=n, s=s: e.tensor_tensor(ht[0:cw, s:s + n], ag[0:cw, 0:n], av[0:cw, 0:n], ALU.mult),
                         reads=[rag, rav], writes=[rht])
                hts.append((ht, rht, cw))
            wds = []
            for i in grp:
                w, rw = self.Wrow.get()
                S.dma("pool", w[:], D["w_down_t"][l, i], writes=[rw])
                wds.append((w, rw))
            nk = len(hts)
            for t in range(NT):
                for half in range(2):
                    ps, rps = self.PA.get()
                    for k in range(nk):
                        ht, rht, cw = hts[k]
                        S.op("pe", lambda e, k=k, ht=ht, cw=cw, ps=ps, t=t, half=half: e.matmul(ps[:], lhsT=ht[0:cw, t * 128:(t + 1) * 128], rhs=wds[k][0][0:cw, half * 512:(half + 1) * 512],
                                                                                              start=(k == 0), stop=(k == nk - 1)),
                             reads=[rht, wds[k][1]], writes=[rps])
                    xs = self.X[:, t, half * 512:(half + 1) * 512]
                    S.op("dve", lambda e, xs=xs, ps=ps: e.tensor_tensor(xs, xs, ps[:], ALU.add), reads=[rps, self.rX[t]], writes=[self.rX[t]])


def _host_layout(inputs):
    L = DEPTH
    f = lambda a: np.ascontiguousarray(np.asarray(a, dtype=np.float32))
    w_in = np.asarray(inputs["w_in"], dtype=np.float32)
    wi = np.concatenate([w_in[:, :, :2048], w_in[:, :, 2056:]], axis=2)
    w_in_t = f(wi.reshape(L, 8, 128, 28, 128).transpose(0, 3, 2, 1, 4))
    w_bd = f(w_in[:, :, 2048:2056].reshape(L, 8, 128, 8).transpose(0, 2, 1, 3))
    w_up = np.asarray(inputs["w_up"], dtype=np.float32)
    pad = NFT * 128 - DFF
    gate = np.pad(w_up[:, :, :DFF], ((0, 0), (0, 0), (0, pad))).reshape(L, 8, 128, NFT, 128)
    val = np.pad(w_up[:, :, DFF:], ((0, 0), (0, 0), (0, pad))).reshape(L, 8, 128, NFT, 128)
    w_up_t = f(np.stack([gate, val], axis=4).transpose(0, 3, 2, 1, 4, 5).reshape(L, NFT, 128, 8, 256))
    w_down = np.asarray(inputs["w_down"], dtype=np.float32)
    w_down_t = f(np.pad(w_down, ((0, 0), (0, pad), (0, 0))).reshape(L, NFT, 128, DM))
    gconv = f(np.asarray(inputs["gdn_conv"], np.float32).reshape(L, 4, 12, 128).transpose(0, 3, 2, 1))
    fc = np.asarray(inputs["ffn_conv"], np.float32)
    fg = np.pad(fc[:, :, :DFF], ((0, 0), (0, 0), (0, pad))).reshape(L, 3, NFT, 128)
    fv = np.pad(fc[:, :, DFF:], ((0, 0), (0, 0), (0, pad))).reshape(L, 3, NFT, 128)
    fconv = f(np.stack([fg, fv], axis=2).transpose(0, 4, 3, 2, 1))
    cf, cbf = host_consts()
    dn = np.asarray(inputs["diff_norm"], np.float32)
    shared = {
        "w_in_t": w_in_t, "w_bd": w_bd, "w_out": f(inputs["w_out"]), "w_up_t": w_up_t, "w_down_t": w_down_t,
        "gconv": gconv, "fconv": fconv,
        "ln1_g": f(inputs["ln1_g"]), "ln1_b": f(inputs["ln1_b"]), "ln2_g": f(inputs["ln2_g"]), "ln2_b": f(inputs["ln2_b"]),
        "a_log": f(inputs["gdn_a_log"]), "dt_bias": f(inputs["gdn_dt_bias"]), "gnorm": f(inputs["gdn_norm"]),
        "dlam": f(np.asarray(inputs["diff_lambda"], np.float32).reshape(L, 128)),
        "dnorm": f(np.concatenate([dn, dn], axis=1).reshape(L, 128, 1)),
        "cf": f(cf), "cbf": f(cbf),
    }
    return shared


_PROG_CACHE = {}


def run_cores(inputs, nlayers=DEPTH, dbg=None, ncores=8):
    key = (nlayers, dbg)
    if key not in _PROG_CACHE:
        _PROG_CACHE[key] = Prog(nlayers, dbg)
    prog = _PROG_CACHE[key]
    shared = _host_layout(inputs)
    x = np.asarray(inputs["x"], dtype=np.float32)
    in_maps = []
    for b in range(ncores):
        m = dict(shared)
        m["x"] = np.ascontiguousarray(x[b])
        in_maps.append(m)
    res = run_bass_kernel_spmd(prog.nc, in_maps, core_ids=list(range(ncores)))
    return np.stack([np.asarray(r["y"], dtype=np.float32) for r in res.results], axis=0)


def kernel(**inputs):
    return run_cores(inputs)
```
